# Optimizing a Trainium2 kernel written in Bass

```python
import math
import jax, jax.numpy as jnp
from jax import lax
import numpy as np

D_MODEL = 1024
BATCH = 2
SEQ = 8192
DEPTH = 1
DEC_BATCH = 128
DEC_SEQ = 4
PAST_LEN = 8192
PAGE_SIZE = 128

MIX_WIDTH = 2 * D_MODEL
HEAD_DIM = 64
ATT_WIDTH = MIX_WIDTH // 4
ATT_HEADS = ATT_WIDTH // HEAD_DIM
SSD_WIDTH = MIX_WIDTH - ATT_WIDTH
SSD_HEADS = SSD_WIDTH // HEAD_DIM
SSD_GROUPS = 4
SSD_STATE = 128
SSD_CONV = 4
SSD_CHUNK = 128
CONV_DIM = SSD_WIDTH + 2 * SSD_GROUPS * SSD_STATE
IN_WIDTH = SSD_WIDTH + CONV_DIM + SSD_HEADS + 3 * ATT_WIDTH
DIL_PATTERNS = ((128, 1), (512, 4), (2048, 16))
W_MAX = max(w for w, _ in DIL_PATTERNS)
ATT_BLOCK = 128
ROPE_THETA = 10000.0
PEER_NKEYS = 128
PEER_EXPERTS = PEER_NKEYS * PEER_NKEYS
PEER_HEADS = 8
PEER_TOPK = 16
PEER_DKEY = 256
PEER_BLOCK = 128
ALPHA = (2 * DEPTH) ** 0.25
BETA = (8 * DEPTH) ** -0.25
LN_EPS = 1e-5

kernel_name = 'hymba_ssd_dilated_window_peer_step'

F32 = jnp.float32


def layer_norm(x, g, b):
    xf = x.astype(F32)
    mu = jnp.mean(xf, axis=-1, keepdims=True)
    xc = xf - mu
    var = jnp.mean(xc * xc, axis=-1, keepdims=True)
    return (xc * lax.rsqrt(var + LN_EPS) * g.astype(F32) + b.astype(F32)).astype(x.dtype)


def rope(t, pos):
    half = HEAD_DIM // 2
    inv = ROPE_THETA ** (-jnp.arange(half, dtype=F32) / half)
    ang = pos.astype(F32)[:, None] * inv[None, :]
    cos = jnp.cos(ang)[None, :, None, :]
    sin = jnp.sin(ang)[None, :, None, :]
    tf = t.astype(F32)
    t1, t2 = tf[..., :half], tf[..., half:]
    return jnp.concatenate([t1 * cos - t2 * sin, t2 * cos + t1 * sin], axis=-1).astype(t.dtype)


def mixer_inputs(x, pos, w_in):
    b, L, _ = x.shape
    proj = jnp.einsum('bld,de->ble', x, w_in)
    o1 = SSD_WIDTH
    o2 = o1 + CONV_DIM
    o3 = o2 + SSD_HEADS
    z = proj[..., :o1]
    xbc = proj[..., o1:o2]
    dt_raw = proj[..., o2:o3]
    qkv = proj[..., o3:].reshape(b, L, 3, ATT_HEADS, HEAD_DIM)
    q = rope(qkv[:, :, 0], pos)
    k = rope(qkv[:, :, 1], pos)
    v = qkv[:, :, 2]
    return z, xbc, dt_raw, q, k, v


def ssd_scan(x, dt, A, Bm, Cm, h0):
    b, L, H, P = x.shape
    G, N = Bm.shape[2], Bm.shape[3]
    R = H // G
    q = min(SSD_CHUNK, L)
    pad = (-L) % q
    if pad:
        pw = lambda t: jnp.pad(t, [(0, 0), (0, pad)] + [(0, 0)] * (t.ndim - 2))
        x, dt, Bm, Cm = pw(x), pw(dt), pw(Bm), pw(Cm)
    c = (L + pad) // q
    x = x.reshape(b, c, q, G, R, P)
    dt = dt.reshape(b, c, q, G, R)
    Bm = Bm.reshape(b, c, q, G, N)
    Cm = Cm.reshape(b, c, q, G, N)
    acs = jnp.cumsum(dt * A.reshape(G, R), axis=2)
    xdt = x * dt[..., None]
    causal = jnp.tril(jnp.ones((q, q), dtype=bool))[:, :, None, None]
    seg = acs[:, :, :, None] - acs[:, :, None, :]
    decay = jnp.exp(jnp.where(causal, seg, -jnp.inf))
    cb = jnp.einsum('bclgn,bcsgn->bclsg', Cm, Bm)
    y_diag = jnp.einsum('bclsgr,bcsgrp->bclgrp', cb[..., None] * decay, xdt)
    decay_to_end = jnp.exp(acs[:, :, -1:] - acs)
    states = jnp.einsum('bclgn,bclgrp->bcgrpn', Bm, xdt * decay_to_end[..., None])
    chunk_decay = jnp.exp(acs[:, :, -1])

    def step(h, inp):
        dec, st = inp
        return dec[..., None, None] * h + st, h

    h_final, h_in = lax.scan(step, h0.reshape(b, G, R, P, N),
                             (jnp.moveaxis(chunk_decay, 1, 0), jnp.moveaxis(states, 1, 0)))
    h_in = jnp.moveaxis(h_in, 0, 1)
    y_off = jnp.einsum('bclgn,bcgrpn->bclgrp', Cm, h_in) * jnp.exp(acs)[..., None]
    y = (y_diag + y_off).reshape(b, c * q, H, P)[:, :L]
    return y, h_final.reshape(b, H, P, N)


def ssd_mixer(z, xbc, dt_raw, conv_prev, h0, conv_w, conv_b, dt_bias, a_log, d_skip, norm_g):
    b, L, _ = z.shape
    xbc_in = jnp.concatenate([conv_prev.astype(xbc.dtype), xbc], axis=1)
    conv = conv_b + xbc_in[:, 0:L] * conv_w[0]
    for tap in range(1, SSD_CONV):
        conv = conv + xbc_in[:, tap:tap + L] * conv_w[tap]
    xbc_act = jax.nn.silu(conv)
    new_conv = xbc_in[:, -(SSD_CONV - 1):]
    gn = SSD_GROUPS * SSD_STATE
    xs = xbc_act[..., :SSD_WIDTH].reshape(b, L, SSD_HEADS, HEAD_DIM).astype(F32)
    bm = xbc_act[..., SSD_WIDTH:SSD_WIDTH + gn].reshape(b, L, SSD_GROUPS, SSD_STATE).astype(F32)
    cm = xbc_act[..., SSD_WIDTH + gn:].reshape(b, L, SSD_GROUPS, SSD_STATE).astype(F32)
    dt = jax.nn.softplus(dt_raw.astype(F32) + dt_bias.astype(F32))
    A = -jnp.exp(a_log.astype(F32))
    y, h = ssd_scan(xs, dt, A, bm, cm, h0.astype(F32))
    y = y + xs * d_skip.astype(F32)[:, None]
    gated = y.reshape(b, L, SSD_WIDTH) * jax.nn.silu(z.astype(F32))
    gg = gated.reshape(b, L, SSD_GROUPS, SSD_WIDTH // SSD_GROUPS)
    gg = gg * lax.rsqrt(jnp.mean(gg * gg, axis=-1, keepdims=True) + LN_EPS)
    out = (gg.reshape(b, L, SSD_WIDTH) * norm_g.astype(F32)).astype(z.dtype)
    return out, new_conv, h.astype(h0.dtype)


def band_attention(q, k, v, n):
    b, L, H, hd = q.shape
    nb = -(-L // ATT_BLOCK)
    Lp = nb * ATT_BLOCK
    if Lp > L:
        pw = lambda t: jnp.pad(t, [(0, 0), (0, Lp - L), (0, 0), (0, 0)])
        q, k, v = pw(q), pw(k), pw(v)
    qb = q.reshape(b, nb, ATT_BLOCK, H, hd)
    kb = k.reshape(b, nb, ATT_BLOCK, H, hd)
    vb = v.reshape(b, nb, ATT_BLOCK, H, hd)
    kc = jnp.concatenate([jnp.concatenate([jnp.zeros_like(kb[:, :1]), kb[:, :-1]], axis=1), kb], axis=2)
    vc = jnp.concatenate([jnp.concatenate([jnp.zeros_like(vb[:, :1]), vb[:, :-1]], axis=1), vb], axis=2)
    s = jnp.einsum('bnqhd,bnkhd->bnhqk', qb, kc).astype(F32) * (hd ** -0.5)
    qi = jnp.arange(ATT_BLOCK)[:, None] + ATT_BLOCK
    ki = jnp.arange(2 * ATT_BLOCK)[None, :]
    dist = qi - ki
    key_pos = (jnp.arange(nb) * ATT_BLOCK)[:, None, None] + ki[None] - ATT_BLOCK
    valid = (dist >= 0)[None] & (dist <= n)[None] & (key_pos >= 0)
    s = jnp.where(valid[None, :, None], s, -jnp.inf)
    lse = jax.nn.logsumexp(s, axis=-1)
    p = jnp.exp(s - lse[..., None])
    o = jnp.einsum('bnhqk,bnkhd->bnqhd', p.astype(v.dtype), vc)
    o = o.reshape(b, Lp, H, hd)[:, :L]
    lse = jnp.transpose(lse, (0, 1, 3, 2)).reshape(b, Lp, H)[:, :L]
    return o, lse


def combine_dilations(outs, lses, dtype):
    wts = jax.nn.softmax(jnp.stack(lses), axis=0)
    o = jnp.einsum('pblh,pblhd->blhd', wts, jnp.stack(outs).astype(F32))
    return o.astype(dtype)


def dilated_attention_prompt(q, k, v):
    b, L, H, hd = q.shape
    outs, lses = [], []
    for w, d in DIL_PATTERNS:
        n = w // d
        to_res = lambda t: jnp.transpose(t.reshape(b, L // d, d, H, hd), (0, 2, 1, 3, 4)).reshape(b * d, L // d, H, hd)
        o, lse = band_attention(to_res(q), to_res(k), to_res(v), n)
        o = jnp.transpose(o.reshape(b, d, L // d, H, hd), (0, 2, 1, 3, 4)).reshape(b, L, H, hd)
        lse = jnp.transpose(lse.reshape(b, d, L // d, H), (0, 2, 1, 3)).reshape(b, L, H)
        outs.append(o)
        lses.append(lse)
    return combine_dilations(outs, lses, q.dtype)


def dilated_attention_sample(q, k_all, v_all):
    b, Lq = q.shape[0], q.shape[1]
    hd = q.shape[-1]
    off = k_all.shape[1] - Lq
    outs, lses = [], []
    for w, d in DIL_PATTERNS:
        n = w // d
        idx = off + jnp.arange(Lq)[:, None] - d * jnp.arange(n + 1)[None, :]
        valid = idx >= 0
        idx = jnp.maximum(idx, 0)
        kg = k_all[:, idx]
        vg = v_all[:, idx]
        s = jnp.einsum('bqhd,bqmhd->bhqm', q, kg).astype(F32) * (hd ** -0.5)
        s = jnp.where(valid[None, None], s, -jnp.inf)
        lse = jax.nn.logsumexp(s, axis=-1)
        p = jnp.exp(s - lse[..., None])
        o = jnp.einsum('bhqm,bqmhd->bqhd', p.astype(v_all.dtype), vg)
        outs.append(o)
        lses.append(jnp.transpose(lse, (0, 2, 1)))
    return combine_dilations(outs, lses, q.dtype)


def peer_ffn(x, w_q, keys_1, keys_2, u_tab, v_tab):
    shp = x.shape
    xt = x.reshape(-1, D_MODEL)
    T = xt.shape[0]
    nblk = -(-T // PEER_BLOCK)
    xp = jnp.pad(xt, [(0, nblk * PEER_BLOCK - T), (0, 0)])
    half = PEER_DKEY // 2

    def block(xb):
        qh = jnp.einsum('td,de->te', xb, w_q).reshape(-1, PEER_HEADS, PEER_DKEY)
        s1 = jnp.einsum('thc,kc->thk', qh[..., :half], keys_1)
        s2 = jnp.einsum('thc,kc->thk', qh[..., half:], keys_2)
        v1, i1 = lax.top_k(s1, PEER_TOPK)
        v2, i2 = lax.top_k(s2, PEER_TOPK)
        cand = (v1[..., :, None] + v2[..., None, :]).reshape(-1, PEER_HEADS, PEER_TOPK * PEER_TOPK)
        sc, ci = lax.top_k(cand, PEER_TOPK)
        e = (jnp.take_along_axis(i1, ci // PEER_TOPK, axis=-1) * PEER_NKEYS
             + jnp.take_along_axis(i2, ci % PEER_TOPK, axis=-1))
        g = jax.nn.softmax(sc.astype(F32), axis=-1)
        hid = jax.nn.gelu(jnp.einsum('thkd,td->thk', u_tab[e], xb).astype(F32), approximate=False)
        return jnp.einsum('thk,thkd->td', (g * hid).astype(xb.dtype), v_tab[e])

    out = lax.map(block, xp.reshape(nblk, PEER_BLOCK, D_MODEL))
    return out.reshape(-1, D_MODEL)[:T].reshape(shp)


def finish_layer(x, y_ssd, att, w_out, ln1_g, ln1_b, peer_w_q, peer_keys_1, peer_keys_2,
                 peer_u, peer_v, ln2_g, ln2_b):
    b, L, _ = x.shape
    mix = jnp.einsum('ble,ed->bld', jnp.concatenate([y_ssd, att.reshape(b, L, ATT_WIDTH)], axis=-1), w_out)
    h = layer_norm(ALPHA * x + mix, ln1_g, ln1_b)
    return layer_norm(ALPHA * h + peer_ffn(h, peer_w_q, peer_keys_1, peer_keys_2, peer_u, peer_v), ln2_g, ln2_b)


def setup_inputs(seed: int = 0) -> dict:
    key = jax.random.key(seed)
    ks = jax.random.split(key, 24)
    buf = min(W_MAX, PAST_LEN)
    nrm = lambda k, shape, s: jax.random.normal(k, shape, F32) * s
    dt0 = jnp.exp(jax.random.uniform(ks[9], (DEPTH, SSD_HEADS), F32) * (math.log(0.1) - math.log(0.001)) + math.log(0.001))
    return {
        'x_prompt': nrm(ks[0], (BATCH, SEQ, D_MODEL), 1.0),
        'x_sample': nrm(ks[1], (DEC_BATCH, DEC_SEQ, D_MODEL), 1.0),
        'cache_attn_k': nrm(ks[2], (DEPTH, DEC_BATCH, buf, ATT_HEADS, HEAD_DIM), 1.0),
        'cache_attn_v': nrm(ks[3], (DEPTH, DEC_BATCH, buf, ATT_HEADS, HEAD_DIM), 1.0),
        'state_conv': nrm(ks[4], (DEPTH, DEC_BATCH, SSD_CONV - 1, CONV_DIM), 1.0),
        'state_ssm': nrm(ks[5], (DEPTH, DEC_BATCH, SSD_HEADS, HEAD_DIM, SSD_STATE), 0.5),
        'w_in': nrm(ks[6], (DEPTH, D_MODEL, IN_WIDTH), D_MODEL ** -0.5),
        'conv_w': nrm(ks[7], (DEPTH, SSD_CONV, CONV_DIM), SSD_CONV ** -0.5),
        'conv_b': nrm(ks[8], (DEPTH, CONV_DIM), 0.02),
        'dt_bias': dt0 + jnp.log(-jnp.expm1(-dt0)),
        'a_log': jnp.log(jax.random.uniform(ks[10], (DEPTH, SSD_HEADS), F32, 1.0, 16.0)),
        'd_skip': 1.0 + nrm(ks[11], (DEPTH, SSD_HEADS), 0.1),
        'ssd_norm_g': 1.0 + nrm(ks[12], (DEPTH, SSD_WIDTH), 0.05),
        'w_out': nrm(ks[13], (DEPTH, MIX_WIDTH, D_MODEL), BETA * MIX_WIDTH ** -0.5),
        'ln1_g': 1.0 + nrm(ks[14], (DEPTH, D_MODEL), 0.05),
        'ln1_b': nrm(ks[15], (DEPTH, D_MODEL), 0.02),
        'peer_w_q': nrm(ks[16], (DEPTH, D_MODEL, PEER_HEADS * PEER_DKEY), D_MODEL ** -0.5),
        'peer_keys_1': nrm(ks[17], (DEPTH, PEER_NKEYS, PEER_DKEY // 2), (PEER_DKEY // 2) ** -0.5),
        'peer_keys_2': nrm(ks[18], (DEPTH, PEER_NKEYS, PEER_DKEY // 2), (PEER_DKEY // 2) ** -0.5),
        'peer_u': nrm(ks[19], (DEPTH, PEER_EXPERTS, D_MODEL), D_MODEL ** -0.5),
        'peer_v': nrm(ks[20], (DEPTH, PEER_EXPERTS, D_MODEL), BETA * PEER_HEADS ** -0.5),
        'ln2_g': 1.0 + nrm(ks[21], (DEPTH, D_MODEL), 0.05),
        'ln2_b': nrm(ks[22], (DEPTH, D_MODEL), 0.02),
    }


def reference(x_prompt, x_sample, cache_attn_k, cache_attn_v, state_conv, state_ssm,
              w_in, conv_w, conv_b, dt_bias, a_log, d_skip, ssd_norm_g, w_out, ln1_g, ln1_b,
              peer_w_q, peer_keys_1, peer_keys_2, peer_u, peer_v, ln2_g, ln2_b):
    bp, Lp_len = x_prompt.shape[0], x_prompt.shape[1]
    Ls = x_sample.shape[1]
    pos_p = jnp.arange(Lp_len)
    pos_s = PAST_LEN + jnp.arange(Ls)
    buf_p = min(W_MAX, Lp_len)
    h_p, h_s = x_prompt, x_sample
    pk, pv, pc, ps, sk, sv, sc, ss = [], [], [], [], [], [], [], []
    for l in range(DEPTH):
        ssd_w = (conv_w[l], conv_b[l], dt_bias[l], a_log[l], d_skip[l], ssd_norm_g[l])
        tail_w = (w_out[l], ln1_g[l], ln1_b[l], peer_w_q[l], peer_keys_1[l], peer_keys_2[l],
                  peer_u[l], peer_v[l], ln2_g[l], ln2_b[l])
        z, xbc, dtr, q, k, v = mixer_inputs(h_p, pos_p, w_in[l])
        conv0 = jnp.zeros((bp, SSD_CONV - 1, CONV_DIM), h_p.dtype)
        ssm0 = jnp.zeros((bp, SSD_HEADS, HEAD_DIM, SSD_STATE), h_p.dtype)
        y_ssd, conv_new, ssm_new = ssd_mixer(z, xbc, dtr, conv0, ssm0, *ssd_w)
        att = dilated_attention_prompt(q, k, v)
        pk.append(k[:, -buf_p:])
        pv.append(v[:, -buf_p:])
        pc.append(conv_new)
        ps.append(ssm_new)
        h_p = finish_layer(h_p, y_ssd, att, *tail_w)
        z, xbc, dtr, q, k, v = mixer_inputs(h_s, pos_s, w_in[l])
        y_ssd, conv_new, ssm_new = ssd_mixer(z, xbc, dtr, state_conv[l], state_ssm[l], *ssd_w)
        k_all = jnp.concatenate([cache_attn_k[l].astype(k.dtype), k], axis=1)
        v_all = jnp.concatenate([cache_attn_v[l].astype(v.dtype), v], axis=1)
        att = dilated_attention_sample(q, k_all, v_all)
        buf_s = cache_attn_k.shape[2]
        sk.append(k_all[:, -buf_s:])
        sv.append(v_all[:, -buf_s:])
        sc.append(conv_new)
        ss.append(ssm_new)
        h_s = finish_layer(h_s, y_ssd, att, *tail_w)
    prompt_attn_k, prompt_attn_v = jnp.stack(pk), jnp.stack(pv)
    prompt_conv, prompt_ssm = jnp.stack(pc), jnp.stack(ps)
    sample_attn_k, sample_attn_v = jnp.stack(sk), jnp.stack(sv)
    sample_conv, sample_ssm = jnp.stack(sc), jnp.stack(ss)
    return (h_p, h_s, prompt_attn_k, prompt_attn_v, prompt_conv, prompt_ssm,
            sample_attn_k, sample_attn_v, sample_conv, sample_ssm)
```

```python
import numpy as np
from contextlib import ExitStack
import concourse.bass as bass
import concourse.mybir as mybir
from concourse.bass_utils import run_bass_kernel_spmd

F32 = mybir.dt.float32
BF16 = mybir.dt.bfloat16
ALU = mybir.AluOpType
AF = mybir.ActivationFunctionType

D_MODEL = 1024
SEQ = 8192
NT = SEQ // 128
BLK = 256
NBLK = SEQ // BLK
HALO = 3
NW = 1414
OX, OB, OC, OZ, OQ, OK_, OV, ODT = 0, 384, 512, 640, 1024, 1152, 1280, 1408
NCONV = 640
KV0_TILE = 48
ROWS_COPY = 2044
NSEQ_COPY = 16


class Buf:
    def __init__(self, t):
        self.t = t
        self.w = None
        self.r = []


class DSem:
    def __init__(self, nc, es, name):
        self.sem = es.enter_context(nc.semaphore(name))
        self.n = 0


class Eng:
    def __init__(self, nc, es, eng, name, same_sync=True):
        self.eng = eng
        self.name = name
        self.sem = es.enter_context(nc.semaphore("prog_" + name))
        self.n = 0
        self.seen = {}
        self.same_sync = same_sync

    def wait_tok(self, tok):
        if tok is None:
            return
        sem, val = tok
        if isinstance(val, DSem):
            val = val.n
        if sem is self.sem and not self.same_sync:
            return
        k = id(sem)
        if self.seen.get(k, 0) >= val:
            return
        self.eng.wait_ge(sem, val)
        self.seen[k] = val

    def deps(self, reads, writes):
        for b in reads:
            self.wait_tok(b.w)
        for b in writes:
            self.wait_tok(b.w)
            for t in b.r:
                self.wait_tok(t)

    @staticmethod
    def _mark(tok, reads, writes):
        for b in reads:
            b.r = [t for t in b.r if t[0] is not tok[0]] + [tok]
        for b in writes:
            b.w = tok
            b.r = []

    def op(self, fn, reads=(), writes=(), **kw):
        self.deps(reads, writes)
        inst = fn(**kw)
        self.n += 1
        inst.then_inc(self.sem, 1)
        tok = (self.sem, self.n)
        self._mark(tok, reads, writes)
        return tok

    def dma(self, dsem, out, in_, reads=(), writes=()):
        self.deps(reads, writes)
        inst = self.eng.dma_start(out=out, in_=in_)
        dsem.n += 16
        inst.then_inc(dsem.sem, 16)
        tok = (dsem.sem, dsem)
        self._mark(tok, reads, writes)
        return tok


U32 = mybir.dt.uint32
AX = mybir.AxisListType

NS = 16
NTOK = 64
IN_W = 5656
ALPHA = 2.0 ** 0.25
LN_EPS = 1e-5


def s2_consts():
    tok = [(l, b) for l in range(4) for b in range(NS)]
    cumsel = np.zeros((64, 64), np.float32)
    bsel = np.zeros((64, 4, 64), np.float32)
    delta = np.zeros((64, 16), np.float32)
    maskneg = np.zeros((64, 4), np.float32)
    multnew = np.zeros((64, 4), np.float32)
    for i, (l1, b1) in enumerate(tok):
        delta[i, b1] = 1.0
        for l in range(4):
            maskneg[i, l] = 0.0 if l1 <= l else -30000.0
            multnew[i, l] = 0.0 if l1 > l else (3.0 if l1 == l else 1.0)
        for j, (l2, b2) in enumerate(tok):
            if b1 == b2 and l1 <= l2:
                cumsel[i, j] = 1.0
            if b1 == b2:
                bsel[i, l1, j] = 1.0
    eye16 = np.broadcast_to(np.eye(16, dtype=np.float32)[None], (128, 16, 16)).copy()
    mask1 = (np.arange(128)[:, None] >= np.arange(4)[None, :]).astype(np.float32)
    selall = np.zeros((8, 127), np.float32)
    selall[:, 63] = 1.0
    iota16 = np.broadcast_to(np.arange(16, dtype=np.float32)[None], (128, 16)).copy()
    half = 32
    inv = (10000.0 ** (-np.arange(half, dtype=np.float32) / half)).astype(np.float32)
    pos = np.array([8192 + l for (l, b) in tok], np.float32)
    ang = pos[:, None] * inv[None, :]
    return {
        "c_ident": np.eye(128, dtype=np.float32), "c_cumsel": cumsel, "c_bsel": bsel, "c_delta": delta,
        "c_maskneg": maskneg, "c_multnew": multnew, "c_eye16": eye16, "c_mask1": mask1, "c_selall": selall,
        "c_iota16": iota16, "c_cos": np.cos(ang).astype(np.float32), "c_sin": np.sin(ang).astype(np.float32),
        "c_bd8": np.eye(8, dtype=np.float32),
    }


S2_INPUTS = {
    "s2_xT": [128, 8, 48 + 64], "s2_x": [128, 1024], "s2_win": [128, 8, IN_W], "s2_cw": [4, 2560], "s2_cb": [1, 2560],
    "s2_dtb": [1, 24], "s2_alog": [1, 24], "s2_dskip": [1, 24], "s2_ng": [1, 1536], "s2_sctap": [3, 64, 2560],
    "s2_h0": [NS, 12, 128, 128], "s2_wout": [128, 16, 1024], "s2_ln1g": [1, 1024], "s2_ln1b": [1, 1024],
    "s2_ln2g": [1, 1024], "s2_ln2b": [1, 1024], "s2_wq": [128, 8, 2048], "s2_k1T": [128, 128], "s2_k2T": [128, 128],
    "s2_pu": [16384, 1024], "s2_pv": [16384, 1024],
    "c_ident": [128, 128], "c_cumsel": [64, 64], "c_bsel": [64, 4, 64], "c_delta": [64, 16], "c_maskneg": [64, 4],
    "c_multnew": [64, 4], "c_eye16": [128, 16, 16], "c_mask1": [128, 4], "c_selall": [8, 127], "c_iota16": [128, 16],
    "c_cos": [64, 32], "c_sin": [64, 32], "c_bd8": [8, 8],
}


def s2_host_inputs(core, I):
    f = lambda a: np.ascontiguousarray(np.asarray(a, dtype=np.float32))
    seqs = slice(NS * core, NS * core + NS)
    xs = I["x_sample"][seqs]
    xs_lb = xs.transpose(1, 0, 2).reshape(64, 1024)
    xT = np.zeros((1024, 48 + 64), np.float32)
    xT[:, 48:] = xs_lb.T
    sc = I["state_conv"][0, seqs]
    sctap = np.zeros((3, 4, NS, 2560), np.float32)
    for k in range(3):
        for l in range(4):
            if l + k <= 2:
                sctap[k, l] = sc[:, l + k, :]
    d = {
        "s2_xT": f(xT.reshape(8, 128, 112).transpose(1, 0, 2)),
        "s2_x": f(np.concatenate([xs_lb, xs_lb], axis=0)),
        "s2_win": f(I["w_in"][0].reshape(8, 128, IN_W).transpose(1, 0, 2)),
        "s2_cw": f(I["conv_w"][0]), "s2_cb": f(I["conv_b"]), "s2_dtb": f(I["dt_bias"]), "s2_alog": f(I["a_log"]),
        "s2_dskip": f(I["d_skip"]), "s2_ng": f(I["ssd_norm_g"]), "s2_sctap": f(sctap.reshape(3, 64, 2560)),
        "s2_h0": f(I["state_ssm"][0, seqs].reshape(NS, 12, 128, 128)),
        "s2_wout": f(I["w_out"][0].reshape(16, 128, 1024).transpose(1, 0, 2)),
        "s2_ln1g": f(I["ln1_g"]), "s2_ln1b": f(I["ln1_b"]), "s2_ln2g": f(I["ln2_g"]), "s2_ln2b": f(I["ln2_b"]),
        "s2_wq": f(I["peer_w_q"][0].reshape(8, 128, 2048).transpose(1, 0, 2)),
        "s2_k1T": f(I["peer_keys_1"][0].T), "s2_k2T": f(I["peer_keys_2"][0].T),
        "s2_pu": f(I["peer_u"][0]), "s2_pv": f(I["peer_v"][0]),
    }
    return d


def emit_s2(nc, es, K, T, ck, cv, ys_out, dbg=None):
    import concourse.bass as bass
    PE, DVE, ACT, POOL, SP = K["PE"], K["DVE"], K["ACT"], K["POOL"], K["SP"]
    sb, dsem, out_toks, PB = K["sb"], K["dsem"], K["out_toks"], K["PB"]
    sb_persist = sb
    V, G, S = nc.vector, nc.gpsimd, nc.scalar

    def mm(out, lhsT, rhs, start, stop, reads, writes):
        PE.op(lambda: nc.tensor.matmul(out, lhsT=lhsT, rhs=rhs, start=start, stop=stop), reads=reads, writes=writes)

    def tt(out, in0, in1, op, reads, writes, eng=None):
        E_, e_ = (DVE, V) if eng is None else eng
        E_.op(lambda: e_.tensor_tensor(out=out, in0=in0, in1=in1, op=op), reads=reads, writes=writes)

    def act(out, in_, func, reads, writes, **kw):
        ACT.op(lambda: S.activation(out=out, in_=in_, func=func, **kw), reads=reads, writes=writes)

    def cp(out, in_, reads, writes, eng="act"):
        if eng == "act":
            ACT.op(lambda: S.copy(out=out, in_=in_), reads=reads, writes=writes)
        else:
            DVE.op(lambda: V.tensor_copy(out=out, in_=in_), reads=reads, writes=writes)

    ld = dsem("s2ld")

    def load(name, shape, src=None, dt=F32):
        b = sb("t_" + name, shape, dt)
        SP.dma(ld, b.t[:], T[name] if src is None else src, writes=[b])
        return b

    def bload(name, n, parts=128):
        b = sb("r_" + name, [parts, n])
        SP.dma(ld, b.t[:], T[name].rearrange("o e -> (o e)").partition_broadcast(parts), writes=[b])
        return b

    ident = load("c_ident", [128, 128])
    cumsel = load("c_cumsel", [64, 64])
    bsel = load("c_bsel", [64, 4, 64])
    delta = load("c_delta", [64, 16])
    maskneg = load("c_maskneg", [64, 4])
    multnew = load("c_multnew", [64, 4])
    eye16 = load("c_eye16", [128, 16, 16])
    mask1 = load("c_mask1", [128, 4])
    selall = load("c_selall", [8, 127])
    iota16 = load("c_iota16", [128, 16])
    cost = load("c_cos", [64, 32])
    sint = load("c_sin", [64, 32])
    bd8 = load("c_bd8", [8, 8])
    dtb = bload("s2_dtb", 24, 64)
    negA = bload("s2_alog", 24, 64)
    dsk = bload("s2_dskip", 24, 64)
    ngr = bload("s2_ng", 1536, 64)
    act(negA.t[:], negA.t[:], AF.Exp, [negA], [negA])
    DVE.op(lambda: V.tensor_scalar(out=negA.t[:], in0=negA.t[:], scalar1=-1.0, scalar2=None, op0=ALU.mult),
           reads=[negA], writes=[negA])

    zt = sb("s2_z", [64, 1536])
    xbc = sb("s2_xbc", [64, 2560])
    dtr = sb("s2_dtr", [64, 24])
    qkv = sb("s2_qkv", [64, 1536])
    ymix = sb("s2_ymix", [64, 2048])
    sb, close1 = K["scope"]()
    xst = sb("s2_xst", [128, 8, 112])
    SP.dma(ld, xst.t[:], T["s2_xT"], writes=[xst])
    xTb = xst
    wst = [sb(f"s2_wst{i}", [128, 8, 512]) for i in range(2)]
    wbf = wst
    wfo_ = sb("s2_wfo", [128, 4, 8, 512])
    wfo = [wfo_, wfo_]
    wld = [dsem(f"s2wld{i}") for i in range(2)]
    wrp = [sb(f"s2_wrp{i}", [128, 4, 512]) for i in range(2)]
    cbr = [sb(f"s2_cbr{i}", [64, 512]) for i in range(2)]
    sct = [sb(f"s2_sct{i}", [64, 3, 512]) for i in range(2)]
    ctmp = sb("s2_ctmp", [64, 512])
    cacc = sb("s2_cacc", [64, 512])
    chunks = [(o, min(512, 1536 - o), "z", zt, o) for o in range(0, 1536, 512)]
    chunks += [(1536 + o, 512, "conv", xbc, o) for o in range(0, 2560, 512)]
    chunks += [(4096, 24, "dt", dtr, 0)]
    chunks += [(4120 + o, 512, "qkv", qkv, o) for o in range(0, 1536, 512)]
    for ci, (c0, n, kind, dst, do) in enumerate(chunks):
        i = ci % 2
        SP.dma(wld[i], wst[i].t[:, :, 0:n], T["s2_win"][:, :, c0:c0 + n], writes=[wst[i]])
        P = PB[i]
        if kind != "conv":
            for c in range(8):
                mm(P.t[0:64, 0:n], xTb.t[:, c, 48:112], wbf[i].t[:, c, 0:n], c == 0, c == 7, [xTb, wbf[i]], [P])
            cp(dst.t[:, do:do + n], P.t[0:64, 0:n], [P], [dst])
        else:
            cc = c0 - 1536
            SP.dma(wld[i], wrp[i].t[:], T["s2_cw"][:, cc:cc + 512].partition_broadcast(128), writes=[wrp[i]])
            SP.dma(wld[i], cbr[i].t[:], T["s2_cb"][0, cc:cc + 512].partition_broadcast(64), writes=[cbr[i]])
            SP.dma(wld[i], sct[i].t[:], T["s2_sctap"][:, :, cc:cc + 512].rearrange("k t e -> t k e"), writes=[sct[i]])
            for tap in range(4):
                eng = (POOL, G) if tap % 2 else (DVE, V)
                tt(wfo[i].t[:, tap, :, :], wst[i].t[:, :, :], wrp[i].t[:, tap, :].unsqueeze(1).to_broadcast([128, 8, 512]),
                   ALU.mult, [wst[i], wrp[i]], [wfo[i]], eng=eng)
            n_ = 0
            for tap in range(4):
                for c in range(8):
                    mm(P.t[0:64, :], xTb.t[:, c, 16 * tap:16 * tap + 64], wfo[i].t[:, tap, c, :], n_ == 0, n_ == 31,
                       [xTb, wfo[i]], [P])
                    n_ += 1
            tt(cacc.t[:], sct[i].t[:, 0, :], wrp[i].t[0:64, 0, :], ALU.mult, [sct[i], wrp[i]], [cacc])
            for k in (1, 2):
                tt(ctmp.t[:], sct[i].t[:, k, :], wrp[i].t[0:64, k, :], ALU.mult, [sct[i], wrp[i]], [ctmp])
                tt(cacc.t[:], cacc.t[:], ctmp.t[:], ALU.add, [cacc, ctmp], [cacc])
            tt(cacc.t[:], cacc.t[:], cbr[i].t[:], ALU.add, [cacc, cbr[i]], [cacc])
            tt(cacc.t[:], cacc.t[:], P.t[0:64, :], ALU.add, [cacc, P], [cacc])
            act(dst.t[:, do:do + 512], cacc.t[:], AF.Silu, [cacc], [dst])
    if dbg is not None:
        out_toks.append(POOL.dma(ld, dbg["d_xbc"][:, :], xbc.t[:], reads=[xbc]))
        out_toks.append(POOL.dma(ld, dbg["d_qkv"][:, :], qkv.t[:], reads=[qkv]))

    close1()
    sb, close2 = K["scope"]()
    dt_ = sb("s2_dt", [64, 24])
    dtA = sb("s2_dtA", [64, 24])
    tt(dt_.t[:], dtr.t[:], dtb.t[:], ALU.add, [dtr, dtb], [dt_])
    act(dt_.t[:], dt_.t[:], AF.Exp, [dt_], [dt_])
    act(dt_.t[:], dt_.t[:], AF.Ln, [dt_], [dt_], bias=1.0)
    tt(dtA.t[:], dt_.t[:], negA.t[:], ALU.mult, [dt_, negA], [dtA])
    a_in = sb("s2_a", [64, 24])
    ea = sb("s2_ea", [64, 24])
    mm(PB[0].t[0:64, 0:24], cumsel.t[:], dtA.t[:], True, True, [cumsel, dtA], [PB[0]])
    cp(a_in.t[:], PB[0].t[0:64, 0:24], [PB[0]], [a_in])
    act(ea.t[:], a_in.t[:], AF.Exp, [a_in], [ea])

    Csh = sb("s2_Csh", [64, 4, 512])
    ash = sb("s2_ash", [64, 4, 24])
    for l in range(4):
        mm(PB[1 + l].t[0:64, :], bsel.t[:, l, :], xbc.t[:, 2048:2560], True, True, [bsel, xbc], [PB[1 + l]])
        cp(Csh.t[:, l, :], PB[1 + l].t[0:64, :], [PB[1 + l]], [Csh])
        mm(PB[5].t[0:64, l * 24:(l + 1) * 24], bsel.t[:, l, :], a_in.t[:], True, True, [bsel, a_in], [PB[5]])
    cp(ash.t[:], PB[5].t[0:64, 0:96].rearrange("p (l h) -> p l h", l=4), [PB[5]], [ash])
    prodG = sb("s2_prodG", [64, 4, 512])
    GT = sb("s2_GT", [64, 4, 4])
    tt(prodG.t[:], Csh.t[:], xbc.t[:, 1536:2048].unsqueeze(1).to_broadcast([64, 4, 512]), ALU.mult, [Csh, xbc], [prodG])
    DVE.op(lambda: V.tensor_reduce(out=GT.t[:].rearrange("p l g -> p (l g)"),
                                   in_=prodG.t[:].rearrange("p l (g n) -> p (l g) n", g=4), axis=AX.X, op=ALU.add),
           reads=[prodG], writes=[GT])
    dec = sb("s2_dec", [64, 4, 24])
    tt(dec.t[:], ash.t[:], a_in.t[:].unsqueeze(1).to_broadcast([64, 4, 24]), ALU.subtract, [ash, a_in], [dec])
    tt(dec.t[:], dec.t[:], maskneg.t[:].unsqueeze(2).to_broadcast([64, 4, 24]), ALU.add, [dec, maskneg], [dec])
    act(dec.t[:], dec.t[:], AF.Exp, [dec], [dec])
    for l in range(4):
        tt(dec.t[:, l, :].rearrange("p (g r) -> p g r", g=4), dec.t[:, l, :].rearrange("p (g r) -> p g r", g=4),
           GT.t[:, l, :].unsqueeze(2).to_broadcast([64, 4, 6]), ALU.mult, [dec, GT], [dec])
    tt(dec.t[:], dec.t[:], dt_.t[:].unsqueeze(1).to_broadcast([64, 4, 24]), ALU.mult, [dec, dt_], [dec])
    Mh = sb("s2_Mh", [64, 24, 4, 16])
    for l in range(4):
        tt(Mh.t[:, :, l, :], dec.t[:, l, :].unsqueeze(2).to_broadcast([64, 24, 16]),
           delta.t[:].unsqueeze(1).to_broadcast([64, 24, 16]), ALU.mult, [dec, delta], [Mh])
    for h in range(24):
        P = PB[1 + h // 8]
        mm(P.t[0:64, (h % 8) * 64:(h % 8 + 1) * 64], Mh.t[:, h, :, :].rearrange("p l b -> p (l b)"),
           xbc.t[:, h * 64:(h + 1) * 64], True, True, [Mh, xbc], [P])
    yd = sb("s2_yd", [64, 1536])
    for k in range(3):
        cp(yd.t[:, k * 512:(k + 1) * 512], PB[1 + k].t[0:64, :], [PB[1 + k]], [yd])

    CTm = sb("s2_CTm", [128, 4, 16, 64])
    CTs = sb("s2_CTs", [128, 64])
    for g in range(4):
        PE.op(lambda g=g: nc.tensor.transpose(out=PB[0].t[:, g * 64:(g + 1) * 64], in_=xbc.t[:, 2048 + g * 128:2048 + (g + 1) * 128],
                                              identity=ident.t[0:64, 0:64]), reads=[xbc, ident], writes=[PB[0]])
        cp(CTs.t[:], PB[0].t[:, g * 64:(g + 1) * 64], [PB[0]], [CTs], eng="dve")
        tt(CTm.t[:, g, :, :].rearrange("p s (l b) -> p s l b", l=4),
           CTs.t[:].rearrange("p (l b) -> p l b", l=4).unsqueeze(1).to_broadcast([128, 16, 4, 16]),
           eye16.t[:].unsqueeze(2).to_broadcast([128, 16, 4, 16]), ALU.mult, [CTs, eye16], [CTm])
    Hb = [sb(f"s2_H{i}", [128, NS, 128]) for i in range(2)]
    hld = [dsem(f"s2hld{i}") for i in range(2)]
    HT = [sb(f"s2_HT{i}", [128, 128]) for i in range(2)]
    n_t = 0
    for hp in range(12):
        H = Hb[hp % 2]
        SP.dma(hld[hp % 2], H.t[:], T["s2_h0"][:, hp, :, :].rearrange("b q n -> q b n"), writes=[H])
        P = PB[4 + hp // 4]
        for b in range(NS):
            qd = PB[7].t[:, (n_t % 4) * 128:(n_t % 4 + 1) * 128]
            PE.op(lambda: nc.tensor.transpose(out=qd, in_=H.t[:, b, :], identity=ident.t[:]), reads=[H, ident], writes=[PB[7]])
            ht = HT[n_t % 2]
            cp(ht.t[:], qd, [PB[7]], [ht], eng=("act" if n_t % 2 else "dve"))
            mm(P.t[0:64, (hp % 4) * 128:(hp % 4 + 1) * 128], CTm.t[:, hp // 3, b, :], ht.t[:], b == 0, b == NS - 1, [CTm, ht], [P])
            n_t += 1
    y = sb("s2_y", [64, 1536])
    tmpy = sb("s2_tmpy", [64, 1536])
    for k in range(3):
        tt(y.t[:, k * 512:(k + 1) * 512].rearrange("p (h f) -> p h f", h=8), PB[4 + k].t[0:64, :].rearrange("p (h f) -> p h f", h=8),
           ea.t[:, k * 8:(k + 1) * 8].unsqueeze(2).to_broadcast([64, 8, 64]), ALU.mult, [PB[4 + k], ea], [y])
    tt(y.t[:], y.t[:], yd.t[:], ALU.add, [y, yd], [y])
    tt(tmpy.t[:].rearrange("p (h f) -> p h f", h=24), xbc.t[:, 0:1536].rearrange("p (h f) -> p h f", h=24),
       dsk.t[:].unsqueeze(2).to_broadcast([64, 24, 64]), ALU.mult, [xbc, dsk], [tmpy])
    tt(y.t[:], y.t[:], tmpy.t[:], ALU.add, [y, tmpy], [y])
    act(zt.t[:], zt.t[:], AF.Silu, [zt], [zt])
    tt(y.t[:], y.t[:], zt.t[:], ALU.mult, [y, zt], [y])
    tt(tmpy.t[:], y.t[:], y.t[:], ALU.mult, [y], [tmpy])
    ss = sb("s2_ss", [64, 4])
    DVE.op(lambda: V.tensor_reduce(out=ss.t[:], in_=tmpy.t[:].rearrange("p (g f) -> p g f", g=4), axis=AX.X, op=ALU.add),
           reads=[tmpy], writes=[ss])
    DVE.op(lambda: V.tensor_scalar(out=ss.t[:], in0=ss.t[:], scalar1=1.0 / 384.0, scalar2=LN_EPS, op0=ALU.mult, op1=ALU.add),
           reads=[ss], writes=[ss])
    act(ss.t[:], ss.t[:], AF.Sqrt, [ss], [ss])
    DVE.op(lambda: V.reciprocal(out=ss.t[:], in_=ss.t[:]), reads=[ss], writes=[ss])
    tt(y.t[:].rearrange("p (g f) -> p g f", g=4), y.t[:].rearrange("p (g f) -> p g f", g=4),
       ss.t[:].unsqueeze(2).to_broadcast([64, 4, 384]), ALU.mult, [y, ss], [y])
    tt(ymix.t[:, 0:1536], y.t[:], ngr.t[:], ALU.mult, [y, ngr], [ymix])
    if dbg is not None:
        out_toks.append(POOL.dma(ld, dbg["d_yssd"][:, :], ymix.t[:, 0:1536], reads=[ymix]))
    close2()
    return dict(ymix=ymix, qkv=qkv, load=load, bload=bload, mm=mm, tt=tt, act=act, cp=cp, ident=ident, bsel=bsel,
                delta=delta, multnew=multnew, mask1=mask1, selall=selall, iota16=iota16, cost=cost, sint=sint, bd8=bd8, ld=ld)


def emit_s2b(nc, es, K, T, ck, cv, ys_out, R, dbg=None, stage=4):
    import concourse.bass as bass
    PE, DVE, ACT, POOL, SP = K["PE"], K["DVE"], K["ACT"], K["POOL"], K["SP"]
    sb, dsem, out_toks, PB = K["sb"], K["dsem"], K["out_toks"], K["PB"]
    V, G, S = nc.vector, nc.gpsimd, nc.scalar
    mm0, tt, act, cp, ld = R["mm"], R["tt"], R["act"], R["cp"], R["ld"]
    ymix, qkv, ident, bsel, delta = R["ymix"], R["qkv"], R["ident"], R["bsel"], R["delta"]
    multnew, mask1, selall, iota16, cost, sint, bd8 = (R[k] for k in ("multnew", "mask1", "selall", "iota16", "cost", "sint", "bd8"))

    def mm(out, lhsT, rhs, start, stop, reads, writes, skip=False):
        PE.op(lambda: nc.tensor.matmul(out, lhsT=lhsT, rhs=rhs, start=start, stop=stop, skip_group_check=skip),
              reads=reads, writes=writes)

    def red(out, in_, reads, writes):
        DVE.op(lambda: V.tensor_reduce(out=out, in_=in_, axis=AX.X, op=ALU.add), reads=reads, writes=writes)

    qr = sb("s2_qr", [64, 512])
    kr = sb("s2_kr", [64, 512])
    sb, close3 = K["scope"]()
    rc = sb("s2_rc", [64, 512])
    rs = sb("s2_rs", [64, 512])
    v4 = lambda ap: ap.rearrange("p (h two f) -> p h two f", h=8, two=2)
    cb4 = cost.t[:].unsqueeze(1).unsqueeze(1).to_broadcast([64, 8, 2, 32])
    sb3 = sint.t[:].unsqueeze(1).to_broadcast([64, 8, 32])
    for src_o, dst in ((0, qr), (512, kr)):
        s4 = v4(qkv.t[:, src_o:src_o + 512])
        tt(v4(rc.t[:]), s4, cb4, ALU.mult, [qkv, cost], [rc])
        tt(v4(rs.t[:])[:, :, 0, :], s4[:, :, 1, :], sb3, ALU.mult, [qkv, sint], [rs])
        tt(v4(rs.t[:])[:, :, 1, :], s4[:, :, 0, :], sb3, ALU.mult, [qkv, sint], [rs])
        tt(v4(dst.t[:])[:, :, 0, :], v4(rc.t[:])[:, :, 0, :], v4(rs.t[:])[:, :, 0, :], ALU.subtract, [rc, rs], [dst])
        tt(v4(dst.t[:])[:, :, 1, :], v4(rc.t[:])[:, :, 1, :], v4(rs.t[:])[:, :, 1, :], ALU.add, [rc, rs], [dst])

    Qsh = sb("s2_Qsh", [64, 4, 512])
    for l in range(4):
        mm(PB[l].t[0:64, :], bsel.t[:, l, :], qr.t[:], True, True, [bsel, qr], [PB[l]])
        cp(Qsh.t[:, l, :], PB[l].t[0:64, :], [PB[l]], [Qsh], eng=("act" if l % 2 else "dve"))
    tt(Qsh.t[:], Qsh.t[:], kr.t[:].unsqueeze(1).to_broadcast([64, 4, 512]), ALU.mult, [Qsh, kr], [Qsh])
    Pn = sb("s2_Pn", [64, 4, 8])
    red(Pn.t[:].rearrange("p l h -> p (l h)"), Qsh.t[:].rearrange("p l (h d) -> p (l h) d", h=8), [Qsh], [Pn])
    act(Pn.t[:], Pn.t[:], AF.Exp, [Pn], [Pn], scale=0.125)
    tt(Pn.t[:], Pn.t[:], multnew.t[:].unsqueeze(2).to_broadcast([64, 4, 8]), ALU.mult, [Pn, multnew], [Pn])
    Mn = sb("s2_Mn", [64, 8, 4, 16])
    for l in range(4):
        tt(Mn.t[:, :, l, :], Pn.t[:, l, :].unsqueeze(2).to_broadcast([64, 8, 16]),
           delta.t[:].unsqueeze(1).to_broadcast([64, 8, 16]), ALU.mult, [Pn, delta], [Mn])
    vaug = sb("s2_vaug", [64, 8, 65])
    DVE.op(lambda: V.memset(vaug.t[:], 1.0), writes=[vaug])
    cp(vaug.t[:, :, 0:64], qkv.t[:, 1024:1536].rearrange("p (h d) -> p h d", h=8), [qkv], [vaug], eng="dve")
    for h in range(8):
        P = PB[4 + h // 4]
        mm(P.t[0:64, (h % 4) * 65:(h % 4 + 1) * 65], Mn.t[:, h, :, :].rearrange("p l b -> p (l b)"), vaug.t[:, h, :],
           h % 4 == 0, False, [Mn, vaug], [P], skip=True)

    ones1 = sb("s2_ones1", [128, 1])
    DVE.op(lambda: V.memset(ones1.t[:], 1.0), writes=[ones1])
    KT = [[sb(f"s2_K{n}_{i}", shp) for n, shp in (("a", [128, 512]), ("b", [128, 4, 512]), ("c", [128, 4, 512]))] for i in range(2)]
    VT = [[sb(f"s2_V{n}_{i}", shp) for n, shp in (("a", [128, 512]), ("b", [128, 4, 512]), ("c", [128, 4, 512]))] for i in range(2)]
    kvld = [dsem(f"s2kv{i}") for i in range(2)]
    selt = [sb(f"s2_selt{i}", [64, 128]) for i in range(2)]
    prod = [sb(f"s2_prod{i}", [128, 512]) for i in range(2)]
    Sall = [sb(f"s2_Sall{i}", [128, 3, 4, 8]) for i in range(2)]
    Om = [sb(f"s2_Om{i}", [8, 8, 65]) for i in range(2)]
    cnt = 0
    for b in range(NS):
        i = b % 2
        for (src, tiles) in ((ck, KT[i]), (cv, VT[i])):
            SP.dma(kvld[i], tiles[0].t[:], src[b, 1920 * 512:2048 * 512].rearrange("(p c) -> p c", c=512), writes=[tiles[0]])
            SP.dma(kvld[i], tiles[1].t[:], src[b, 1536 * 512:2048 * 512].rearrange("(p q c) -> p q c", q=4, c=512), writes=[tiles[1]])
            SP.dma(kvld[i], tiles[2].t[:], src[b, :].rearrange("(p s c) -> p s c", s=16, c=512)[:, 0:4, :], writes=[tiles[2]])
        SA = Sall[i]
        for l in range(4):
            tok = l * 16 + b
            st = selt[cnt % 2]
            DVE.op(lambda: V.tensor_copy(out=st.t[:], in_=ident.t[0:64, tok:tok + 1].to_broadcast([64, 128])), reads=[ident], writes=[st])
            Qb = PB[cnt % 2]
            mm(Qb.t[:, :], st.t[:], qr.t[:], True, True, [st, qr], [Qb])
            for pat, kt in ((0, KT[i][0].t[:, :]), (1, KT[i][1].t[:, l, :]), (2, KT[i][2].t[:, l, :])):
                pr = prod[pat % 2]
                tt(pr.t[:], kt, Qb.t[:, :], ALU.mult, [KT[i][pat], Qb], [pr])
                red(SA.t[:, pat, l, :], pr.t[:].rearrange("p (h d) -> p h d", h=8), [pr], [SA])
            cnt += 1
        act(SA.t[:], SA.t[:], AF.Exp, [SA], [SA], scale=0.125)
        tt(SA.t[:, 0, :, :], SA.t[:, 0, :, :], mask1.t[:].unsqueeze(2).to_broadcast([128, 4, 8]), ALU.mult, [SA, mask1], [SA])
        for l in range(4):
            tok = l * 16 + b
            pO = PB[2 + l % 2]
            zc = l % 2
            for pat, vt in ((0, VT[i][0].t[:, :]), (1, VT[i][1].t[:, l, :]), (2, VT[i][2].t[:, l, :])):
                mm(pO.t[0:8, :], SA.t[:, pat, l, :], vt, pat == 0, pat == 2, [SA, VT[i][pat]], [pO])
                mm(PB[6].t[0:8, zc:zc + 1], SA.t[:, pat, l, :], ones1.t[:], pat == 0, pat == 2, [SA, ones1], [PB[6]])
            om = Om[l % 2]
            tt(om.t[:, :, 0:64], pO.t[0:8, :].rearrange("p (h d) -> p h d", h=8), bd8.t[:].unsqueeze(2).to_broadcast([8, 8, 64]),
               ALU.mult, [pO, bd8], [om])
            DVE.op(lambda: V.tensor_scalar(out=om.t[:, :, 64], in0=bd8.t[:], scalar1=PB[6].t[0:8, zc:zc + 1], scalar2=None,
                                           op0=ALU.mult), reads=[bd8, PB[6]], writes=[om])
            last = (b == NS - 1 and l == 3)
            for hb in range(2):
                mm(PB[4 + hb].t[0:64, 0:260], selall.t[:, 63 - tok:127 - tok], om.t[:, 4 * hb:4 * hb + 4, :].rearrange("p h e -> p (h e)"),
                   False, last, [selall, om], [PB[4 + hb]], skip=True)
    zr = sb("s2_zr", [64, 8])
    for hb in range(2):
        pa = PB[4 + hb].t[0:64, 0:260].rearrange("p (h e) -> p h e", h=4)
        DVE.op(lambda: V.reciprocal(out=zr.t[:, 4 * hb:4 * hb + 4], in_=pa[:, :, 64]), reads=[PB[4 + hb]], writes=[zr])
        tt(ymix.t[:, 1536 + 256 * hb:1536 + 256 * (hb + 1)].rearrange("p (h d) -> p h d", h=4), pa[:, :, 0:64],
           zr.t[:, 4 * hb:4 * hb + 4].unsqueeze(2).to_broadcast([64, 4, 64]), ALU.mult, [PB[4 + hb], zr], [ymix])
    if dbg is not None:
        out_toks.append(POOL.dma(ld, dbg["d_att"][:, :], ymix.t[:, 1536:2048], reads=[ymix]))
    close3()
    if stage < 3:
        return

    sb = K["sb"]
    h1 = sb("s2_h1", [128, 1024])
    lg = [sb(f"s2_lng{i}", [128, 1024]) for i in range(2)]
    sb, close4 = K["scope"]()

    def bl(dst, name):
        SP.dma(ld, dst.t[:], T[name].rearrange("o e -> (o e)").partition_broadcast(128), writes=[dst])

    def layer_norm(r, gname, bname, out):
        bl(lg[0], gname)
        bl(lg[1], bname)
        stats = sb("s2_st_" + gname, [128, 2, 6])
        mv = sb("s2_mv_" + gname, [128, 2])
        rsd = sb("s2_rs_" + gname, [128, 1])
        for c in range(2):
            DVE.op(lambda c=c: V.bn_stats(out=stats.t[:, c, :], in_=r.t[:, c * 512:(c + 1) * 512]), reads=[r], writes=[stats])
        DVE.op(lambda: V.bn_aggr(out=mv.t[:], in_=stats.t[:].rearrange("p c s -> p (c s)")), reads=[stats], writes=[mv])
        DVE.op(lambda: V.tensor_scalar(out=rsd.t[:], in0=mv.t[:, 1:2], scalar1=LN_EPS, scalar2=None, op0=ALU.add), reads=[mv], writes=[rsd])
        act(rsd.t[:], rsd.t[:], AF.Sqrt, [rsd], [rsd])
        DVE.op(lambda: V.reciprocal(out=rsd.t[:], in_=rsd.t[:]), reads=[rsd], writes=[rsd])
        DVE.op(lambda: V.tensor_scalar(out=out.t[:], in0=r.t[:], scalar1=mv.t[:, 0:1], scalar2=rsd.t[:, 0:1], op0=ALU.subtract, op1=ALU.mult),
               reads=[r, mv, rsd], writes=[out])
        tt(out.t[:], out.t[:], lg[0].t[:], ALU.mult, [out, lg[0]], [out])
        tt(out.t[:], out.t[:], lg[1].t[:], ALU.add, [out, lg[1]], [out])

    ymT = sb("s2_ymT", [128, 16, 128])
    for kc in range(16):
        qd = PB[7].t[:, (kc % 4) * 64:(kc % 4) * 64 + 64]
        PE.op(lambda: nc.tensor.transpose(out=qd, in_=ymix.t[:, kc * 128:(kc + 1) * 128], identity=ident.t[0:64, 0:64]),
              reads=[ymix, ident], writes=[PB[7]])
        cp(ymT.t[:, kc, 0:64], qd, [PB[7]], [ymT], eng="act")
        cp(ymT.t[:, kc, 64:128], qd, [PB[7]], [ymT], eng="dve")
    wob = sb("s2_wob", [128, 16, 1024])
    for pc in range(4):
        SP.dma(ld, wob.t[:, 4 * pc:4 * pc + 4, :], T["s2_wout"][:, 4 * pc:4 * pc + 4, :], writes=[wob])
    xres = sb("s2_xres", [128, 1024])
    SP.dma(ld, xres.t[:], T["s2_x"][:, :], writes=[xres])
    r1 = sb("s2_r1", [128, 1024])
    for n in range(2):
        for kc in range(16):
            mm(PB[n].t[:, :], ymT.t[:, kc, :], wob.t[:, kc, n * 512:(n + 1) * 512], kc == 0, kc == 15, [ymT, wob], [PB[n]])
        DVE.op(lambda n=n: V.scalar_tensor_tensor(out=r1.t[:, n * 512:(n + 1) * 512], in0=xres.t[:, n * 512:(n + 1) * 512], scalar=ALPHA,
                                                  in1=PB[n].t[:, :], op0=ALU.mult, op1=ALU.add), reads=[xres, PB[n]], writes=[r1])
    layer_norm(r1, "s2_ln1g", "s2_ln1b", h1)
    if dbg is not None:
        out_toks.append(POOL.dma(ld, dbg["d_h1"][:, :], h1.t[:], reads=[h1]))
    close4()
    if stage < 4:
        return

    sb, close5 = K["scope"]()
    h1T = sb("s2_h1T", [128, 8, 128])
    for kc in range(8):
        qd = PB[7].t[:, (kc % 4) * 128:(kc % 4 + 1) * 128]
        PE.op(lambda: nc.tensor.transpose(out=qd, in_=h1.t[:, kc * 128:(kc + 1) * 128], identity=ident.t[:]), reads=[h1, ident], writes=[PB[7]])
        cp(h1T.t[:, kc, :], qd, [PB[7]], [h1T], eng=("act" if kc % 2 else "dve"))
    wqb = sb("s2_wqb", [128, 8, 2048])
    for pc in range(4):
        SP.dma(ld, wqb.t[:, :, 512 * pc:512 * (pc + 1)], T["s2_wq"][:, :, 512 * pc:512 * (pc + 1)], writes=[wqb])
    kTb = sb("s2_kTs", [128, 2, 128])
    SP.dma(ld, kTb.t[:, 0, :], T["s2_k1T"][:, :], writes=[kTb])
    SP.dma(ld, kTb.t[:, 1, :], T["s2_k2T"][:, :], writes=[kTb])
    qTb = sb("s2_qTb", [128, 16, 128])
    for j in range(16):
        P = PB[2 + j // 4]
        for kc in range(8):
            mm(P.t[:, (j % 4) * 128:(j % 4 + 1) * 128], wqb.t[:, kc, j * 128:(j + 1) * 128], h1T.t[:, kc, :], kc == 0, kc == 7, [wqb, h1T], [P])
        if j % 4 == 3:
            cp(qTb.t[:, j - 3:j + 1, :], P.t[:, :].rearrange("p (j t) -> p j t", j=4), [P], [qTb], eng=("act" if (j // 4) % 2 else "dve"))
    Ssb = sb("s2_Ssb", [128, 16, 128])
    S2b = sb("s2_S2b", [128, 16, 128])
    sbanks = [PB[0], PB[1], PB[6], PB[7]]
    for j in range(16):
        P = sbanks[j // 4]
        mm(P.t[:, (j % 4) * 128:(j % 4 + 1) * 128], qTb.t[:, j, :], kTb.t[:, j % 2, :], True, True, [qTb, kTb], [P])
        if j % 4 == 3:
            cp(Ssb.t[:, j - 3:j + 1, :], P.t[:, :].rearrange("p (j t) -> p j t", j=4), [P], [Ssb], eng=("act" if (j // 4) % 2 else "dve"))
    vals = sb("s2_vals", [128, 16, 16])
    idx = sb("s2_idx", [128, 16, 16], U32)

    def top16(src, tmp, vout, iout, rd):
        DVE.op(lambda: V.max(out=vout[:, 0:8], in_=src), reads=rd, writes=[vals_b])
        DVE.op(lambda: V.max_index(out=iout[:, 0:8], in_max=vout[:, 0:8], in_values=src), reads=rd + [vals_b], writes=[idx_b])
        DVE.op(lambda: V.match_replace(out=tmp, in_to_replace=vout[:, 0:8], in_values=src, imm_value=-1e30), reads=rd + [vals_b], writes=[tmp_b])
        DVE.op(lambda: V.max(out=vout[:, 8:16], in_=tmp), reads=[tmp_b], writes=[vals_b])
        DVE.op(lambda: V.max_index(out=iout[:, 8:16], in_max=vout[:, 8:16], in_values=tmp), reads=[tmp_b, vals_b], writes=[idx_b])

    vals_b, idx_b, tmp_b = vals, idx, S2b
    for j in range(16):
        top16(Ssb.t[:, j, :], S2b.t[:, j, :], vals.t[:, j, :], idx.t[:, j, :], [Ssb])
    idxf = sb("s2_idxf", [128, 16, 16])
    cp(idxf.t[:], idx.t[:], [idx], [idxf], eng="dve")
    cand = sb("s2_cand", [128, 8, 256])
    cand2 = sb("s2_cand2", [128, 8, 256])
    v4v = vals.t[:].rearrange("p (h s) a -> p h s a", s=2)
    tt(cand.t[:].rearrange("p h (a b) -> p h a b", a=16), v4v[:, :, 0, :].unsqueeze(3).to_broadcast([128, 8, 16, 16]),
       v4v[:, :, 1, :].unsqueeze(2).to_broadcast([128, 8, 16, 16]), ALU.add, [vals], [cand])
    sc = sb("s2_sc", [128, 8, 16])
    ci = sb("s2_ci", [128, 8, 16], U32)
    vals_b, idx_b, tmp_b = sc, ci, cand2
    for h in range(8):
        top16(cand.t[:, h, :], cand2.t[:, h, :], sc.t[:, h, :], ci.t[:, h, :], [cand])
    au = sb("s2_au", [128, 8, 16], U32)
    bu = sb("s2_bu", [128, 8, 16], U32)
    af = sb("s2_af", [128, 8, 16])
    bf = sb("s2_bf", [128, 8, 16])
    DVE.op(lambda: V.tensor_scalar(out=au.t[:], in0=ci.t[:], scalar1=4, scalar2=None, op0=ALU.logical_shift_right), reads=[ci], writes=[au])
    DVE.op(lambda: V.tensor_scalar(out=bu.t[:], in0=ci.t[:], scalar1=15, scalar2=None, op0=ALU.bitwise_and), reads=[ci], writes=[bu])
    cp(af.t[:], au.t[:], [au], [af], eng="dve")
    cp(bf.t[:], bu.t[:], [bu], [bf], eng="dve")
    eq = sb("s2_eq", [128, 8, 16, 16])
    isel = sb("s2_isel", [128, 2, 8, 16])
    i4 = idxf.t[:].rearrange("p (h s) a -> p h s a", s=2)
    io4 = iota16.t[:].unsqueeze(1).unsqueeze(1).to_broadcast([128, 8, 16, 16])
    for side, xf in ((0, af), (1, bf)):
        tt(eq.t[:], xf.t[:].unsqueeze(3).to_broadcast([128, 8, 16, 16]), io4, ALU.is_equal, [xf, iota16], [eq])
        tt(eq.t[:], eq.t[:], i4[:, :, side, :].unsqueeze(2).to_broadcast([128, 8, 16, 16]), ALU.mult, [eq, idxf], [eq])
        DVE.op(lambda side=side: V.tensor_reduce(out=isel.t[:, side, :, :].rearrange("p h k -> p (h k)"),
                                                 in_=eq.t[:].rearrange("p h k a -> p (h k) a"), axis=AX.X, op=ALU.add),
               reads=[eq], writes=[isel])
    ef = sb("s2_ef", [128, 128])
    eu = sb("s2_eu", [128, 128], U32)
    DVE.op(lambda: V.scalar_tensor_tensor(out=ef.t[:], in0=isel.t[:, 0, :, :].rearrange("p h k -> p (h k)"), scalar=128.0,
                                          in1=isel.t[:, 1, :, :].rearrange("p h k -> p (h k)"), op0=ALU.mult, op1=ALU.add),
           reads=[isel], writes=[ef])
    cp(eu.t[:], ef.t[:], [ef], [eu], eng="dve")
    gt = sb("s2_gt", [128, 8, 16])
    gs = sb("s2_gs", [128, 8])
    tt(gt.t[:], sc.t[:], sc.t[:, :, 0:1].to_broadcast([128, 8, 16]), ALU.subtract, [sc], [gt])
    act(gt.t[:], gt.t[:], AF.Exp, [gt], [gt])
    red(gs.t[:], gt.t[:], [gt], [gs])
    DVE.op(lambda: V.reciprocal(out=gs.t[:], in_=gs.t[:]), reads=[gs], writes=[gs])
    tt(gt.t[:], gt.t[:], gs.t[:].unsqueeze(2).to_broadcast([128, 8, 16]), ALU.mult, [gt, gs], [gt])
    NB = 4
    ub = [sb(f"s2_ub{i}", [128, 1024]) for i in range(NB)]
    ug = [dsem(f"s2ug{i}") for i in range(NB)]
    junk = sb("s2_junk", [128, 1024])
    hid = sb("s2_hid", [128, 128])

    def gather(slot, table):
        u = ub[gather.n % NB]
        d = ug[gather.n % NB]
        gather.n += 1
        POOL.deps([eu], [u])
        inst = G.indirect_dma_start(out=u.t[:], out_offset=None, in_=T[table][:, :],
                                    in_offset=bass.IndirectOffsetOnAxis(ap=eu.t[:, slot:slot + 1], axis=0))
        d.n += 16
        inst.then_inc(d.sem, 16)
        tok = (d.sem, d)
        POOL._mark(tok, [eu], [u])
        return u
    gather.n = 0
    for slot in range(128):
        u = gather(slot, "s2_pu")
        DVE.op(lambda: V.scalar_tensor_tensor(out=junk.t[:], in0=u.t[:], scalar=1.0, in1=h1.t[:], op0=ALU.mult, op1=ALU.mult,
                                              accum_out=hid.t[:, slot:slot + 1]), reads=[u, h1], writes=[junk, hid])
    act(hid.t[:], hid.t[:], AF.Gelu, [hid], [hid])
    tt(hid.t[:], hid.t[:], gt.t[:].rearrange("p h k -> p (h k)"), ALU.mult, [hid, gt], [hid])
    pacc = sb("s2_pacc", [128, 1024])
    for slot in range(128):
        u = gather(slot, "s2_pv")
        if slot == 0:
            DVE.op(lambda: V.tensor_scalar(out=pacc.t[:], in0=u.t[:], scalar1=hid.t[:, 0:1], scalar2=None, op0=ALU.mult),
                   reads=[u, hid], writes=[pacc])
        else:
            DVE.op(lambda: V.scalar_tensor_tensor(out=pacc.t[:], in0=u.t[:], scalar=hid.t[:, slot:slot + 1], in1=pacc.t[:],
                                                  op0=ALU.mult, op1=ALU.add), reads=[u, hid, pacc], writes=[pacc])
    r2 = pacc
    DVE.op(lambda: V.scalar_tensor_tensor(out=r2.t[:], in0=h1.t[:], scalar=ALPHA, in1=pacc.t[:], op0=ALU.mult, op1=ALU.add),
           reads=[h1, pacc], writes=[r2])
    yo = junk
    layer_norm(r2, "s2_ln2g", "s2_ln2b", yo)
    out_toks.append(POOL.dma(ld, ys_out[:, :], yo.t[0:64, :], reads=[yo]))
    close5()


TAIL_TILES = 16

F32 = mybir.dt.float32
BF16 = mybir.dt.bfloat16
ALU = mybir.AluOpType
AF = mybir.ActivationFunctionType
AX = mybir.AxisListType
LN_EPS = 1e-5
SEQ = 8192
X0, B0, C0, Z0, Q0, QS0, K0, KS0, V0, DT0, NW2 = 0, 384, 512, 640, 1024, 1152, 1280, 1408, 1536, 1664, 1670
BLK = 256
HALO = 3


def p2_inputs(seq):
    return {
        "p_xT": [128, 8, seq], "p_w": [128, 8, NW2], "p_cw": [4, 640], "p_cb": [1, 640], "p_dtb": [1, 6], "p_alog": [1, 6],
        "p_dsk": [1, 6], "p_ng": [1, 384], "p_cosT": [128, seq], "p_sinT": [128, seq], "p_tri": [128, 128], "p_trii": [128, 128],
        "p_maskT": [128, 2, 128], "p_esel": [65, 64], "p_ident": [128, 128],
    }


def p2_host_inputs(b, g, I, seq=SEQ):
    f = lambda a: np.ascontiguousarray(np.asarray(a, dtype=np.float32))
    sw = np.concatenate([np.arange(32, 64), np.arange(0, 32)])
    swap2 = np.concatenate([sw, 64 + sw])
    qc = 4120 + 128 * g + np.arange(128)
    kc = 4632 + 128 * g + np.arange(128)
    cols = np.concatenate([1536 + 384 * g + np.arange(384), 3072 + 128 * g + np.arange(128), 3584 + 128 * g + np.arange(128),
                           384 * g + np.arange(384), qc, qc[swap2], kc, kc[swap2], 5144 + 128 * g + np.arange(128),
                           4096 + 6 * g + np.arange(6)])
    ccols = np.concatenate([384 * g + np.arange(384), 1536 + 128 * g + np.arange(128), 2048 + 128 * g + np.arange(128)])
    inv = (10000.0 ** (-np.arange(32, dtype=np.float32) / 32)).astype(np.float32)
    ang = np.arange(seq, dtype=np.float32)[None, :] * inv[:, None]
    cos64 = np.concatenate([np.cos(ang), np.cos(ang)], axis=0)
    sin64 = np.concatenate([-np.sin(ang), np.sin(ang)], axis=0)
    i_ = np.arange(128)
    return {
        "p_xT": f(I["x_prompt"][b, :seq].T.reshape(8, 128, seq).transpose(1, 0, 2)),
        "p_w": f(I["w_in"][0][:, cols].reshape(8, 128, NW2).transpose(1, 0, 2)),
        "p_cw": f(I["conv_w"][0][:, ccols]), "p_cb": f(I["conv_b"][0][ccols][None]),
        "p_dtb": f(I["dt_bias"][0][6 * g:6 * g + 6][None]), "p_alog": f(I["a_log"][0][6 * g:6 * g + 6][None]),
        "p_dsk": f(I["d_skip"][0][6 * g:6 * g + 6][None]), "p_ng": f(I["ssd_norm_g"][0][384 * g:384 * g + 384][None]),
        "p_cosT": f(np.concatenate([cos64, cos64], axis=0)), "p_sinT": f(np.concatenate([sin64, sin64], axis=0)),
        "p_tri": f((i_[:, None] > i_[None, :])), "p_trii": f((i_[:, None] <= i_[None, :])),
        "p_maskT": f(np.stack([(i_[:, None] >= i_[None, :]), (i_[:, None] <= i_[None, :])], axis=1)),
        "p_esel": f(np.concatenate([np.zeros((64, 64)), np.ones((1, 64))], axis=0)), "p_ident": np.eye(128, dtype=np.float32),
    }


def emit_p2(nc, K, T, yssd_out, attT_out, seq=SEQ, dbg=None, att3d=False):
    PE, DVE, ACT, POOL, SP = K["PE"], K["DVE"], K["ACT"], K["POOL"], K["SP"]
    sb, dsem, out_toks, PB = K["sb"], K["dsem"], K["out_toks"], K["PB"]
    V, G, S = nc.vector, nc.gpsimd, nc.scalar
    NT = seq // 128
    NBLK = seq // BLK

    def mm(out, lhsT, rhs, start, stop, reads, writes):
        PE.op(lambda: nc.tensor.matmul(out, lhsT=lhsT, rhs=rhs, start=start, stop=stop), reads=reads, writes=writes)

    def tt(out, in0, in1, op, reads, writes, eng=None):
        E_, e_ = (DVE, V) if eng is None else eng
        E_.op(lambda: e_.tensor_tensor(out=out, in0=in0, in1=in1, op=op), reads=reads, writes=writes)

    def act(out, in_, func, reads, writes, **kw):
        ACT.op(lambda: S.activation(out=out, in_=in_, func=func, **kw), reads=reads, writes=writes)

    def cp(out, in_, reads, writes, eng="act"):
        if eng == "act":
            ACT.op(lambda: S.copy(out=out, in_=in_), reads=reads, writes=writes)
        else:
            DVE.op(lambda: V.tensor_copy(out=out, in_=in_), reads=reads, writes=writes)

    ld = dsem("p2ld")
    st = dsem("p2st")

    def bload(name, n):
        b = sb("r_" + name, [128, n])
        SP.dma(ld, b.t[:], T[name].rearrange("o e -> (o e)").partition_broadcast(128), writes=[b])
        return b

    qT_all = sb("p_qT", [128, seq], BF16)
    kT_all = sb("p_kT", [128, seq], BF16)
    vT_all = sb("p_vT", [128, seq])
    maskT = sb("p_maskTt", [128, 2, 128])
    esel = sb("p_eselt", [65, 64])
    identf = sb("p_identf", [128, 128])
    SP.dma(ld, identf.t[:], T["p_ident"][:, :], writes=[identf])
    SP.dma(ld, maskT.t[:], T["p_maskT"][:, :, :], writes=[maskT])
    SP.dma(ld, esel.t[:], T["p_esel"][:, :], writes=[esel])

    sb, closeA = K["scope"]()
    tri = sb("p_trit", [128, 128])
    trii = sb("p_triit", [128, 128])
    SP.dma(ld, tri.t[:], T["p_tri"][:, :], writes=[tri])
    SP.dma(ld, trii.t[:], T["p_trii"][:, :], writes=[trii])
    onest = sb("p_onest", [128, 128])
    DVE.op(lambda: V.memset(onest.t[:], 1.0), writes=[onest])
    wrep = sb("p_wrep", [128, 4, 640])
    SP.dma(ld, wrep.t[:], T["p_cw"].rearrange("t e -> (t e)").partition_broadcast(128).rearrange("p (t e) -> p t e", t=4), writes=[wrep])
    cbrep = bload("p_cb", 640)
    dtbrep = bload("p_dtb", 6)
    negA = bload("p_alog", 6)
    dsk = bload("p_dsk", 6)
    ngr = bload("p_ng", 384)
    cbcol = sb("p_cbcol", [128, 2])
    SP.dma(ld, cbcol.t[:, 0:1], T["p_cb"][0:1, 384:512].rearrange("o e -> e o"), writes=[cbcol])
    SP.dma(ld, cbcol.t[:, 1:2], T["p_cb"][0:1, 512:640].rearrange("o e -> e o"), writes=[cbcol])
    act(negA.t[:], negA.t[:], AF.Exp, [negA], [negA])
    DVE.op(lambda: V.tensor_scalar(out=negA.t[:], in0=negA.t[:], scalar1=-1.0, scalar2=None, op0=ALU.mult), reads=[negA], writes=[negA])

    wb = sb("p_wb", [128, 8, NW2], BF16)
    wf = sb("p_wf", [128, 4, 8, 640], BF16)
    wst = [sb(f"p_wst{i}", [128, 8, 128]) for i in range(2)]
    wld = [dsem(f"p2wld{i}") for i in range(2)]
    for i_, o in enumerate(range(0, NW2, 128)):
        n = min(128, NW2 - o)
        s_ = wst[i_ % 2]
        SP.dma(wld[i_ % 2], s_.t[:, :, 0:n], T["p_w"][:, :, o:o + n], writes=[s_])
        cp(wb.t[:, :, o:o + n], s_.t[:, :, 0:n], [s_], [wb], eng="dve")
        if o < 640:
            n2 = min(n, 640 - o)
            for tap in range(4):
                eng = (POOL, G) if tap % 2 else (DVE, V)
                tt(wf.t[:, tap, :, o:o + n2], s_.t[:, :, 0:n2], wrep.t[:, tap, o:o + n2].unsqueeze(1).to_broadcast([128, 8, n2]),
                   ALU.mult, [s_, wrep], [wf], eng=eng)

    xst_ = sb("p_xst", [128, 8, HALO + BLK])
    xst = [xst_, xst_]
    xtb = [sb(f"p_xtb{i}", [128, 8, HALO + BLK], BF16) for i in range(2)]
    xld = [dsem(f"p2xld{i}") for i in range(2)]
    xcount = [0]

    def load_block(blk):
        i = xcount[0] % 2
        xcount[0] += 1
        s_, tb = xst[i], xtb[i]
        if blk == 0:
            DVE.op(lambda: V.memset(s_.t[:, :, 0:HALO], 0.0), writes=[s_])
            SP.dma(xld[i], s_.t[:, :, HALO:HALO + BLK], T["p_xT"][:, :, 0:BLK], writes=[s_])
        else:
            SP.dma(xld[i], s_.t[:, :, :], T["p_xT"][:, :, blk * BLK - HALO:(blk + 1) * BLK], writes=[s_])
        eng, e = (DVE, V) if blk % 2 == 0 else (POOL, G)
        eng.op(lambda: e.tensor_copy(out=tb.t[:], in_=s_.t[:]), reads=[s_], writes=[tb])
        return tb

    psDT = PB[7]
    for blk in range(NBLK):
        tb = load_block(blk)
        for sub in range(BLK // 128):
            j = blk * (BLK // 128) + sub
            c0 = HALO + sub * 128
            for c in range(8):
                mm(psDT.t[:, j * 6:(j + 1) * 6], tb.t[:, c, c0:c0 + 128], wb.t[:, c, DT0:DT0 + 6], c == 0, c == 7, [tb, wb], [psDT])
    NC6 = NT * 6
    dts = sb("p_dts", [128, NC6])
    dtA = sb("p_dtA", [128, NC6])
    wloc = sb("p_wloc", [128, NC6])
    ainc = sb("p_ainc", [128, NC6])
    ea = sb("p_ea", [128, NC6])
    decrow = sb("p_decrow", [128, NC6])
    v3 = lambda ap: ap.rearrange("p (j h) -> p j h", h=6)
    tt(v3(dts.t[:]), v3(psDT.t[:, 0:NC6]), dtbrep.t[:].unsqueeze(1).to_broadcast([128, NT, 6]), ALU.add, [psDT, dtbrep], [dts])
    act(dts.t[:], dts.t[:], AF.Exp, [dts], [dts])
    act(dts.t[:], dts.t[:], AF.Ln, [dts], [dts], bias=1.0)
    tt(v3(dtA.t[:]), v3(dts.t[:]), negA.t[:].unsqueeze(1).to_broadcast([128, NT, 6]), ALU.mult, [dts, negA], [dtA])
    mm(PB[0].t[:, 0:NC6], tri.t[:], dtA.t[:], True, True, [tri, dtA], [PB[0]])
    mm(PB[1].t[:, 0:NC6], trii.t[:], dtA.t[:], True, True, [trii, dtA], [PB[1]])
    mm(PB[2].t[:, 0:NC6], onest.t[:], dtA.t[:], True, True, [onest, dtA], [PB[2]])
    act(wloc.t[:], PB[0].t[:, 0:NC6], AF.Exp, [PB[0]], [wloc])
    tt(wloc.t[:], wloc.t[:], dts.t[:], ALU.mult, [wloc, dts], [wloc])
    cp(ainc.t[:], PB[1].t[:, 0:NC6], [PB[1]], [ainc])
    act(ea.t[:], PB[1].t[:, 0:NC6], AF.Exp, [PB[1]], [ea])
    act(decrow.t[:], PB[2].t[:, 0:NC6], AF.Exp, [PB[2]], [decrow])

    psA, psSt, psZ, psKV, psBC, psSeg, psY, psYo = PB[0], PB[1], PB[2], PB[3], PB[4], PB[5], PB[6], PB[7]
    pre = sb("p_pre", [128, 512])
    xs = sb("p_xs", [128, 384])
    Btok = sb("p_Btok", [128, 128], BF16)
    xdt = sb("p_xdt", [128, 384], BF16)
    xwl = sb("p_xwl", [128, 384], BF16)
    BT = sb("p_BT", [128, 128], BF16)
    CT = sb("p_CT", [128, 128], BF16)
    Gm = sb("p_Gm", [128, 128])
    Uh = [sb(f"p_Uh{i}", [128, 128]) for i in range(2)]
    Eh = [sb(f"p_Eh{i}", [128, 128]) for i in range(2)]
    Mh = [sb(f"p_Mh{i}", [128, 128], BF16) for i in range(2)]
    hT = sb("p_hT", [128, 384])
    hTb = sb("p_hTb", [128, 384], BF16)
    DVE.op(lambda: V.memset(hT.t[:], 0.0), writes=[hT])
    DVE.op(lambda: V.memset(hTb.t[:], 0.0), writes=[hTb])
    zs = sb("p_zs", [128, 384])
    yt = sb("p_yt", [128, 384])
    ytmp = sb("p_ytmp", [128, 384])
    ss = sb("p_ss", [128, 1])
    yo = [sb(f"p_yo{i}", [128, 384]) for i in range(2)]
    yst = [dsem(f"p2yst{i}") for i in range(2)]
    cst = [sb(f"p_cst{i}", [128, 2, 128]) for i in range(2)]
    cld = [dsem(f"p2cld{i}") for i in range(2)]
    rq = sb("p_rq", [128, 128])
    rq2 = sb("p_rq2", [128, 128])

    for blk in range(NBLK):
        tb = load_block(blk)
        for sub in range(BLK // 128):
            j = blk * (BLK // 128) + sub
            i = j % 2
            c0 = HALO + sub * 128
            t0 = j * 128
            SP.dma(cld[i], cst[i].t[:, 0, :], T["p_cosT"][:, t0:t0 + 128], writes=[cst[i]])
            SP.dma(cld[i], cst[i].t[:, 1, :], T["p_sinT"][:, t0:t0 + 128], writes=[cst[i]])
            n = 0
            for tap in range(4):
                for c in range(8):
                    s0 = c0 - 3 + tap
                    mm(psA.t[:, :], tb.t[:, c, s0:s0 + 128], wf.t[:, tap, c, 0:512], n == 0, n == 31, [tb, wf], [psA])
                    n += 1
            for which in range(2):
                n = 0
                for tap in range(4):
                    for c in range(8):
                        s0 = c0 - 3 + tap
                        mm(psBC.t[:, which * 128:(which + 1) * 128], wf.t[:, tap, c, 384 + which * 128:512 + which * 128],
                           tb.t[:, c, s0:s0 + 128], n == 0, n == 31, [tb, wf], [psBC])
                        n += 1
            for c in range(8):
                mm(psZ.t[:, 0:384], tb.t[:, c, c0:c0 + 128], wb.t[:, c, Z0:Z0 + 384], c == 0, c == 7, [tb, wb], [psZ])
            for m in range(4):
                for c in range(8):
                    mm(psKV.t[:, m * 128:(m + 1) * 128], wb.t[:, c, Q0 + m * 128:Q0 + (m + 1) * 128], tb.t[:, c, c0:c0 + 128],
                       c == 0, c == 7, [tb, wb], [psKV])
            for c in range(8):
                mm(psZ.t[:, 384:512], wb.t[:, c, V0:V0 + 128], tb.t[:, c, c0:c0 + 128], c == 0, c == 7, [tb, wb], [psZ])
            tt(pre.t[:], psA.t[:, :], cbrep.t[:, 0:512], ALU.add, [psA, cbrep], [pre])
            act(xs.t[:], pre.t[:, 0:384], AF.Silu, [pre], [xs])
            act(Btok.t[:], pre.t[:, 384:512], AF.Silu, [pre], [Btok])
            act(BT.t[:], psBC.t[:, 0:128], AF.Silu, [psBC, cbcol], [BT], bias=cbcol.t[:, 0:1])
            act(CT.t[:], psBC.t[:, 128:256], AF.Silu, [psBC, cbcol], [CT], bias=cbcol.t[:, 1:2])
            act(zs.t[:], psZ.t[:, 0:384], AF.Silu, [psZ], [zs])
            cp(vT_all.t[:, t0:t0 + 128], psZ.t[:, 384:512], [psZ], [vT_all])
            for (dst, o) in ((qT_all, 0), (kT_all, 256)):
                tt(rq.t[:], psKV.t[:, o:o + 128], cst[i].t[:, 0, :], ALU.mult, [psKV, cst[i]], [rq])
                tt(rq2.t[:], psKV.t[:, o + 128:o + 256], cst[i].t[:, 1, :], ALU.mult, [psKV, cst[i]], [rq2])
                tt(dst.t[:, t0:t0 + 128], rq.t[:], rq2.t[:], ALU.add, [rq, rq2], [dst], eng=(POOL, G))
            dtj = dts.t[:, j * 6:(j + 1) * 6]
            tt(xdt.t[:].rearrange("p (h f) -> p h f", h=6), xs.t[:].rearrange("p (h f) -> p h f", h=6),
               dtj.unsqueeze(2).to_broadcast([128, 6, 64]), ALU.mult, [xs, dts], [xdt])
            tt(xwl.t[:].rearrange("p (h f) -> p h f", h=6), xs.t[:].rearrange("p (h f) -> p h f", h=6),
               wloc.t[:, j * 6:(j + 1) * 6].unsqueeze(2).to_broadcast([128, 6, 64]), ALU.mult, [xs, wloc], [xwl])
            mm(psBC.t[:, 256:384], BT.t[:], CT.t[:], True, True, [BT, CT], [psBC])
            tt(Gm.t[:], psBC.t[:, 256:384], trii.t[:], ALU.mult, [psBC, trii], [Gm])
            mm(psYo.t[:, 0:384], CT.t[:], hTb.t[:], True, True, [CT, hTb], [psYo])
            mm(psSt.t[:, 0:384], Btok.t[:], xwl.t[:], True, True, [Btok, xwl], [psSt])
            for h in range(6):
                u, e_, m_ = Uh[h % 2], Eh[h % 2], Mh[h % 2]
                DVE.op(lambda: V.tensor_scalar(out=u.t[:], in0=tri.t[:], scalar1=dtA.t[:, j * 6 + h:j * 6 + h + 1], scalar2=None, op0=ALU.mult),
                       reads=[tri, dtA], writes=[u])
                sg = psSeg.t[:, (h % 4) * 128:(h % 4 + 1) * 128]
                mm(sg, u.t[:], trii.t[:], True, True, [u, trii], [psSeg])
                act(e_.t[:], sg, AF.Exp, [psSeg], [e_])
                tt(m_.t[:], e_.t[:], Gm.t[:], ALU.mult, [e_, Gm], [m_], eng=((POOL, G) if h % 2 else None))
                mm(psY.t[:, h * 64:(h + 1) * 64], m_.t[:], xdt.t[:, h * 64:(h + 1) * 64], True, True, [m_, xdt], [psY])
            tt(yt.t[:].rearrange("p (h f) -> p h f", h=6), psYo.t[:, 0:384].rearrange("p (h f) -> p h f", h=6),
               ea.t[:, j * 6:(j + 1) * 6].unsqueeze(2).to_broadcast([128, 6, 64]), ALU.mult, [psYo, ea], [yt])
            tt(yt.t[:], yt.t[:], psY.t[:, 0:384], ALU.add, [yt, psY], [yt])
            tt(ytmp.t[:].rearrange("p (h f) -> p h f", h=6), xs.t[:].rearrange("p (h f) -> p h f", h=6),
               dsk.t[:].unsqueeze(2).to_broadcast([128, 6, 64]), ALU.mult, [xs, dsk], [ytmp])
            tt(yt.t[:], yt.t[:], ytmp.t[:], ALU.add, [yt, ytmp], [yt])
            tt(yt.t[:], yt.t[:], zs.t[:], ALU.mult, [yt, zs], [yt])
            DVE.op(lambda: V.scalar_tensor_tensor(out=ytmp.t[:], in0=yt.t[:], scalar=1.0, in1=yt.t[:], op0=ALU.mult, op1=ALU.mult,
                                                  accum_out=ss.t[:, 0:1]), reads=[yt], writes=[ytmp, ss])
            DVE.op(lambda: V.tensor_scalar(out=ss.t[:], in0=ss.t[:], scalar1=1.0 / 384.0, scalar2=LN_EPS, op0=ALU.mult, op1=ALU.add),
                   reads=[ss], writes=[ss])
            act(ss.t[:], ss.t[:], AF.Sqrt, [ss], [ss])
            DVE.op(lambda: V.reciprocal(out=ss.t[:], in_=ss.t[:]), reads=[ss], writes=[ss])
            DVE.op(lambda: V.scalar_tensor_tensor(out=yo[i].t[:], in0=yt.t[:], scalar=ss.t[:, 0:1], in1=ngr.t[:], op0=ALU.mult, op1=ALU.mult),
                   reads=[yt, ss, ngr], writes=[yo[i]])
            out_toks.append(POOL.dma(yst[i], yssd_out[t0:t0 + 128, :], yo[i].t[:], reads=[yo[i]]))
            tt(hT.t[:].rearrange("p (h f) -> p h f", h=6), hT.t[:].rearrange("p (h f) -> p h f", h=6),
               decrow.t[:, j * 6:(j + 1) * 6].unsqueeze(2).to_broadcast([128, 6, 64]), ALU.mult, [hT, decrow], [hT])
            tt(hT.t[:], hT.t[:], psSt.t[:, 0:384], ALU.add, [hT, psSt], [hT])
            cp(hTb.t[:], hT.t[:], [hT], [hTb], eng="dve")
    closeA()

    sb, closeB = K["scope"]()
    NSB = seq // 2048
    TPS = 48
    Vp = [sb(f"p_Vp{i}", [128, TPS, 2, 65], BF16) for i in range(2)]
    for v_ in Vp:
        DVE.op(lambda v_=v_: V.memset(v_.t[:], 1.0), writes=[v_])
    acc = [sb(f"p_acc{i}", [65, 2048]) for i in range(2)]
    Pt = [sb(f"p_Pt{i}", [128, 2, 128]) for i in range(2)]
    Pb = [sb(f"p_Pb{i}", [128, 2, 128], BF16) for i in range(2)]
    zr = sb("p_zr", [64, 512])
    ao = [sb(f"p_ao{i}", [64, 512]) for i in range(2)]
    ast = [dsem(f"p2ast{i}") for i in range(2)]
    pats = ((1, 0), (4, 16), (16, 32))

    def tile_id(d, off, r, nbl):
        return off + r * (16 // d) + nbl

    cnt = 0
    for sbk in range(NSB):
        vp = Vp[sbk % 2]
        base = sbk * 2048
        for d, off in pats:
            for r in range(d):
                for nbl in range(16 // d):
                    start = base + r + d * 128 * nbl
                    pst = PB[6 + cnt % 2]
                    PE.op(lambda: nc.tensor.transpose(out=pst.t[:, 0:128], in_=vT_all.t[:, start:start + d * 127 + 1:d], identity=identf.t[:]),
                          reads=[vT_all, identf], writes=[pst])
                    cp(vp.t[:, tile_id(d, off, r, nbl), :, 0:64], pst.t[:, 0:128].rearrange("p (h e) -> p h e", h=2), [pst], [vp],
                       eng=("act" if cnt % 2 else "dve"))
                    cnt += 1
        for h in range(2):
            ac = acc[h]
            DVE.op(lambda: V.memset(ac.t[:], 0.0), writes=[ac])
            hs = slice(64 * h, 64 * h + 64)
            u_ = 0
            for d, off in pats:
                for r in range(d):
                    for nbl in range(16 // d):
                        qs = base + r + d * 128 * nbl
                        qsl = slice(qs, qs + d * 127 + 1, d)
                        blocks = []
                        if nbl > 0:
                            blocks.append((0, vp, tile_id(d, off, r, nbl - 1), qs - d * 128))
                        elif sbk > 0:
                            blocks.append((0, Vp[(sbk - 1) % 2], tile_id(d, off, r, 16 // d - 1), qs - d * 128))
                        blocks.append((1, vp, tile_id(d, off, r, nbl), qs))
                        psc = PB[u_ % 2]
                        pt, pb = Pt[u_ % 2], Pb[u_ % 2]
                        for kb, _, _, ks in blocks:
                            mm(psc.t[:, kb * 128:(kb + 1) * 128], kT_all.t[hs, ks:ks + d * 127 + 1:d], qT_all.t[hs, qsl], True, True,
                               [kT_all, qT_all], [psc])
                        k0 = blocks[0][0]
                        act(pt.t[:, k0:2, :], psc.t[:, k0 * 128:256].rearrange("p (a q) -> p a q", q=128), AF.Exp, [psc], [pt], scale=0.125)
                        tt(pb.t[:, k0:2, :], pt.t[:, k0:2, :], maskT.t[:, k0:2, :], ALU.mult, [pt, maskT], [pb], eng=((POOL, G) if u_ % 2 else None))
                        po = PB[2 + u_ % 2]
                        for n_, (kb, vsrc, tid, _) in enumerate(blocks):
                            mm(po.t[0:65, 0:128], vsrc.t[:, tid, h, :], pb.t[:, kb, :], n_ == 0, n_ == len(blocks) - 1, [vsrc, pb], [po])
                        asl = ac.t[:, r + d * 128 * nbl:r + d * 128 * nbl + d * 127 + 1:d]
                        tt(asl, asl, po.t[0:65, 0:128], ALU.add, [ac, po], [ac])
                        u_ += 1
            for c4 in range(4):
                pz = PB[4 + c4 % 2]
                mm(pz.t[0:64, :], esel.t[:, :], ac.t[:, c4 * 512:(c4 + 1) * 512], True, True, [esel, ac], [pz])
                DVE.op(lambda: V.reciprocal(out=zr.t[:], in_=pz.t[0:64, :]), reads=[pz], writes=[zr])
                a_ = ao[c4 % 2]
                tt(a_.t[:], ac.t[0:64, c4 * 512:(c4 + 1) * 512], zr.t[:], ALU.mult, [ac, zr], [a_])
                dst_ = (attT_out[sbk, 64 * h:64 * h + 64, c4 * 512:(c4 + 1) * 512] if att3d
                        else attT_out[64 * h:64 * h + 64, base + c4 * 512:base + (c4 + 1) * 512])
                out_toks.append(POOL.dma(ast[c4 % 2], dst_, a_.t[:], reads=[a_]))
    closeB()


F32 = mybir.dt.float32
U32 = mybir.dt.uint32
ALU = mybir.AluOpType
AF = mybir.ActivationFunctionType
AX = mybir.AxisListType
ALPHA = 2.0 ** 0.25
LN_EPS = 1e-5

TAIL_INPUTS = lambda ntok: {
    "t_ymT": [128, 16, ntok], "t_x": [ntok, 1024], "t_wout": [128, 16, 1024], "t_ln1g": [1, 1024], "t_ln1b": [1, 1024],
    "t_ln2g": [1, 1024], "t_ln2b": [1, 1024], "t_wq": [128, 8, 2048], "t_k1T": [128, 128], "t_k2T": [128, 128],
    "t_pu": [16384, 1024], "t_pv": [16384, 1024], "t_ident": [128, 128], "t_iota16": [128, 16],
}


def emit_tail(nc, K, T, y_out, ntiles, ag=None):
    import concourse.bass as bass
    PE, DVE, ACT, POOL, SP = K["PE"], K["DVE"], K["ACT"], K["POOL"], K["SP"]
    sb, dsem, out_toks, PB = K["sb"], K["dsem"], K["out_toks"], K["PB"]
    V, G, S = nc.vector, nc.gpsimd, nc.scalar

    def mm(out, lhsT, rhs, start, stop, reads, writes):
        PE.op(lambda: nc.tensor.matmul(out, lhsT=lhsT, rhs=rhs, start=start, stop=stop), reads=reads, writes=writes)

    def tt(out, in0, in1, op, reads, writes):
        DVE.op(lambda: V.tensor_tensor(out=out, in0=in0, in1=in1, op=op), reads=reads, writes=writes)

    def act(out, in_, func, reads, writes, **kw):
        ACT.op(lambda: S.activation(out=out, in_=in_, func=func, **kw), reads=reads, writes=writes)

    def cp(out, in_, reads, writes, eng="act"):
        if eng == "act":
            ACT.op(lambda: S.copy(out=out, in_=in_), reads=reads, writes=writes)
        else:
            DVE.op(lambda: V.tensor_copy(out=out, in_=in_), reads=reads, writes=writes)

    def red(out, in_, reads, writes):
        DVE.op(lambda: V.tensor_reduce(out=out, in_=in_, axis=AX.X, op=ALU.add), reads=reads, writes=writes)

    ld = dsem("tld")
    ident = sb("t_identt", [128, 128])
    iota16 = sb("t_iotat", [128, 16])
    SP.dma(ld, ident.t[:], T["t_ident"][:, :], writes=[ident])
    SP.dma(ld, iota16.t[:], T["t_iota16"][:, :], writes=[iota16])
    lnp = {}
    for nm in ("t_ln1g", "t_ln1b", "t_ln2g", "t_ln2b"):
        lnp[nm] = sb("r_" + nm, [128, 1024])
        SP.dma(ld, lnp[nm].t[:], T[nm].rearrange("o e -> (o e)").partition_broadcast(128), writes=[lnp[nm]])
    kTb = sb("t_kT", [128, 2, 128])
    SP.dma(ld, kTb.t[:, 0, :], T["t_k1T"][:, :], writes=[kTb])
    SP.dma(ld, kTb.t[:, 1, :], T["t_k2T"][:, :], writes=[kTb])

    stats = sb("t_stats", [128, 2, 6])
    mv = sb("t_mv", [128, 2])
    rsd = sb("t_rsd", [128, 1])

    def layer_norm(r, g, b, out):
        for c in range(2):
            DVE.op(lambda c=c: V.bn_stats(out=stats.t[:, c, :], in_=r.t[:, c * 512:(c + 1) * 512]), reads=[r], writes=[stats])
        DVE.op(lambda: V.bn_aggr(out=mv.t[:], in_=stats.t[:].rearrange("p c s -> p (c s)")), reads=[stats], writes=[mv])
        DVE.op(lambda: V.tensor_scalar(out=rsd.t[:], in0=mv.t[:, 1:2], scalar1=LN_EPS, scalar2=None, op0=ALU.add), reads=[mv], writes=[rsd])
        act(rsd.t[:], rsd.t[:], AF.Sqrt, [rsd], [rsd])
        DVE.op(lambda: V.reciprocal(out=rsd.t[:], in_=rsd.t[:]), reads=[rsd], writes=[rsd])
        DVE.op(lambda: V.tensor_scalar(out=out.t[:], in0=r.t[:], scalar1=mv.t[:, 0:1], scalar2=rsd.t[:, 0:1], op0=ALU.subtract, op1=ALU.mult),
               reads=[r, mv, rsd], writes=[out])
        tt(out.t[:], out.t[:], g.t[:], ALU.mult, [out, g], [out])
        tt(out.t[:], out.t[:], b.t[:], ALU.add, [out, b], [out])

    ymT = [sb(f"t_ymT{i}", [128, 16, 128]) for i in range(2)]
    xres = [sb(f"t_xres{i}", [128, 1024]) for i in range(2)]
    tl = [dsem(f"t_tl{i}") for i in range(2)]
    wpc = [sb(f"t_wpc{i}", [128, 4096]) for i in range(2)]
    wl = [dsem(f"t_wl{i}") for i in range(2)]
    h1 = sb("t_h1", [128, 1024])
    h1T = sb("t_h1T", [128, 8, 128])
    qTb = sb("t_qTb", [128, 16, 128])
    Ssb = sb("t_Ssb", [128, 16, 128])
    S2b = sb("t_S2b", [128, 16, 128])
    vals = sb("t_vals", [128, 16, 16])
    idx = sb("t_idx", [128, 16, 16], U32)
    idxf = sb("t_idxf", [128, 16, 16])
    cand = Ssb
    cand2 = S2b
    sc = sb("t_sc", [128, 8, 16])
    ci = sb("t_ci", [128, 8, 16], U32)
    au = sb("t_au", [128, 8, 16], U32)
    bu = sb("t_bu", [128, 8, 16], U32)
    af = sb("t_af", [128, 8, 16])
    bf = sb("t_bf", [128, 8, 16])
    eq = sb("t_eq", [128, 8, 16, 16])
    isel = sb("t_isel", [128, 2, 8, 16])
    ef = sb("t_ef", [128, 128])
    eu = sb("t_eu", [128, 128], U32)
    gt = sb("t_gt", [128, 8, 16])
    gs = sb("t_gs", [128, 8])
    NB = 4
    ub = [sb(f"t_ub{i}", [128, 1024]) for i in range(NB)]
    ug = [dsem(f"t_ug{i}") for i in range(NB)]
    junk = sb("t_junk", [128, 1024])
    hid = sb("t_hid", [128, 128])
    pacc = sb("t_pacc", [128, 1024])
    yst = dsem("t_yst")
    wcount = [0]
    gcount = [0]

    def wload(src_ap, shape3):
        i = wcount[0] % 2
        wcount[0] += 1
        w = wpc[i]
        view = w.t[:].rearrange("p (a b) -> p a b", a=shape3[0])
        SP.dma(wl[i], view, src_ap, writes=[w])
        return w, view

    def gather(slot, table):
        u = ub[gcount[0] % NB]
        d = ug[gcount[0] % NB]
        gcount[0] += 1
        POOL.deps([eu], [u])
        inst = G.indirect_dma_start(out=u.t[:], out_offset=None, in_=T[table][:, :],
                                    in_offset=bass.IndirectOffsetOnAxis(ap=eu.t[:, slot:slot + 1], axis=0))
        d.n += 16
        inst.then_inc(d.sem, 16)
        POOL._mark((d.sem, d), [eu], [u])
        return u

    def top16(src, tmp, vout, iout, src_b, tmp_b, v_b, i_b):
        DVE.op(lambda: V.max(out=vout[:, 0:8], in_=src), reads=[src_b], writes=[v_b])
        DVE.op(lambda: V.max_index(out=iout[:, 0:8], in_max=vout[:, 0:8], in_values=src), reads=[src_b, v_b], writes=[i_b])
        DVE.op(lambda: V.match_replace(out=tmp, in_to_replace=vout[:, 0:8], in_values=src, imm_value=-1e30), reads=[src_b, v_b], writes=[tmp_b])
        DVE.op(lambda: V.max(out=vout[:, 8:16], in_=tmp), reads=[tmp_b], writes=[v_b])
        DVE.op(lambda: V.max_index(out=iout[:, 8:16], in_max=vout[:, 8:16], in_values=tmp), reads=[tmp_b, v_b], writes=[i_b])

    if ag is not None:
        ridx = sb("t_ridx", [128, ntiles * 4], U32)
        aidx = sb("t_aidx", [128, 4], U32)
        SP.dma(ld, ridx.t[:], ag["rowidx"][:, :], writes=[ridx])
        SP.dma(ld, aidx.t[:], ag["attidx"][:, :], writes=[aidx])
        attseg = sb("t_attseg", [128, 4, 2048])
        agd = dsem("t_agd")
        for r in range(4):
            POOL.deps([aidx, ag["dep"]], [attseg])
            inst = G.indirect_dma_start(out=attseg.t[:, r, :], out_offset=None, in_=ag["a"],
                                        in_offset=bass.IndirectOffsetOnAxis(ap=aidx.t[:, r:r + 1], axis=0))
            agd.n += 16
            inst.then_inc(agd.sem, 16)
            POOL._mark((agd.sem, agd), [aidx, ag["dep"]], [attseg])
        ytok = [sb(f"t_ytok{i}", [128, 1536]) for i in range(2)]
        ygd = [dsem(f"t_ygd{i}") for i in range(2)]
    for t in range(ntiles):
        i = t % 2
        if ag is None:
            SP.dma(tl[i], ymT[i].t[:], T["t_ymT"][:, :, t * 128:(t + 1) * 128], writes=[ymT[i]])
        else:
            yk = ytok[i]
            for r in range(4):
                POOL.deps([ridx, ag["dep"]], [yk])
                inst = G.indirect_dma_start(out=yk.t[:, r * 384:(r + 1) * 384], out_offset=None, in_=ag["y"],
                                            in_offset=bass.IndirectOffsetOnAxis(ap=ridx.t[:, t * 4 + r:t * 4 + r + 1], axis=0))
                ygd[i].n += 16
                inst.then_inc(ygd[i].sem, 16)
                POOL._mark((ygd[i].sem, ygd[i]), [ridx, ag["dep"]], [yk])
            for kc in range(12):
                qd = PB[7].t[:, (kc % 4) * 128:(kc % 4 + 1) * 128]
                PE.op(lambda: nc.tensor.transpose(out=qd, in_=yk.t[:, kc * 128:(kc + 1) * 128], identity=ident.t[:]), reads=[yk, ident], writes=[PB[7]])
                cp(ymT[i].t[:, kc, :], qd, [PB[7]], [ymT[i]], eng=("act" if kc % 2 else "dve"))
        SP.dma(tl[i], xres[i].t[:], T["t_x"][t * 128:(t + 1) * 128, :], writes=[xres[i]])
        for pc in range(4):
            w, wv = wload(T["t_wout"][:, 4 * pc:4 * pc + 4, :], (4, 1024))
            for n in range(2):
                for kk in range(4):
                    kc = 4 * pc + kk
                    if ag is not None and kc >= 12:
                        lhs, lrd = attseg.t[:, kc - 12, t * 128:(t + 1) * 128], attseg
                    else:
                        lhs, lrd = ymT[i].t[:, kc, :], ymT[i]
                    mm(PB[n].t[:, :], lhs, wv[:, kk, n * 512:(n + 1) * 512], kc == 0, kc == 15, [lrd, w], [PB[n]])
        r1 = junk
        for n in range(2):
            DVE.op(lambda n=n: V.scalar_tensor_tensor(out=r1.t[:, n * 512:(n + 1) * 512], in0=xres[i].t[:, n * 512:(n + 1) * 512], scalar=ALPHA,
                                                      in1=PB[n].t[:, :], op0=ALU.mult, op1=ALU.add), reads=[xres[i], PB[n]], writes=[r1])
        layer_norm(r1, lnp["t_ln1g"], lnp["t_ln1b"], h1)
        for kc in range(8):
            qd = PB[7].t[:, (kc % 4) * 128:(kc % 4 + 1) * 128]
            PE.op(lambda: nc.tensor.transpose(out=qd, in_=h1.t[:, kc * 128:(kc + 1) * 128], identity=ident.t[:]), reads=[h1, ident], writes=[PB[7]])
            cp(h1T.t[:, kc, :], qd, [PB[7]], [h1T], eng=("act" if kc % 2 else "dve"))
        for pc in range(4):
            w, wv = wload(T["t_wq"][:, :, 512 * pc:512 * (pc + 1)], (8, 512))
            P = PB[2 + pc]
            for jj in range(4):
                for kc in range(8):
                    mm(P.t[:, jj * 128:(jj + 1) * 128], wv[:, kc, jj * 128:(jj + 1) * 128], h1T.t[:, kc, :], kc == 0, kc == 7, [w, h1T], [P])
            cp(qTb.t[:, 4 * pc:4 * pc + 4, :], P.t[:, :].rearrange("p (j t) -> p j t", j=4), [P], [qTb], eng=("act" if pc % 2 else "dve"))
        sbanks = [PB[0], PB[1], PB[6], PB[7]]
        for j in range(16):
            P = sbanks[j // 4]
            mm(P.t[:, (j % 4) * 128:(j % 4 + 1) * 128], qTb.t[:, j, :], kTb.t[:, j % 2, :], True, True, [qTb, kTb], [P])
            if j % 4 == 3:
                cp(Ssb.t[:, j - 3:j + 1, :], P.t[:, :].rearrange("p (j t) -> p j t", j=4), [P], [Ssb], eng=("act" if (j // 4) % 2 else "dve"))
        for j in range(16):
            top16(Ssb.t[:, j, :], S2b.t[:, j, :], vals.t[:, j, :], idx.t[:, j, :], Ssb, S2b, vals, idx)
        cp(idxf.t[:], idx.t[:], [idx], [idxf], eng="dve")
        v4v = vals.t[:].rearrange("p (h s) a -> p h s a", s=2)
        c3 = cand.t[:].rearrange("p (h x) t -> p h (x t)", h=8)
        c23 = cand2.t[:].rearrange("p (h x) t -> p h (x t)", h=8)
        tt(c3.rearrange("p h (a b) -> p h a b", a=16), v4v[:, :, 0, :].unsqueeze(3).to_broadcast([128, 8, 16, 16]),
           v4v[:, :, 1, :].unsqueeze(2).to_broadcast([128, 8, 16, 16]), ALU.add, [vals], [cand])
        for h in range(8):
            top16(c3[:, h, :], c23[:, h, :], sc.t[:, h, :], ci.t[:, h, :], cand, cand2, sc, ci)
        DVE.op(lambda: V.tensor_scalar(out=au.t[:], in0=ci.t[:], scalar1=4, scalar2=None, op0=ALU.logical_shift_right), reads=[ci], writes=[au])
        DVE.op(lambda: V.tensor_scalar(out=bu.t[:], in0=ci.t[:], scalar1=15, scalar2=None, op0=ALU.bitwise_and), reads=[ci], writes=[bu])
        cp(af.t[:], au.t[:], [au], [af], eng="dve")
        cp(bf.t[:], bu.t[:], [bu], [bf], eng="dve")
        i4 = idxf.t[:].rearrange("p (h s) a -> p h s a", s=2)
        io4 = iota16.t[:].unsqueeze(1).unsqueeze(1).to_broadcast([128, 8, 16, 16])
        for side, xf in ((0, af), (1, bf)):
            tt(eq.t[:], xf.t[:].unsqueeze(3).to_broadcast([128, 8, 16, 16]), io4, ALU.is_equal, [xf, iota16], [eq])
            tt(eq.t[:], eq.t[:], i4[:, :, side, :].unsqueeze(2).to_broadcast([128, 8, 16, 16]), ALU.mult, [eq, idxf], [eq])
            DVE.op(lambda side=side: V.tensor_reduce(out=isel.t[:, side, :, :].rearrange("p h k -> p (h k)"),
                                                     in_=eq.t[:].rearrange("p h k a -> p (h k) a"), axis=AX.X, op=ALU.add),
                   reads=[eq], writes=[isel])
        DVE.op(lambda: V.scalar_tensor_tensor(out=ef.t[:], in0=isel.t[:, 0, :, :].rearrange("p h k -> p (h k)"), scalar=128.0,
                                              in1=isel.t[:, 1, :, :].rearrange("p h k -> p (h k)"), op0=ALU.mult, op1=ALU.add),
               reads=[isel], writes=[ef])
        cp(eu.t[:], ef.t[:], [ef], [eu], eng="dve")
        tt(gt.t[:], sc.t[:], sc.t[:, :, 0:1].to_broadcast([128, 8, 16]), ALU.subtract, [sc], [gt])
        act(gt.t[:], gt.t[:], AF.Exp, [gt], [gt])
        red(gs.t[:], gt.t[:], [gt], [gs])
        DVE.op(lambda: V.reciprocal(out=gs.t[:], in_=gs.t[:]), reads=[gs], writes=[gs])
        tt(gt.t[:], gt.t[:], gs.t[:].unsqueeze(2).to_broadcast([128, 8, 16]), ALU.mult, [gt, gs], [gt])
        for slot in range(128):
            u = gather(slot, "t_pu")
            DVE.op(lambda: V.scalar_tensor_tensor(out=junk.t[:], in0=u.t[:], scalar=1.0, in1=h1.t[:], op0=ALU.mult, op1=ALU.mult,
                                                  accum_out=hid.t[:, slot:slot + 1]), reads=[u, h1], writes=[junk, hid])
        act(hid.t[:], hid.t[:], AF.Gelu, [hid], [hid])
        tt(hid.t[:], hid.t[:], gt.t[:].rearrange("p h k -> p (h k)"), ALU.mult, [hid, gt], [hid])
        for slot in range(128):
            u = gather(slot, "t_pv")
            if slot == 0:
                DVE.op(lambda: V.tensor_scalar(out=pacc.t[:], in0=u.t[:], scalar1=hid.t[:, 0:1], scalar2=None, op0=ALU.mult),
                       reads=[u, hid], writes=[pacc])
            else:
                DVE.op(lambda: V.scalar_tensor_tensor(out=pacc.t[:], in0=u.t[:], scalar=hid.t[:, slot:slot + 1], in1=pacc.t[:],
                                                      op0=ALU.mult, op1=ALU.add), reads=[u, hid, pacc], writes=[pacc])
        DVE.op(lambda: V.scalar_tensor_tensor(out=pacc.t[:], in0=h1.t[:], scalar=ALPHA, in1=pacc.t[:], op0=ALU.mult, op1=ALU.add),
               reads=[h1, pacc], writes=[pacc])
        layer_norm(pacc, lnp["t_ln2g"], lnp["t_ln2b"], junk)
        out_toks.append(SP.dma(yst, y_out[t * 128:(t + 1) * 128, :], junk.t[:], reads=[junk]))


def build_program():
    nc = bass.Bass("TRN2", target_bir_lowering=False)

    def din(name, shape):
        return nc.dram_tensor(name, list(shape), F32, kind="ExternalInput").ap()

    def dout(name, shape):
        return nc.dram_tensor(name, list(shape), F32, kind="ExternalOutput").ap()

    xT = din("xT", [128, 8, SEQ])
    wc = din("wc", [128, 8, NW])
    cw = din("cw", [4, NCONV])
    cb = din("cb", [1, NCONV])
    dtb = din("dtb", [1, 6])
    alog = din("alog", [1, 6])
    cosP = din("cosP", [128, NT, 32])
    sinP = din("sinP", [128, NT, 32])
    cosS = din("cosS", [128, 4, 32])
    sinS = din("sinS", [128, 4, 32])
    tri = din("tri", [128, 128])
    xsT = din("xsT", [128, 8, 512])
    scv = din("scv", [128, 3, NCONV])
    ssm0 = din("ssm0", [64, 6, 64, 128])
    ck = din("ck", [NSEQ_COPY, 2048 * 512])
    cv = din("cv", [NSEQ_COPY, 2048 * 512])

    T2 = {n_: din(n_, shp_) for n_, shp_ in S2_INPUTS.items()}
    TP = {n_: (xT if n_ == "p_xT" else din(n_, shp_)) for n_, shp_ in p2_inputs(SEQ).items()}
    t_x = din("t_x", [TAIL_TILES * 128, 1024])
    rowidx = nc.dram_tensor("rowidx", [128, TAIL_TILES * 4], U32, kind="ExternalInput").ap()
    attidx = nc.dram_tensor("attidx", [128, 4], U32, kind="ExternalInput").ap()
    t_y = dout("t_y", [TAIL_TILES * 128, 1024])
    ag_y_in = nc.dram_tensor("ag_y_in", [SEQ, 384], F32)
    ag_y_out = nc.dram_tensor("ag_y_out", [4 * SEQ, 384], F32)
    ag_a_in = nc.dram_tensor("ag_a_in", [4, 128, 2048], F32)
    ag_a_out = nc.dram_tensor("ag_a_out", [4 * 512, 2048], F32)
    ys = dout("ys", [64, 1024])
    pk = dout("pk", [2048, 128])
    pv = dout("pv", [2048, 128])
    pcv = dout("pcv", [3, NCONV])
    pss = dout("pss", [6 * 64, 128])
    skc = dout("skc", [NSEQ_COPY, ROWS_COPY * 512])
    svc = dout("svc", [NSEQ_COPY, ROWS_COPY * 512])
    skn = dout("skn", [64, 4, 128])
    svn = dout("svn", [64, 4, 128])
    scvo = dout("scvo", [64, 3, NCONV])
    sss = dout("sss", [64, 6, 64, 128])

    es = ExitStack()
    with es:
        PE = Eng(nc, es, nc.tensor, "pe", same_sync=False)
        DVE = Eng(nc, es, nc.vector, "dve")
        ACT = Eng(nc, es, nc.scalar, "act")
        POOL = Eng(nc, es, nc.gpsimd, "pool")
        SP = Eng(nc, es, nc.sync, "sp")
        out_toks = []

        es_p = ExitStack()
        all_dsems = []

        es_main = ExitStack()

        def sb(name, shape, dt=F32):
            return Buf(es_main.enter_context(nc.sbuf_tensor(name, list(shape), dt)))

        def sb_top(name, shape, dt=F32):
            return Buf(es.enter_context(nc.sbuf_tensor(name, list(shape), dt)))

        def scope():
            sub = ExitStack()

            def sbs(name, shape, dt=F32):
                return Buf(sub.enter_context(nc.sbuf_tensor(name, list(shape), dt)))

            def close():
                sub.close()
                barrier()
            return sbs, close

        def sbp(name, shape, dt=F32):
            return Buf(es_p.enter_context(nc.sbuf_tensor(name, list(shape), dt)))

        def ps(name, shape, dt=F32):
            return Buf(es.enter_context(nc.psum_tensor(name, list(shape), dt)))

        def dsem(name):
            d = DSem(nc, es, name)
            all_dsems.append(d)
            return d

        def barrier():
            toks = [(e_.sem, e_.n) for e_ in (PE, DVE, ACT, POOL) if e_.n > 0]
            toks += [(d.sem, d.n) for d in all_dsems if d.n > 0 and d is not cp_sem]
            for e_ in (PE, DVE, ACT, POOL, SP):
                for t in toks:
                    e_.wait_tok(t)

        cp_sem = DSem(nc, es, "cp")
        for s in range(NSEQ_COPY):
            for src, dst in ((ck, skc), (cv, svc)):
                i_ap = src[s:s + 1, 4 * 512:2048 * 512].rearrange("o (a f) -> (o a) f", a=16)
                o_ap = dst[s:s + 1, :].rearrange("o (a f) -> (o a) f", a=16)
                out_toks.append(ACT.dma(cp_sem, o_ap, i_ap))

        wb = sb("wb", [128, 8, NW], BF16)
        wrep = sb("wrep", [128, 4, NCONV])
        cbrep = sb("cbrep", [128, NCONV])
        dtbrep = sb("dtbrep", [128, 6])
        negA = sb("negA", [128, 6])
        cosSt = sb("cosSt", [128, 4, 32])
        sinSt = sb("sinSt", [128, 4, 32])
        rtc = sb("rtc", [128, 128])
        rts = sb("rts", [128, 128])
        wf = sbp("wf", [128, 4, 8, 512], BF16)
        trit = sbp("trit", [128, 128])
        onest = sbp("onest", [128, 128])
        ones64 = sbp("ones64", [128, NT])
        cosPt = sbp("cosPt", [128, NT - KV0_TILE, 32])
        sinPt = sbp("sinPt", [128, NT - KV0_TILE, 32])
        wfac = sbp("wfac", [128, NT * 6])
        ld = dsem("ld_const")

        SP.dma(ld, wrep.t[:], cw.rearrange("t e -> (t e)").partition_broadcast(128)
               .rearrange("p (t e) -> p t e", t=4), writes=[wrep])
        SP.dma(ld, cbrep.t[:], cb.rearrange("o e -> (o e)").partition_broadcast(128), writes=[cbrep])
        SP.dma(ld, dtbrep.t[:], dtb.rearrange("o e -> (o e)").partition_broadcast(128), writes=[dtbrep])
        SP.dma(ld, negA.t[:], alog.rearrange("o e -> (o e)").partition_broadcast(128), writes=[negA])
        SP.dma(ld, trit.t[:], tri[:, :], writes=[trit])
        SP.dma(ld, cosPt.t[:], cosP[:, KV0_TILE:NT, :], writes=[cosPt])
        SP.dma(ld, sinPt.t[:], sinP[:, KV0_TILE:NT, :], writes=[sinPt])
        SP.dma(ld, cosSt.t[:], cosS[:, :, :], writes=[cosSt])
        SP.dma(ld, sinSt.t[:], sinS[:, :, :], writes=[sinSt])

        DVE.op(lambda: nc.vector.memset(onest.t[:], 1.0), writes=[onest])
        DVE.op(lambda: nc.vector.memset(ones64.t[:], 1.0), writes=[ones64])
        ACT.op(lambda: nc.scalar.activation(out=negA.t[:], in_=negA.t[:], func=AF.Exp),
               reads=[negA], writes=[negA])
        DVE.op(lambda: nc.vector.tensor_scalar(out=negA.t[:], in0=negA.t[:], scalar1=-1.0, scalar2=None,
                                               op0=ALU.mult), reads=[negA], writes=[negA])

        wst = [sbp(f"wst{i}", [128, 8, 256]) for i in range(2)]
        wld = [dsem(f"wld{i}") for i in range(2)]
        pieces = [(o, min(256, NW - o)) for o in range(0, NW, 256)]
        for i, (o, n) in enumerate(pieces):
            st = wst[i % 2]
            SP.dma(wld[i % 2], st.t[:, :, 0:n], wc[:, :, o:o + n], writes=[st])
            DVE.op(lambda st=st, o=o, n=n: nc.vector.tensor_copy(out=wb.t[:, :, o:o + n], in_=st.t[:, :, 0:n]),
                   reads=[st], writes=[wb])
            if o < 512:
                for tap in range(4):
                    eng = POOL if tap % 2 else DVE
                    e = nc.gpsimd if tap % 2 else nc.vector
                    eng.op(lambda e=e, st=st, o=o, n=n, tap=tap: e.tensor_tensor(
                        out=wf.t[:, tap, :, o:o + n], in0=st.t[:, :, 0:n],
                        in1=wrep.t[:, tap, o:o + n].unsqueeze(1).to_broadcast([128, 8, n]), op=ALU.mult),
                        reads=[st, wrep], writes=[wf])

        xst = [sbp(f"xst{i}", [128, 8, HALO + BLK]) for i in range(2)]
        xtb = [sbp(f"xtb{i}", [128, 8, HALO + BLK], BF16) for i in range(2)]
        xld = [dsem(f"xld{i}") for i in range(2)]
        xcount = [0]

        def load_block(blk):
            i = xcount[0] % 2
            xcount[0] += 1
            st, tb = xst[i], xtb[i]
            if blk == 0:
                DVE.op(lambda: nc.vector.memset(st.t[:, :, 0:HALO], 0.0), writes=[st])
                SP.dma(xld[i], st.t[:, :, HALO:HALO + BLK], xT[:, :, 0:BLK], writes=[st])
            else:
                SP.dma(xld[i], st.t[:, :, :], xT[:, :, blk * BLK - HALO:(blk + 1) * BLK], writes=[st])
            eng, e = (DVE, nc.vector) if blk % 2 == 0 else (POOL, nc.gpsimd)
            eng.op(lambda: e.tensor_copy(out=tb.t[:], in_=st.t[:]), reads=[st], writes=[tb])
            return tb

        psA = [ps(f"psA{i}", [128, 512]) for i in range(2)]
        psKV = ps("psKV", [128, 512])
        psDT = ps("psDT", [128, 512])
        psS = [ps(f"psS{i}", [128, 512]) for i in range(3)]

        for blk in range(NBLK):
            tb = load_block(blk)
            for sub in range(BLK // 128):
                j = blk * (BLK // 128) + sub
                c0 = HALO + sub * 128
                for c in range(8):
                    PE.op(lambda c=c: nc.tensor.matmul(psDT.t[:, j * 6:(j + 1) * 6], lhsT=tb.t[:, c, c0:c0 + 128],
                                                       rhs=wb.t[:, c, ODT:ODT + 6], start=(c == 0), stop=(c == 7)),
                          reads=[tb, wb], writes=[psDT])
        dts = sbp("dts", [128, NT * 6])
        dtA = sbp("dtA", [128, NT * 6])
        tot_hj = sbp("tot_hj", [128, 6, NT])
        pre_hj = sbp("pre_hj", [128, 6, NT])
        Rt = sbp("Rt", [128, NT * 6])
        v3 = lambda b: b.t[:].rearrange("p (j h) -> p j h", h=6)
        DVE.op(lambda: nc.vector.tensor_tensor(out=v3(dts), in0=psDT.t[:, 0:NT * 6].rearrange("p (j h) -> p j h", h=6),
                                               in1=dtbrep.t[:].unsqueeze(1).to_broadcast([128, NT, 6]), op=ALU.add),
               reads=[psDT, dtbrep], writes=[dts])
        ACT.op(lambda: nc.scalar.activation(out=dts.t[:], in_=dts.t[:], func=AF.Exp), reads=[dts], writes=[dts])
        ACT.op(lambda: nc.scalar.activation(out=dts.t[:], in_=dts.t[:], func=AF.Ln, bias=1.0), reads=[dts], writes=[dts])
        DVE.op(lambda: nc.vector.tensor_tensor(out=v3(dtA), in0=v3(dts),
                                               in1=negA.t[:].unsqueeze(1).to_broadcast([128, NT, 6]), op=ALU.mult),
               reads=[dts, negA], writes=[dtA])
        PE.op(lambda: nc.tensor.matmul(psA[0].t[:, 0:NT * 6], lhsT=trit.t[:], rhs=dtA.t[:], start=True, stop=True),
              reads=[trit, dtA], writes=[psA[0]])
        PE.op(lambda: nc.tensor.matmul(psA[1].t[:, 0:NT * 6], lhsT=onest.t[:], rhs=dtA.t[:], start=True, stop=True),
              reads=[onest, dtA], writes=[psA[1]])
        DVE.op(lambda: nc.vector.tensor_copy(out=tot_hj.t[:], in_=psA[1].t[:, 0:NT * 6].rearrange("p (j h) -> p h j", h=6)),
               reads=[psA[1]], writes=[tot_hj])
        for h in range(6):
            DVE.op(lambda h=h: nc.vector.tensor_tensor_scan(out=pre_hj.t[:, h, :], data0=ones64.t[:], data1=tot_hj.t[:, h, :],
                                                            initial=0.0, op0=ALU.mult, op1=ALU.add),
                   reads=[ones64, tot_hj], writes=[pre_hj])
        DVE.op(lambda: nc.vector.tensor_tensor(out=v3(Rt), in0=psA[0].t[:, 0:NT * 6].rearrange("p (j h) -> p j h", h=6),
                                               in1=pre_hj.t[:].rearrange("p h j -> p j h"), op=ALU.subtract),
               reads=[psA[0], pre_hj], writes=[Rt])
        DVE.op(lambda: nc.vector.tensor_tensor(out=v3(Rt), in0=v3(Rt),
                                               in1=pre_hj.t[:, :, NT - 1:NT].rearrange("p h o -> p o h").to_broadcast([128, NT, 6]),
                                               op=ALU.add),
               reads=[Rt, pre_hj], writes=[Rt])
        ACT.op(lambda: nc.scalar.activation(out=Rt.t[:], in_=Rt.t[:], func=AF.Exp), reads=[Rt], writes=[Rt])
        DVE.op(lambda: nc.vector.tensor_tensor(out=wfac.t[:], in0=Rt.t[:], in1=dts.t[:], op=ALU.mult),
               reads=[Rt, dts], writes=[wfac])

        pre_sb = [sbp(f"pre{i}", [128, 512]) for i in range(2)]
        xs_sb = [sbp(f"xs{i}", [128, 384]) for i in range(2)]
        B_sb = [sbp(f"Bb{i}", [128, 128], BF16) for i in range(2)]
        xw_sb = [sbp(f"xw{i}", [128, 384], BF16) for i in range(2)]
        kvo = [sbp(f"kvo{i}", [128, 256]) for i in range(2)]
        kvst = [dsem(f"kvst{i}") for i in range(2)]

        def rope(eng_pair, src_ap, dst_ap, cos_ap, sin_ap, reads, writes):
            ENG, e = eng_pair
            s4 = src_ap.rearrange("p (h two f) -> p h two f", h=2, two=2)
            d4 = dst_ap.rearrange("p (h two f) -> p h two f", h=2, two=2)
            c4 = rtc.t[:].rearrange("p (h two f) -> p h two f", h=2, two=2)
            q4 = rts.t[:].rearrange("p (h two f) -> p h two f", h=2, two=2)
            cb4 = cos_ap.unsqueeze(1).unsqueeze(1).to_broadcast([128, 2, 2, 32])
            sb3 = sin_ap.unsqueeze(1).to_broadcast([128, 2, 32])
            ENG.op(lambda: e.tensor_tensor(out=c4, in0=s4, in1=cb4, op=ALU.mult), reads=reads, writes=[rtc])
            ENG.op(lambda: e.tensor_tensor(out=q4[:, :, 0, :], in0=s4[:, :, 1, :], in1=sb3, op=ALU.mult),
                   reads=reads, writes=[rts])
            ENG.op(lambda: e.tensor_tensor(out=q4[:, :, 1, :], in0=s4[:, :, 0, :], in1=sb3, op=ALU.mult),
                   reads=reads, writes=[rts])
            ENG.op(lambda: e.tensor_tensor(out=d4[:, :, 0, :], in0=c4[:, :, 0, :], in1=q4[:, :, 0, :], op=ALU.subtract),
                   reads=[rtc, rts], writes=writes)
            ENG.op(lambda: e.tensor_tensor(out=d4[:, :, 1, :], in0=c4[:, :, 1, :], in1=q4[:, :, 1, :], op=ALU.add),
                   reads=[rtc, rts], writes=writes)

        def state_mm(j):
            i = j % 2
            for hp in range(3):
                PE.op(lambda hp=hp: nc.tensor.matmul(psS[hp].t[:, 0:128], lhsT=xw_sb[i].t[:, hp * 128:(hp + 1) * 128],
                                                     rhs=B_sb[i].t[:], start=(j == 0), stop=(j == NT - 1)),
                      reads=[xw_sb[i], B_sb[i]], writes=[psS[hp]])

        last_tb = None
        for blk in range(NBLK):
            tb = load_block(blk)
            last_tb = tb
            for sub in range(BLK // 128):
                j = blk * (BLK // 128) + sub
                i = j % 2
                c0 = HALO + sub * 128
                A = psA[i]
                n = 0
                for tap in range(4):
                    for c in range(8):
                        s0 = c0 - 3 + tap
                        PE.op(lambda tap=tap, c=c, s0=s0, n=n: nc.tensor.matmul(
                            A.t[:, :], lhsT=tb.t[:, c, s0:s0 + 128], rhs=wf.t[:, tap, c, :],
                            start=(n == 0), stop=(n == 31)), reads=[tb, wf], writes=[A])
                        n += 1
                if j >= KV0_TILE:
                    kvp = psKV.t[:, i * 256:(i + 1) * 256]
                    for c in range(8):
                        PE.op(lambda c=c: nc.tensor.matmul(kvp, lhsT=tb.t[:, c, c0:c0 + 128], rhs=wb.t[:, c, OK_:OK_ + 256],
                                                           start=(c == 0), stop=(c == 7)), reads=[tb, wb], writes=[psKV])
                if j >= 1:
                    state_mm(j - 1)
                DVE.op(lambda: nc.vector.tensor_tensor(out=pre_sb[i].t[:], in0=A.t[:, :], in1=cbrep.t[:, 0:512], op=ALU.add),
                       reads=[A, cbrep], writes=[pre_sb[i]])
                ACT.op(lambda: nc.scalar.activation(out=xs_sb[i].t[:], in_=pre_sb[i].t[:, 0:384], func=AF.Silu),
                       reads=[pre_sb[i]], writes=[xs_sb[i]])
                ACT.op(lambda: nc.scalar.activation(out=B_sb[i].t[:], in_=pre_sb[i].t[:, 384:512], func=AF.Silu),
                       reads=[pre_sb[i]], writes=[B_sb[i]])
                DVE.op(lambda: nc.vector.tensor_tensor(
                    out=xw_sb[i].t[:].rearrange("p (h f) -> p h f", h=6),
                    in0=xs_sb[i].t[:].rearrange("p (h f) -> p h f", h=6),
                    in1=wfac.t[:, j * 6:(j + 1) * 6].unsqueeze(2).to_broadcast([128, 6, 64]), op=ALU.mult),
                    reads=[xs_sb[i], wfac], writes=[xw_sb[i]])
                if j >= KV0_TILE:
                    jj = j - KV0_TILE
                    rope((POOL, nc.gpsimd) if False else (DVE, nc.vector), psKV.t[:, i * 256:i * 256 + 128],
                         kvo[i].t[:, 0:128], cosPt.t[:, jj, :], sinPt.t[:, jj, :],
                         reads=[psKV, cosPt, sinPt], writes=[kvo[i]])
                    ACT.op(lambda: nc.scalar.copy(out=kvo[i].t[:, 128:256], in_=psKV.t[:, i * 256 + 128:(i + 1) * 256]),
                           reads=[psKV], writes=[kvo[i]])
                    out_toks.append(POOL.dma(kvst[i], pk[jj * 128:(jj + 1) * 128, :], kvo[i].t[:, 0:128], reads=[kvo[i]]))
                    out_toks.append(POOL.dma(kvst[i], pv[jj * 128:(jj + 1) * 128, :], kvo[i].t[:, 128:256], reads=[kvo[i]]))
        state_mm(NT - 1)

        misc = dsem("misc")
        pcv_sb = sbp("pcv_sb", [3, NCONV])
        lc = HALO + BLK - 3
        for (o, n, dstp) in ((0, 512, psA[0]), (512, 128, psA[1])):
            for c in range(8):
                PE.op(lambda c=c, o=o, n=n, dstp=dstp: nc.tensor.matmul(
                    dstp.t[0:3, 0:n], lhsT=last_tb.t[:, c, lc:lc + 3], rhs=wb.t[:, c, o:o + n],
                    start=(c == 0), stop=(c == 7)), reads=[last_tb, wb], writes=[dstp])
            ACT.op(lambda o=o, n=n, dstp=dstp: nc.scalar.copy(out=pcv_sb.t[:, o:o + n], in_=dstp.t[0:3, 0:n]),
                   reads=[dstp], writes=[pcv_sb])
        out_toks.append(POOL.dma(misc, pcv[:, :], pcv_sb.t[:], reads=[pcv_sb]))

        pss_sb = sbp("pss_sb", [128, 3, 128])
        for hp in range(3):
            ACT.op(lambda hp=hp: nc.scalar.copy(out=pss_sb.t[:, hp, :], in_=psS[hp].t[:, 0:128]),
                   reads=[psS[hp]], writes=[pss_sb])
        out_toks.append(POOL.dma(misc, pss.rearrange("(hp q) n -> q hp n", hp=3), pss_sb.t[:], reads=[pss_sb]))

        es_p.close()
        barrier()
        S_sample(nc, es, locals())
        es_main.close()
        barrier()
        psX = ps("psX", [128, 512])
        PBK = [psA[0], psA[1], psKV, psDT, psS[0], psS[1], psS[2], psX]
        sbp2, closeP2 = scope()
        n_before = len(out_toks)
        KP = dict(scope=scope, PE=PE, DVE=DVE, ACT=ACT, POOL=POOL, SP=SP, sb=sbp2, dsem=dsem, out_toks=out_toks, PB=PBK)
        emit_p2(nc, KP, TP, ag_y_in.ap(), ag_a_in.ap(), seq=SEQ, att3d=True)
        closeP2()
        p2_toks = out_toks[n_before:]
        del out_toks[n_before:]
        for tk in p2_toks:
            POOL.wait_tok(tk)
        cc_sem = es.enter_context(nc.semaphore("cc_sem"))
        rg = [[0, 1, 2, 3], [4, 5, 6, 7]]
        n_cc = 0
        ayi, ayo = ag_y_in.ap(), ag_y_out.ap()
        for i_ in range(16):
            nc.gpsimd.collective_compute("AllGather", ALU.bypass, replica_groups=rg, ins=[ayi[512 * i_:512 * (i_ + 1), :].opt()],
                                         outs=[ayo[2048 * i_:2048 * (i_ + 1), :].opt()]).then_inc(cc_sem)
            n_cc += 1
        aai, aao = ag_a_in.ap().rearrange("s c t -> (s c) t"), ag_a_out.ap()
        for i_ in range(8):
            nc.gpsimd.collective_compute("AllGather", ALU.bypass, replica_groups=rg, ins=[aai[64 * i_:64 * (i_ + 1), :].opt()],
                                         outs=[aao[256 * i_:256 * (i_ + 1), :].opt()]).then_inc(cc_sem)
            n_cc += 1
        agbuf = Buf(None)
        agbuf.w = (cc_sem, n_cc)
        sbs2, closeS2 = scope()
        K2 = dict(scope=scope, PE=PE, DVE=DVE, ACT=ACT, POOL=POOL, SP=SP, sb=sbs2, dsem=dsem, out_toks=out_toks, PB=PBK)
        R2 = emit_s2(nc, es, K2, T2, ck, cv, ys)
        emit_s2b(nc, es, K2, T2, ck, cv, ys, R2)
        closeS2()
        TT = {"t_x": t_x, "t_wout": T2["s2_wout"], "t_ln1g": T2["s2_ln1g"], "t_ln1b": T2["s2_ln1b"], "t_ln2g": T2["s2_ln2g"],
              "t_ln2b": T2["s2_ln2b"], "t_wq": T2["s2_wq"], "t_k1T": T2["s2_k1T"], "t_k2T": T2["s2_k2T"], "t_pu": T2["s2_pu"],
              "t_pv": T2["s2_pv"], "t_ident": T2["c_ident"], "t_iota16": T2["c_iota16"]}
        KT_ = dict(scope=scope, PE=PE, DVE=DVE, ACT=ACT, POOL=POOL, SP=SP, sb=sb_top, dsem=dsem, out_toks=out_toks, PB=PBK)
        emit_tail(nc, KT_, TT, t_y, TAIL_TILES,
                  ag=dict(y=ag_y_out.ap(), a=ag_a_out.ap(), rowidx=rowidx, attidx=attidx, dep=agbuf))
        final = {}
        for sem, val in out_toks:
            if isinstance(val, DSem):
                val = val.n
            k = id(sem)
            if k not in final or final[k][1] < val:
                final[k] = (sem, val)
        for sem, val in final.values():
            nc.gpsimd.wait_ge(sem, val)
    return nc


def S_sample(nc, es, L):
    PE, DVE, ACT, POOL, SP = L["PE"], L["DVE"], L["ACT"], L["POOL"], L["SP"]
    sb, ps, dsem, out_toks = L["sb"], L["ps"], L["dsem"], L["out_toks"]
    wb, wrep, cbrep, dtbrep, negA = L["wb"], L["wrep"], L["cbrep"], L["dtbrep"], L["negA"]
    psA, psKV, psDT = L["psA"], L["psKV"], L["psDT"]
    cosSt, sinSt, rope = L["cosSt"], L["sinSt"], L["rope"]
    xsT, scv, ssm0 = L["xsT"], L["scv"], L["ssm0"]
    skn, svn, scvo, sss = L["skn"], L["svn"], L["scvo"], L["sss"]

    sld = dsem("sld")
    sst = dsem("sst")
    xs_st = sb("xs_st", [128, 8, 512])
    xs_b = sb("xs_b", [128, 8, 512], BF16)
    cat = sb("cat", [128, 7, NCONV])
    SP.dma(sld, xs_st.t[:], xsT[:, :, :], writes=[xs_st])
    SP.dma(sld, cat.t[:, 0:3, :], scv[:, :, :], writes=[cat])
    DVE.op(lambda: nc.vector.tensor_copy(out=xs_b.t[:], in_=xs_st.t[:]), reads=[xs_st], writes=[xs_b])

    kvs = sb("kvs", [128, 4, 256])
    for l in range(4):
        lhs = lambda c: xs_b.t[:, c, l * 128:(l + 1) * 128]
        A = psA[l % 2]
        for c in range(8):
            PE.op(lambda c=c: nc.tensor.matmul(A.t[:, :], lhsT=lhs(c), rhs=wb.t[:, c, 0:512], start=(c == 0), stop=(c == 7)),
                  reads=[xs_b, wb], writes=[A])
        ACT.op(lambda: nc.scalar.copy(out=cat.t[:, 3 + l, 0:512], in_=A.t[:, :]), reads=[A], writes=[cat])
        kvp = psKV.t[:, (l % 2) * 256:(l % 2 + 1) * 256]
        for c in range(8):
            PE.op(lambda c=c: nc.tensor.matmul(kvp[:, 0:128], lhsT=lhs(c), rhs=wb.t[:, c, 512:640], start=(c == 0), stop=(c == 7)),
                  reads=[xs_b, wb], writes=[psKV])
        ACT.op(lambda: nc.scalar.copy(out=cat.t[:, 3 + l, 512:640], in_=kvp[:, 0:128]), reads=[psKV], writes=[cat])
        for c in range(8):
            PE.op(lambda c=c: nc.tensor.matmul(kvp, lhsT=lhs(c), rhs=wb.t[:, c, OK_:OK_ + 256], start=(c == 0), stop=(c == 7)),
                  reads=[xs_b, wb], writes=[psKV])
        rope((DVE, nc.vector), kvp[:, 0:128], kvs.t[:, l, 0:128], cosSt.t[:, l, :], sinSt.t[:, l, :],
             reads=[psKV, cosSt, sinSt], writes=[kvs])
        ACT.op(lambda: nc.scalar.copy(out=kvs.t[:, l, 128:256], in_=kvp[:, 128:256]), reads=[psKV], writes=[kvs])
        for c in range(8):
            PE.op(lambda c=c: nc.tensor.matmul(psDT.t[:, l * 6:(l + 1) * 6], lhsT=lhs(c), rhs=wb.t[:, c, ODT:ODT + 6],
                                               start=(c == 0), stop=(c == 7)), reads=[xs_b, wb], writes=[psDT])
    out_toks.append(POOL.dma(sst, skn[:, :, :], kvs.t[0:64, :, 0:128], reads=[kvs]))
    out_toks.append(POOL.dma(sst, svn[:, :, :], kvs.t[0:64, :, 128:256], reads=[kvs]))
    out_toks.append(POOL.dma(sst, scvo[:, :, :], cat.t[0:64, 4:7, :], reads=[cat]))

    acc = sb("s_acc", [128, 4, NCONV])
    tmp = sb("s_tmp", [128, 4, NCONV])
    wtap = lambda k: wrep.t[:, k, :].unsqueeze(1).to_broadcast([128, 4, NCONV])
    DVE.op(lambda: nc.vector.tensor_tensor(out=acc.t[:], in0=cat.t[:, 0:4, :], in1=wtap(0), op=ALU.mult),
           reads=[cat, wrep], writes=[acc])
    for k in range(1, 4):
        POOL.op(lambda k=k: nc.gpsimd.tensor_tensor(out=tmp.t[:], in0=cat.t[:, k:k + 4, :], in1=wtap(k), op=ALU.mult),
                reads=[cat, wrep], writes=[tmp])
        DVE.op(lambda: nc.vector.tensor_tensor(out=acc.t[:], in0=acc.t[:], in1=tmp.t[:], op=ALU.add),
               reads=[acc, tmp], writes=[acc])
    DVE.op(lambda: nc.vector.tensor_tensor(out=acc.t[:], in0=acc.t[:],
                                           in1=cbrep.t[:].unsqueeze(1).to_broadcast([128, 4, NCONV]), op=ALU.add),
           reads=[acc, cbrep], writes=[acc])
    ACT.op(lambda: nc.scalar.activation(out=acc.t[:], in_=acc.t[:], func=AF.Silu), reads=[acc], writes=[acc])

    sdt = sb("sdt", [128, 4, 6])
    sdtA = sb("sdtA", [128, 4, 6])
    E = sb("sE", [128, 5, 6])
    swf = sb("swf", [128, 4, 6])
    DVE.op(lambda: nc.vector.tensor_tensor(out=sdt.t[:], in0=psDT.t[:, 0:24].rearrange("p (l h) -> p l h", h=6),
                                           in1=dtbrep.t[:].unsqueeze(1).to_broadcast([128, 4, 6]), op=ALU.add),
           reads=[psDT, dtbrep], writes=[sdt])
    ACT.op(lambda: nc.scalar.activation(out=sdt.t[:], in_=sdt.t[:], func=AF.Exp), reads=[sdt], writes=[sdt])
    ACT.op(lambda: nc.scalar.activation(out=sdt.t[:], in_=sdt.t[:], func=AF.Ln, bias=1.0), reads=[sdt], writes=[sdt])
    DVE.op(lambda: nc.vector.tensor_tensor(out=sdtA.t[:], in0=sdt.t[:],
                                           in1=negA.t[:].unsqueeze(1).to_broadcast([128, 4, 6]), op=ALU.mult),
           reads=[sdt, negA], writes=[sdtA])
    DVE.op(lambda: nc.vector.memset(E.t[:, 3, :], 0.0), writes=[E])
    DVE.op(lambda: nc.vector.tensor_copy(out=E.t[:, 2, :], in_=sdtA.t[:, 3, :]), reads=[sdtA], writes=[E])
    DVE.op(lambda: nc.vector.tensor_tensor(out=E.t[:, 1, :], in0=E.t[:, 2, :], in1=sdtA.t[:, 2, :], op=ALU.add),
           reads=[sdtA, E], writes=[E])
    DVE.op(lambda: nc.vector.tensor_tensor(out=E.t[:, 0, :], in0=E.t[:, 1, :], in1=sdtA.t[:, 1, :], op=ALU.add),
           reads=[sdtA, E], writes=[E])
    DVE.op(lambda: nc.vector.tensor_tensor(out=E.t[:, 4, :], in0=E.t[:, 0, :], in1=sdtA.t[:, 0, :], op=ALU.add),
           reads=[sdtA, E], writes=[E])
    ACT.op(lambda: nc.scalar.activation(out=E.t[:], in_=E.t[:], func=AF.Exp), reads=[E], writes=[E])
    DVE.op(lambda: nc.vector.tensor_tensor(out=swf.t[:], in0=sdt.t[:], in1=E.t[:, 0:4, :], op=ALU.mult),
           reads=[sdt, E], writes=[swf])

    xwsel = sb("xwsel", [128, 3, 4, 64])
    dAsel = sb("dAsel", [128, 3])
    for hp in range(3):
        for hh in range(2):
            h = 2 * hp + hh
            r = slice(64 * hh, 64 * hh + 64)
            DVE.op(lambda hp=hp, h=h, r=r: nc.vector.tensor_tensor(
                out=xwsel.t[r, hp, :, :], in0=acc.t[r, :, h * 64:(h + 1) * 64],
                in1=swf.t[r, :, h:h + 1].to_broadcast([64, 4, 64]), op=ALU.mult),
                reads=[acc, swf], writes=[xwsel])
            DVE.op(lambda hp=hp, h=h, r=r: nc.vector.tensor_copy(out=dAsel.t[r, hp:hp + 1], in_=E.t[r, 4, h:h + 1]),
                   reads=[E], writes=[dAsel])

    Ht = [sb(f"sH{i}", [128, 32, 128]) for i in range(2)]
    Tt = [sb(f"sT{i}", [128, 32, 128]) for i in range(2)]
    hld = [dsem(f"hld{i}") for i in range(2)]
    hst = [dsem(f"hst{i}") for i in range(2)]
    it = 0
    tcount = 0
    for hp in range(3):
        for ph in range(2):
            H = Ht[it % 2]
            for hh in range(2):
                SP.dma(hld[it % 2], H.t[64 * hh:64 * hh + 64, :, :], ssm0[:, 2 * hp + hh, 32 * ph:32 * ph + 32, :], writes=[H])
            for l in range(4):
                T = Tt[tcount % 2]
                tcount += 1
                POOL.op(lambda l=l, T=T: nc.gpsimd.tensor_tensor(
                    out=T.t[:], in0=xwsel.t[:, hp, l, 32 * ph:32 * ph + 32].unsqueeze(2).to_broadcast([128, 32, 128]),
                    in1=acc.t[:, l, 384:512].unsqueeze(1).to_broadcast([128, 32, 128]), op=ALU.mult),
                    reads=[xwsel, acc], writes=[T])
                if l == 0:
                    DVE.op(lambda T=T: nc.vector.scalar_tensor_tensor(
                        out=H.t[:].rearrange("p a b -> p (a b)"), in0=H.t[:].rearrange("p a b -> p (a b)"),
                        scalar=dAsel.t[:, hp:hp + 1], in1=T.t[:].rearrange("p a b -> p (a b)"),
                        op0=ALU.mult, op1=ALU.add), reads=[H, dAsel, T], writes=[H])
                else:
                    DVE.op(lambda T=T: nc.vector.tensor_tensor(out=H.t[:], in0=H.t[:], in1=T.t[:], op=ALU.add),
                           reads=[H, T], writes=[H])
            for hh in range(2):
                out_toks.append(SP.dma(hst[it % 2], sss[:, 2 * hp + hh, 32 * ph:32 * ph + 32, :],
                                       H.t[64 * hh:64 * hh + 64, :, :], reads=[H]))
            it += 1


_NC_CACHE = {}


def _rope_tables(pos):
    half = 32
    inv = (10000.0 ** (-np.arange(half, dtype=np.float32) / half)).astype(np.float32)
    ang = pos.astype(np.float32)[:, None] * inv[None, :]
    return np.cos(ang).astype(np.float32), np.sin(ang).astype(np.float32)


def kernel(x_prompt, x_sample, cache_attn_k, cache_attn_v, state_conv, state_ssm,
           w_in, conv_w, conv_b, dt_bias, a_log, d_skip, ssd_norm_g, w_out, ln1_g, ln1_b,
           peer_w_q, peer_keys_1, peer_keys_2, peer_u, peer_v, ln2_g, ln2_b):
    f = lambda a: np.ascontiguousarray(np.asarray(a, dtype=np.float32))
    x_prompt, x_sample = f(x_prompt), f(x_sample)
    w_in0 = f(w_in)[0]
    if "nc" not in _NC_CACHE:
        _NC_CACHE["nc"] = build_program()
    nc = _NC_CACHE["nc"]

    cosp, sinp = _rope_tables(np.arange(SEQ))
    cosP = f(cosp.reshape(NT, 128, 32).transpose(1, 0, 2))
    sinP = f(sinp.reshape(NT, 128, 32).transpose(1, 0, 2))
    coss, sins = _rope_tables(8192 + np.arange(4))
    cosS = f(np.broadcast_to(coss[None], (128, 4, 32)))
    sinS = f(np.broadcast_to(sins[None], (128, 4, 32)))
    tri = f(np.triu(np.ones((128, 128), np.float32), 1).T)

    ck_all = f(cache_attn_k)[0].reshape(128, 2048 * 512)
    cv_all = f(cache_attn_v)[0].reshape(128, 2048 * 512)
    sc0 = f(state_conv)[0]
    ss0 = f(state_ssm)[0]
    cw0, cb0 = f(conv_w)[0], f(conv_b)[0]
    dtb0, alog0 = f(dt_bias)[0], f(a_log)[0]

    I_all = {"x_sample": x_sample, "state_conv": f(state_conv), "state_ssm": f(state_ssm), "w_in": f(w_in),
             "conv_w": f(conv_w), "conv_b": f(conv_b), "dt_bias": f(dt_bias), "a_log": f(a_log), "d_skip": f(d_skip),
             "ssd_norm_g": f(ssd_norm_g), "w_out": f(w_out), "ln1_g": f(ln1_g), "ln1_b": f(ln1_b), "ln2_g": f(ln2_g),
             "ln2_b": f(ln2_b), "peer_w_q": f(peer_w_q), "peer_keys_1": f(peer_keys_1), "peer_keys_2": f(peer_keys_2),
             "peer_u": f(peer_u), "peer_v": f(peer_v)}
    consts2 = s2_consts()
    I_p2 = {"x_prompt": x_prompt, "w_in": f(w_in), "conv_w": f(conv_w), "conv_b": f(conv_b), "dt_bias": f(dt_bias),
            "a_log": f(a_log), "d_skip": f(d_skip), "ssd_norm_g": f(ssd_norm_g)}
    in_maps = []
    meta = []
    for core in range(8):
        b, g = core // 4, core % 4
        cols = np.concatenate([
            1536 + 384 * g + np.arange(384), 3072 + 128 * g + np.arange(128), 3584 + 128 * g + np.arange(128),
            384 * g + np.arange(384), 4120 + 128 * g + np.arange(128), 4632 + 128 * g + np.arange(128),
            5144 + 128 * g + np.arange(128), 4096 + 6 * g + np.arange(6)])
        ccols = np.concatenate([384 * g + np.arange(384), 1536 + 128 * g + np.arange(128), 2048 + 128 * g + np.arange(128)])
        seqs = 64 * b + np.arange(64)
        xs = x_sample[seqs]
        xs_t = xs.transpose(2, 1, 0)
        xs_t = np.concatenate([xs_t, xs_t], axis=2)
        scv = sc0[seqs][:, :, ccols]
        in_maps.append({
            "xT": f(x_prompt[b].T.reshape(8, 128, SEQ).transpose(1, 0, 2)),
            "wc": f(w_in0[:, cols].reshape(8, 128, NW).transpose(1, 0, 2)),
            "cw": f(cw0[:, ccols]), "cb": f(cb0[ccols][None]),
            "dtb": f(dtb0[6 * g:6 * g + 6][None]), "alog": f(alog0[6 * g:6 * g + 6][None]),
            "cosP": cosP, "sinP": sinP, "cosS": cosS, "sinS": sinS, "tri": tri,
            "xsT": f(xs_t.reshape(8, 128, 512).transpose(1, 0, 2)),
            "scv": f(np.concatenate([scv, scv], axis=0)),
            "ssm0": f(ss0[seqs][:, 6 * g:6 * g + 6]),
            "ck": ck_all[16 * core:16 * core + 16], "cv": cv_all[16 * core:16 * core + 16],
        })
        in_maps[-1].update(s2_host_inputs(core, I_all))
        pin = p2_host_inputs(b, g, I_p2)
        pin.pop("p_xT")
        in_maps[-1].update(pin)
        seg = core % 4
        nt_ = TAIL_TILES * 128
        in_maps[-1]["t_x"] = f(x_prompt[b, nt_ * seg:nt_ * (seg + 1)])
        pp = np.arange(128, dtype=np.int64)[:, None]
        tt_ = np.arange(TAIL_TILES, dtype=np.int64)[None, :, None]
        rr = np.arange(4, dtype=np.int64)[None, None, :]
        chunk_ = seg * 4 + tt_ // 4
        in_maps[-1]["rowidx"] = np.ascontiguousarray((chunk_ * 2048 + rr * 512 + (tt_ % 4) * 128 + pp[:, :, None])
                                                     .reshape(128, TAIL_TILES * 4).astype(np.uint32))
        row_in = seg * 128 + pp
        in_maps[-1]["attidx"] = np.ascontiguousarray(((row_in // 64) * 256 + np.arange(4, dtype=np.int64)[None, :] * 64 + row_in % 64)
                                                     .astype(np.uint32))
        in_maps[-1].update(consts2)
        meta.append((b, g, ccols, seqs))

    res = run_bass_kernel_spmd(nc, in_maps, core_ids=list(range(8)))
    R = res.results

    y_prompt = np.zeros((2, SEQ, D_MODEL), np.float32)
    y_sample = np.zeros((128, 4, D_MODEL), np.float32)
    p_k = np.zeros((1, 2, 2048, 8, 64), np.float32)
    p_v = np.zeros((1, 2, 2048, 8, 64), np.float32)
    p_conv = np.zeros((1, 2, 3, 2560), np.float32)
    p_ssm = np.zeros((1, 2, 24, 64, 128), np.float32)
    s_k = np.zeros((1, 128, 2048, 8, 64), np.float32)
    s_v = np.zeros((1, 128, 2048, 8, 64), np.float32)
    s_conv = np.zeros((1, 128, 3, 2560), np.float32)
    s_ssm = np.zeros((1, 128, 24, 64, 128), np.float32)
    for core in range(8):
        b, g, ccols, seqs = meta[core]
        r = R[core]
        p_k[0, b, :, 2 * g:2 * g + 2, :] = r["pk"].reshape(2048, 2, 64)
        p_v[0, b, :, 2 * g:2 * g + 2, :] = r["pv"].reshape(2048, 2, 64)
        p_conv[0, b][:, ccols] = r["pcv"]
        p_ssm[0, b, 6 * g:6 * g + 6] = r["pss"].reshape(6, 64, 128)
        s_k[0, 16 * core:16 * core + 16, 0:ROWS_COPY] = r["skc"].reshape(16, ROWS_COPY, 8, 64)
        s_v[0, 16 * core:16 * core + 16, 0:ROWS_COPY] = r["svc"].reshape(16, ROWS_COPY, 8, 64)
        s_k[0, seqs, ROWS_COPY:2048, 2 * g:2 * g + 2, :] = r["skn"].reshape(64, 4, 2, 64)
        s_v[0, seqs, ROWS_COPY:2048, 2 * g:2 * g + 2, :] = r["svn"].reshape(64, 4, 2, 64)
        sc = s_conv[0, seqs]
        sc[:, :, ccols] = r["scvo"]
        s_conv[0, seqs] = sc
        s_ssm[0, seqs, 6 * g:6 * g + 6] = r["sss"]
        y_sample[16 * core:16 * core + 16] = r["ys"].reshape(4, 16, D_MODEL).transpose(1, 0, 2)
        nt_ = TAIL_TILES * 128
        y_prompt[b, nt_ * (core % 4):nt_ * (core % 4 + 1)] = r["t_y"]
    return (y_prompt, y_sample, p_k, p_v, p_conv, p_ssm, s_k, s_v, s_conv, s_ssm)
```

```python
import numpy as np
from contextlib import ExitStack
import concourse.bass as bass
import concourse.mybir as mybir
from concourse.bass_utils import run_bass_kernel_spmd

F32 = mybir.dt.float32
BF16 = mybir.dt.bfloat16
ALU = mybir.AluOpType
AF = mybir.ActivationFunctionType

D_MODEL = 1024
SEQ = 8192
NT = SEQ // 128
BLK = 256
NBLK = SEQ // BLK
HALO = 3
NW = 1414
OX, OB, OC, OZ, OQ, OK_, OV, ODT = 0, 384, 512, 640, 1024, 1152, 1280, 1408
NCONV = 640
KV0_TILE = 48
ROWS_COPY = 2044
NSEQ_COPY = 16


class Buf:
    def __init__(self, t):
        self.t = t
        self.w = None
        self.r = []


class DSem:
    def __init__(self, nc, es, name):
        self.sem = es.enter_context(nc.semaphore(name))
        self.n = 0


class Eng:
    def __init__(self, nc, es, eng, name, same_sync=True):
        self.eng = eng
        self.name = name
        self.sem = es.enter_context(nc.semaphore("prog_" + name))
        self.n = 0
        self.seen = {}
        self.same_sync = same_sync

    def wait_tok(self, tok):
        if tok is None:
            return
        sem, val = tok
        if isinstance(val, DSem):
            val = val.n
        if sem is self.sem and not self.same_sync:
            return
        k = id(sem)
        if self.seen.get(k, 0) >= val:
            return
        self.eng.wait_ge(sem, val)
        self.seen[k] = val

    def deps(self, reads, writes):
        for b in reads:
            self.wait_tok(b.w)
        for b in writes:
            self.wait_tok(b.w)
            for t in b.r:
                self.wait_tok(t)

    @staticmethod
    def _mark(tok, reads, writes):
        for b in reads:
            b.r = [t for t in b.r if t[0] is not tok[0]] + [tok]
        for b in writes:
            b.w = tok
            b.r = []

    def op(self, fn, reads=(), writes=(), **kw):
        self.deps(reads, writes)
        inst = fn(**kw)
        self.n += 1
        inst.then_inc(self.sem, 1)
        tok = (self.sem, self.n)
        self._mark(tok, reads, writes)
        return tok

    def dma(self, dsem, out, in_, reads=(), writes=()):
        self.deps(reads, writes)
        inst = self.eng.dma_start(out=out, in_=in_)
        dsem.n += 16
        inst.then_inc(dsem.sem, 16)
        tok = (dsem.sem, dsem)
        self._mark(tok, reads, writes)
        return tok


U32 = mybir.dt.uint32
AX = mybir.AxisListType

NS = 16
NTOK = 64
IN_W = 5656
ALPHA = 2.0 ** 0.25
LN_EPS = 1e-5


def s2_consts():
    tok = [(l, b) for l in range(4) for b in range(NS)]
    cumsel = np.zeros((64, 64), np.float32)
    bsel = np.zeros((64, 4, 64), np.float32)
    delta = np.zeros((64, 16), np.float32)
    maskneg = np.zeros((64, 4), np.float32)
    multnew = np.zeros((64, 4), np.float32)
    for i, (l1, b1) in enumerate(tok):
        delta[i, b1] = 1.0
        for l in range(4):
            maskneg[i, l] = 0.0 if l1 <= l else -30000.0
            multnew[i, l] = 0.0 if l1 > l else (3.0 if l1 == l else 1.0)
        for j, (l2, b2) in enumerate(tok):
            if b1 == b2 and l1 <= l2:
                cumsel[i, j] = 1.0
            if b1 == b2:
                bsel[i, l1, j] = 1.0
    eye16 = np.broadcast_to(np.eye(16, dtype=np.float32)[None], (128, 16, 16)).copy()
    mask1 = (np.arange(128)[:, None] >= np.arange(4)[None, :]).astype(np.float32)
    selall = np.zeros((8, 127), np.float32)
    selall[:, 63] = 1.0
    iota16 = np.broadcast_to(np.arange(16, dtype=np.float32)[None], (128, 16)).copy()
    half = 32
    inv = (10000.0 ** (-np.arange(half, dtype=np.float32) / half)).astype(np.float32)
    pos = np.array([8192 + l for (l, b) in tok], np.float32)
    ang = pos[:, None] * inv[None, :]
    return {
        "c_ident": np.eye(128, dtype=np.float32), "c_cumsel": cumsel, "c_bsel": bsel, "c_delta": delta,
        "c_maskneg": maskneg, "c_multnew": multnew, "c_eye16": eye16, "c_mask1": mask1, "c_selall": selall,
        "c_iota16": iota16, "c_cos": np.cos(ang).astype(np.float32), "c_sin": np.sin(ang).astype(np.float32),
        "c_bd8": np.eye(8, dtype=np.float32),
    }


S2_INPUTS = {
    "s2_xT": [128, 8, 48 + 64], "s2_x": [128, 1024], "s2_win": [128, 8, IN_W], "s2_cw": [4, 2560], "s2_cb": [1, 2560],
    "s2_dtb": [1, 24], "s2_alog": [1, 24], "s2_dskip": [1, 24], "s2_ng": [1, 1536], "s2_sctap": [3, 64, 2560],
    "s2_h0": [NS, 12, 128, 128], "s2_wout": [128, 16, 1024], "s2_ln1g": [1, 1024], "s2_ln1b": [1, 1024],
    "s2_ln2g": [1, 1024], "s2_ln2b": [1, 1024], "s2_wq": [128, 8, 2048], "s2_k1T": [128, 128], "s2_k2T": [128, 128],
    "s2_pu": [16384, 1024], "s2_pv": [16384, 1024],
    "c_ident": [128, 128], "c_cumsel": [64, 64], "c_bsel": [64, 4, 64], "c_delta": [64, 16], "c_maskneg": [64, 4],
    "c_multnew": [64, 4], "c_eye16": [128, 16, 16], "c_mask1": [128, 4], "c_selall": [8, 127], "c_iota16": [128, 16],
    "c_cos": [64, 32], "c_sin": [64, 32], "c_bd8": [8, 8],
}


def s2_host_inputs(core, I):
    f = lambda a: np.ascontiguousarray(np.asarray(a, dtype=np.float32))
    seqs = slice(NS * core, NS * core + NS)
    xs = I["x_sample"][seqs]
    xs_lb = xs.transpose(1, 0, 2).reshape(64, 1024)
    xT = np.zeros((1024, 48 + 64), np.float32)
    xT[:, 48:] = xs_lb.T
    sc = I["state_conv"][0, seqs]
    sctap = np.zeros((3, 4, NS, 2560), np.float32)
    for k in range(3):
        for l in range(4):
            if l + k <= 2:
                sctap[k, l] = sc[:, l + k, :]
    d = {
        "s2_xT": f(xT.reshape(8, 128, 112).transpose(1, 0, 2)),
        "s2_x": f(np.concatenate([xs_lb, xs_lb], axis=0)),
        "s2_win": f(I["w_in"][0].reshape(8, 128, IN_W).transpose(1, 0, 2)),
        "s2_cw": f(I["conv_w"][0]), "s2_cb": f(I["conv_b"]), "s2_dtb": f(I["dt_bias"]), "s2_alog": f(I["a_log"]),
        "s2_dskip": f(I["d_skip"]), "s2_ng": f(I["ssd_norm_g"]), "s2_sctap": f(sctap.reshape(3, 64, 2560)),
        "s2_h0": f(I["state_ssm"][0, seqs].reshape(NS, 12, 128, 128)),
        "s2_wout": f(I["w_out"][0].reshape(16, 128, 1024).transpose(1, 0, 2)),
        "s2_ln1g": f(I["ln1_g"]), "s2_ln1b": f(I["ln1_b"]), "s2_ln2g": f(I["ln2_g"]), "s2_ln2b": f(I["ln2_b"]),
        "s2_wq": f(I["peer_w_q"][0].reshape(8, 128, 2048).transpose(1, 0, 2)),
        "s2_k1T": f(I["peer_keys_1"][0].T), "s2_k2T": f(I["peer_keys_2"][0].T),
        "s2_pu": f(I["peer_u"][0]), "s2_pv": f(I["peer_v"][0]),
    }
    return d


def emit_s2(nc, es, K, T, ck, cv, ys_out, dbg=None):
    import concourse.bass as bass
    PE, DVE, ACT, POOL, SP = K["PE"], K["DVE"], K["ACT"], K["POOL"], K["SP"]
    sb, dsem, out_toks, PB = K["sb"], K["dsem"], K["out_toks"], K["PB"]
    sb_persist = sb
    V, G, S = nc.vector, nc.gpsimd, nc.scalar

    def mm(out, lhsT, rhs, start, stop, reads, writes):
        PE.op(lambda: nc.tensor.matmul(out, lhsT=lhsT, rhs=rhs, start=start, stop=stop), reads=reads, writes=writes)

    def tt(out, in0, in1, op, reads, writes, eng=None):
        E_, e_ = (DVE, V) if eng is None else eng
        E_.op(lambda: e_.tensor_tensor(out=out, in0=in0, in1=in1, op=op), reads=reads, writes=writes)

    def act(out, in_, func, reads, writes, **kw):
        ACT.op(lambda: S.activation(out=out, in_=in_, func=func, **kw), reads=reads, writes=writes)

    def cp(out, in_, reads, writes, eng="act"):
        if eng == "act":
            ACT.op(lambda: S.copy(out=out, in_=in_), reads=reads, writes=writes)
        else:
            DVE.op(lambda: V.tensor_copy(out=out, in_=in_), reads=reads, writes=writes)

    ld = dsem("s2ld")

    def load(name, shape, src=None, dt=F32):
        b = sb("t_" + name, shape, dt)
        SP.dma(ld, b.t[:], T[name] if src is None else src, writes=[b])
        return b

    def bload(name, n, parts=128):
        b = sb("r_" + name, [parts, n])
        SP.dma(ld, b.t[:], T[name].rearrange("o e -> (o e)").partition_broadcast(parts), writes=[b])
        return b

    ident = load("c_ident", [128, 128])
    cumsel = load("c_cumsel", [64, 64])
    bsel = load("c_bsel", [64, 4, 64])
    delta = load("c_delta", [64, 16])
    maskneg = load("c_maskneg", [64, 4])
    multnew = load("c_multnew", [64, 4])
    eye16 = load("c_eye16", [128, 16, 16])
    mask1 = load("c_mask1", [128, 4])
    selall = load("c_selall", [8, 127])
    iota16 = load("c_iota16", [128, 16])
    cost = load("c_cos", [64, 32])
    sint = load("c_sin", [64, 32])
    bd8 = load("c_bd8", [8, 8])
    dtb = bload("s2_dtb", 24, 64)
    negA = bload("s2_alog", 24, 64)
    dsk = bload("s2_dskip", 24, 64)
    ngr = bload("s2_ng", 1536, 64)
    act(negA.t[:], negA.t[:], AF.Exp, [negA], [negA])
    DVE.op(lambda: V.tensor_scalar(out=negA.t[:], in0=negA.t[:], scalar1=-1.0, scalar2=None, op0=ALU.mult),
           reads=[negA], writes=[negA])

    zt = sb("s2_z", [64, 1536])
    xbc = sb("s2_xbc", [64, 2560])
    dtr = sb("s2_dtr", [64, 24])
    qkv = sb("s2_qkv", [64, 1536])
    ymix = sb("s2_ymix", [64, 2048])
    sb, close1 = K["scope"]()
    xst = sb("s2_xst", [128, 8, 112])
    SP.dma(ld, xst.t[:], T["s2_xT"], writes=[xst])
    xTb = xst
    wst = [sb(f"s2_wst{i}", [128, 8, 512]) for i in range(2)]
    wbf = wst
    wfo_ = sb("s2_wfo", [128, 4, 8, 512])
    wfo = [wfo_, wfo_]
    wld = [dsem(f"s2wld{i}") for i in range(2)]
    wrp = [sb(f"s2_wrp{i}", [128, 4, 512]) for i in range(2)]
    cbr = [sb(f"s2_cbr{i}", [64, 512]) for i in range(2)]
    sct = [sb(f"s2_sct{i}", [64, 3, 512]) for i in range(2)]
    ctmp = sb("s2_ctmp", [64, 512])
    cacc = sb("s2_cacc", [64, 512])
    chunks = [(o, min(512, 1536 - o), "z", zt, o) for o in range(0, 1536, 512)]
    chunks += [(1536 + o, 512, "conv", xbc, o) for o in range(0, 2560, 512)]
    chunks += [(4096, 24, "dt", dtr, 0)]
    chunks += [(4120 + o, 512, "qkv", qkv, o) for o in range(0, 1536, 512)]
    for ci, (c0, n, kind, dst, do) in enumerate(chunks):
        i = ci % 2
        SP.dma(wld[i], wst[i].t[:, :, 0:n], T["s2_win"][:, :, c0:c0 + n], writes=[wst[i]])
        P = PB[i]
        if kind != "conv":
            for c in range(8):
                mm(P.t[0:64, 0:n], xTb.t[:, c, 48:112], wbf[i].t[:, c, 0:n], c == 0, c == 7, [xTb, wbf[i]], [P])
            cp(dst.t[:, do:do + n], P.t[0:64, 0:n], [P], [dst])
        else:
            cc = c0 - 1536
            SP.dma(wld[i], wrp[i].t[:], T["s2_cw"][:, cc:cc + 512].partition_broadcast(128), writes=[wrp[i]])
            SP.dma(wld[i], cbr[i].t[:], T["s2_cb"][0, cc:cc + 512].partition_broadcast(64), writes=[cbr[i]])
            SP.dma(wld[i], sct[i].t[:], T["s2_sctap"][:, :, cc:cc + 512].rearrange("k t e -> t k e"), writes=[sct[i]])
            for tap in range(4):
                eng = (POOL, G) if tap % 2 else (DVE, V)
                tt(wfo[i].t[:, tap, :, :], wst[i].t[:, :, :], wrp[i].t[:, tap, :].unsqueeze(1).to_broadcast([128, 8, 512]),
                   ALU.mult, [wst[i], wrp[i]], [wfo[i]], eng=eng)
            n_ = 0
            for tap in range(4):
                for c in range(8):
                    mm(P.t[0:64, :], xTb.t[:, c, 16 * tap:16 * tap + 64], wfo[i].t[:, tap, c, :], n_ == 0, n_ == 31,
                       [xTb, wfo[i]], [P])
                    n_ += 1
            tt(cacc.t[:], sct[i].t[:, 0, :], wrp[i].t[0:64, 0, :], ALU.mult, [sct[i], wrp[i]], [cacc])
            for k in (1, 2):
                tt(ctmp.t[:], sct[i].t[:, k, :], wrp[i].t[0:64, k, :], ALU.mult, [sct[i], wrp[i]], [ctmp])
                tt(cacc.t[:], cacc.t[:], ctmp.t[:], ALU.add, [cacc, ctmp], [cacc])
            tt(cacc.t[:], cacc.t[:], cbr[i].t[:], ALU.add, [cacc, cbr[i]], [cacc])
            tt(cacc.t[:], cacc.t[:], P.t[0:64, :], ALU.add, [cacc, P], [cacc])
            act(dst.t[:, do:do + 512], cacc.t[:], AF.Silu, [cacc], [dst])
    if dbg is not None:
        out_toks.append(POOL.dma(ld, dbg["d_xbc"][:, :], xbc.t[:], reads=[xbc]))
        out_toks.append(POOL.dma(ld, dbg["d_qkv"][:, :], qkv.t[:], reads=[qkv]))

    close1()
    sb, close2 = K["scope"]()
    dt_ = sb("s2_dt", [64, 24])
    dtA = sb("s2_dtA", [64, 24])
    tt(dt_.t[:], dtr.t[:], dtb.t[:], ALU.add, [dtr, dtb], [dt_])
    act(dt_.t[:], dt_.t[:], AF.Exp, [dt_], [dt_])
    act(dt_.t[:], dt_.t[:], AF.Ln, [dt_], [dt_], bias=1.0)
    tt(dtA.t[:], dt_.t[:], negA.t[:], ALU.mult, [dt_, negA], [dtA])
    a_in = sb("s2_a", [64, 24])
    ea = sb("s2_ea", [64, 24])
    mm(PB[0].t[0:64, 0:24], cumsel.t[:], dtA.t[:], True, True, [cumsel, dtA], [PB[0]])
    cp(a_in.t[:], PB[0].t[0:64, 0:24], [PB[0]], [a_in])
    act(ea.t[:], a_in.t[:], AF.Exp, [a_in], [ea])

    Csh = sb("s2_Csh", [64, 4, 512])
    ash = sb("s2_ash", [64, 4, 24])
    for l in range(4):
        mm(PB[1 + l].t[0:64, :], bsel.t[:, l, :], xbc.t[:, 2048:2560], True, True, [bsel, xbc], [PB[1 + l]])
        cp(Csh.t[:, l, :], PB[1 + l].t[0:64, :], [PB[1 + l]], [Csh])
        mm(PB[5].t[0:64, l * 24:(l + 1) * 24], bsel.t[:, l, :], a_in.t[:], True, True, [bsel, a_in], [PB[5]])
    cp(ash.t[:], PB[5].t[0:64, 0:96].rearrange("p (l h) -> p l h", l=4), [PB[5]], [ash])
    prodG = sb("s2_prodG", [64, 4, 512])
    GT = sb("s2_GT", [64, 4, 4])
    tt(prodG.t[:], Csh.t[:], xbc.t[:, 1536:2048].unsqueeze(1).to_broadcast([64, 4, 512]), ALU.mult, [Csh, xbc], [prodG])
    DVE.op(lambda: V.tensor_reduce(out=GT.t[:].rearrange("p l g -> p (l g)"),
                                   in_=prodG.t[:].rearrange("p l (g n) -> p (l g) n", g=4), axis=AX.X, op=ALU.add),
           reads=[prodG], writes=[GT])
    dec = sb("s2_dec", [64, 4, 24])
    tt(dec.t[:], ash.t[:], a_in.t[:].unsqueeze(1).to_broadcast([64, 4, 24]), ALU.subtract, [ash, a_in], [dec])
    tt(dec.t[:], dec.t[:], maskneg.t[:].unsqueeze(2).to_broadcast([64, 4, 24]), ALU.add, [dec, maskneg], [dec])
    act(dec.t[:], dec.t[:], AF.Exp, [dec], [dec])
    for l in range(4):
        tt(dec.t[:, l, :].rearrange("p (g r) -> p g r", g=4), dec.t[:, l, :].rearrange("p (g r) -> p g r", g=4),
           GT.t[:, l, :].unsqueeze(2).to_broadcast([64, 4, 6]), ALU.mult, [dec, GT], [dec])
    tt(dec.t[:], dec.t[:], dt_.t[:].unsqueeze(1).to_broadcast([64, 4, 24]), ALU.mult, [dec, dt_], [dec])
    Mh = sb("s2_Mh", [64, 24, 4, 16])
    for l in range(4):
        tt(Mh.t[:, :, l, :], dec.t[:, l, :].unsqueeze(2).to_broadcast([64, 24, 16]),
           delta.t[:].unsqueeze(1).to_broadcast([64, 24, 16]), ALU.mult, [dec, delta], [Mh])
    for h in range(24):
        P = PB[1 + h // 8]
        mm(P.t[0:64, (h % 8) * 64:(h % 8 + 1) * 64], Mh.t[:, h, :, :].rearrange("p l b -> p (l b)"),
           xbc.t[:, h * 64:(h + 1) * 64], True, True, [Mh, xbc], [P])
    yd = sb("s2_yd", [64, 1536])
    for k in range(3):
        cp(yd.t[:, k * 512:(k + 1) * 512], PB[1 + k].t[0:64, :], [PB[1 + k]], [yd])

    CTm = sb("s2_CTm", [128, 4, 16, 64])
    CTs = sb("s2_CTs", [128, 64])
    for g in range(4):
        PE.op(lambda g=g: nc.tensor.transpose(out=PB[0].t[:, g * 64:(g + 1) * 64], in_=xbc.t[:, 2048 + g * 128:2048 + (g + 1) * 128],
                                              identity=ident.t[0:64, 0:64]), reads=[xbc, ident], writes=[PB[0]])
        cp(CTs.t[:], PB[0].t[:, g * 64:(g + 1) * 64], [PB[0]], [CTs], eng="dve")
        tt(CTm.t[:, g, :, :].rearrange("p s (l b) -> p s l b", l=4),
           CTs.t[:].rearrange("p (l b) -> p l b", l=4).unsqueeze(1).to_broadcast([128, 16, 4, 16]),
           eye16.t[:].unsqueeze(2).to_broadcast([128, 16, 4, 16]), ALU.mult, [CTs, eye16], [CTm])
    Hb = [sb(f"s2_H{i}", [128, NS, 128]) for i in range(2)]
    hld = [dsem(f"s2hld{i}") for i in range(2)]
    HT = [sb(f"s2_HT{i}", [128, 128]) for i in range(2)]
    n_t = 0
    for hp in range(12):
        H = Hb[hp % 2]
        SP.dma(hld[hp % 2], H.t[:], T["s2_h0"][:, hp, :, :].rearrange("b q n -> q b n"), writes=[H])
        P = PB[4 + hp // 4]
        for b in range(NS):
            qd = PB[7].t[:, (n_t % 4) * 128:(n_t % 4 + 1) * 128]
            PE.op(lambda: nc.tensor.transpose(out=qd, in_=H.t[:, b, :], identity=ident.t[:]), reads=[H, ident], writes=[PB[7]])
            ht = HT[n_t % 2]
            cp(ht.t[:], qd, [PB[7]], [ht], eng=("act" if n_t % 2 else "dve"))
            mm(P.t[0:64, (hp % 4) * 128:(hp % 4 + 1) * 128], CTm.t[:, hp // 3, b, :], ht.t[:], b == 0, b == NS - 1, [CTm, ht], [P])
            n_t += 1
    y = sb("s2_y", [64, 1536])
    tmpy = sb("s2_tmpy", [64, 1536])
    for k in range(3):
        tt(y.t[:, k * 512:(k + 1) * 512].rearrange("p (h f) -> p h f", h=8), PB[4 + k].t[0:64, :].rearrange("p (h f) -> p h f", h=8),
           ea.t[:, k * 8:(k + 1) * 8].unsqueeze(2).to_broadcast([64, 8, 64]), ALU.mult, [PB[4 + k], ea], [y])
    tt(y.t[:], y.t[:], yd.t[:], ALU.add, [y, yd], [y])
    tt(tmpy.t[:].rearrange("p (h f) -> p h f", h=24), xbc.t[:, 0:1536].rearrange("p (h f) -> p h f", h=24),
       dsk.t[:].unsqueeze(2).to_broadcast([64, 24, 64]), ALU.mult, [xbc, dsk], [tmpy])
    tt(y.t[:], y.t[:], tmpy.t[:], ALU.add, [y, tmpy], [y])
    act(zt.t[:], zt.t[:], AF.Silu, [zt], [zt])
    tt(y.t[:], y.t[:], zt.t[:], ALU.mult, [y, zt], [y])
    tt(tmpy.t[:], y.t[:], y.t[:], ALU.mult, [y], [tmpy])
    ss = sb("s2_ss", [64, 4])
    DVE.op(lambda: V.tensor_reduce(out=ss.t[:], in_=tmpy.t[:].rearrange("p (g f) -> p g f", g=4), axis=AX.X, op=ALU.add),
           reads=[tmpy], writes=[ss])
    DVE.op(lambda: V.tensor_scalar(out=ss.t[:], in0=ss.t[:], scalar1=1.0 / 384.0, scalar2=LN_EPS, op0=ALU.mult, op1=ALU.add),
           reads=[ss], writes=[ss])
    act(ss.t[:], ss.t[:], AF.Sqrt, [ss], [ss])
    DVE.op(lambda: V.reciprocal(out=ss.t[:], in_=ss.t[:]), reads=[ss], writes=[ss])
    tt(y.t[:].rearrange("p (g f) -> p g f", g=4), y.t[:].rearrange("p (g f) -> p g f", g=4),
       ss.t[:].unsqueeze(2).to_broadcast([64, 4, 384]), ALU.mult, [y, ss], [y])
    tt(ymix.t[:, 0:1536], y.t[:], ngr.t[:], ALU.mult, [y, ngr], [ymix])
    if dbg is not None:
        out_toks.append(POOL.dma(ld, dbg["d_yssd"][:, :], ymix.t[:, 0:1536], reads=[ymix]))
    close2()
    return dict(ymix=ymix, qkv=qkv, load=load, bload=bload, mm=mm, tt=tt, act=act, cp=cp, ident=ident, bsel=bsel,
                delta=delta, multnew=multnew, mask1=mask1, selall=selall, iota16=iota16, cost=cost, sint=sint, bd8=bd8, ld=ld)


def emit_s2b(nc, es, K, T, ck, cv, ys_out, R, dbg=None, stage=4):
    import concourse.bass as bass
    PE, DVE, ACT, POOL, SP = K["PE"], K["DVE"], K["ACT"], K["POOL"], K["SP"]
    sb, dsem, out_toks, PB = K["sb"], K["dsem"], K["out_toks"], K["PB"]
    V, G, S = nc.vector, nc.gpsimd, nc.scalar
    mm0, tt, act, cp, ld = R["mm"], R["tt"], R["act"], R["cp"], R["ld"]
    ymix, qkv, ident, bsel, delta = R["ymix"], R["qkv"], R["ident"], R["bsel"], R["delta"]
    multnew, mask1, selall, iota16, cost, sint, bd8 = (R[k] for k in ("multnew", "mask1", "selall", "iota16", "cost", "sint", "bd8"))

    def mm(out, lhsT, rhs, start, stop, reads, writes, skip=False):
        PE.op(lambda: nc.tensor.matmul(out, lhsT=lhsT, rhs=rhs, start=start, stop=stop, skip_group_check=skip),
              reads=reads, writes=writes)

    def red(out, in_, reads, writes):
        DVE.op(lambda: V.tensor_reduce(out=out, in_=in_, axis=AX.X, op=ALU.add), reads=reads, writes=writes)

    qr = sb("s2_qr", [64, 512])
    kr = sb("s2_kr", [64, 512])
    sb, close3 = K["scope"]()
    rc = sb("s2_rc", [64, 512])
    rs = sb("s2_rs", [64, 512])
    v4 = lambda ap: ap.rearrange("p (h two f) -> p h two f", h=8, two=2)
    cb4 = cost.t[:].unsqueeze(1).unsqueeze(1).to_broadcast([64, 8, 2, 32])
    sb3 = sint.t[:].unsqueeze(1).to_broadcast([64, 8, 32])
    for src_o, dst in ((0, qr), (512, kr)):
        s4 = v4(qkv.t[:, src_o:src_o + 512])
        tt(v4(rc.t[:]), s4, cb4, ALU.mult, [qkv, cost], [rc])
        tt(v4(rs.t[:])[:, :, 0, :], s4[:, :, 1, :], sb3, ALU.mult, [qkv, sint], [rs])
        tt(v4(rs.t[:])[:, :, 1, :], s4[:, :, 0, :], sb3, ALU.mult, [qkv, sint], [rs])
        tt(v4(dst.t[:])[:, :, 0, :], v4(rc.t[:])[:, :, 0, :], v4(rs.t[:])[:, :, 0, :], ALU.subtract, [rc, rs], [dst])
        tt(v4(dst.t[:])[:, :, 1, :], v4(rc.t[:])[:, :, 1, :], v4(rs.t[:])[:, :, 1, :], ALU.add, [rc, rs], [dst])

    Qsh = sb("s2_Qsh", [64, 4, 512])
    for l in range(4):
        mm(PB[l].t[0:64, :], bsel.t[:, l, :], qr.t[:], True, True, [bsel, qr], [PB[l]])
        cp(Qsh.t[:, l, :], PB[l].t[0:64, :], [PB[l]], [Qsh], eng=("act" if l % 2 else "dve"))
    tt(Qsh.t[:], Qsh.t[:], kr.t[:].unsqueeze(1).to_broadcast([64, 4, 512]), ALU.mult, [Qsh, kr], [Qsh])
    Pn = sb("s2_Pn", [64, 4, 8])
    red(Pn.t[:].rearrange("p l h -> p (l h)"), Qsh.t[:].rearrange("p l (h d) -> p (l h) d", h=8), [Qsh], [Pn])
    act(Pn.t[:], Pn.t[:], AF.Exp, [Pn], [Pn], scale=0.125)
    tt(Pn.t[:], Pn.t[:], multnew.t[:].unsqueeze(2).to_broadcast([64, 4, 8]), ALU.mult, [Pn, multnew], [Pn])
    Mn = sb("s2_Mn", [64, 8, 4, 16])
    for l in range(4):
        tt(Mn.t[:, :, l, :], Pn.t[:, l, :].unsqueeze(2).to_broadcast([64, 8, 16]),
           delta.t[:].unsqueeze(1).to_broadcast([64, 8, 16]), ALU.mult, [Pn, delta], [Mn])
    vaug = sb("s2_vaug", [64, 8, 65])
    DVE.op(lambda: V.memset(vaug.t[:], 1.0), writes=[vaug])
    cp(vaug.t[:, :, 0:64], qkv.t[:, 1024:1536].rearrange("p (h d) -> p h d", h=8), [qkv], [vaug], eng="dve")
    for h in range(8):
        P = PB[4 + h // 4]
        mm(P.t[0:64, (h % 4) * 65:(h % 4 + 1) * 65], Mn.t[:, h, :, :].rearrange("p l b -> p (l b)"), vaug.t[:, h, :],
           h % 4 == 0, False, [Mn, vaug], [P], skip=True)

    ones1 = sb("s2_ones1", [128, 1])
    DVE.op(lambda: V.memset(ones1.t[:], 1.0), writes=[ones1])
    KT = [[sb(f"s2_K{n}_{i}", shp) for n, shp in (("a", [128, 512]), ("b", [128, 4, 512]), ("c", [128, 4, 512]))] for i in range(2)]
    VT = [[sb(f"s2_V{n}_{i}", shp) for n, shp in (("a", [128, 512]), ("b", [128, 4, 512]), ("c", [128, 4, 512]))] for i in range(2)]
    kvld = [dsem(f"s2kv{i}") for i in range(2)]
    selt = [sb(f"s2_selt{i}", [64, 128]) for i in range(2)]
    prod = [sb(f"s2_prod{i}", [128, 512]) for i in range(2)]
    Sall = [sb(f"s2_Sall{i}", [128, 3, 4, 8]) for i in range(2)]
    Om = [sb(f"s2_Om{i}", [8, 8, 65]) for i in range(2)]
    cnt = 0
    for b in range(NS):
        i = b % 2
        for (src, tiles) in ((ck, KT[i]), (cv, VT[i])):
            SP.dma(kvld[i], tiles[0].t[:], src[b, 1920 * 512:2048 * 512].rearrange("(p c) -> p c", c=512), writes=[tiles[0]])
            SP.dma(kvld[i], tiles[1].t[:], src[b, 1536 * 512:2048 * 512].rearrange("(p q c) -> p q c", q=4, c=512), writes=[tiles[1]])
            SP.dma(kvld[i], tiles[2].t[:], src[b, :].rearrange("(p s c) -> p s c", s=16, c=512)[:, 0:4, :], writes=[tiles[2]])
        SA = Sall[i]
        for l in range(4):
            tok = l * 16 + b
            st = selt[cnt % 2]
            DVE.op(lambda: V.tensor_copy(out=st.t[:], in_=ident.t[0:64, tok:tok + 1].to_broadcast([64, 128])), reads=[ident], writes=[st])
            Qb = PB[cnt % 2]
            mm(Qb.t[:, :], st.t[:], qr.t[:], True, True, [st, qr], [Qb])
            for pat, kt in ((0, KT[i][0].t[:, :]), (1, KT[i][1].t[:, l, :]), (2, KT[i][2].t[:, l, :])):
                pr = prod[pat % 2]
                tt(pr.t[:], kt, Qb.t[:, :], ALU.mult, [KT[i][pat], Qb], [pr])
                red(SA.t[:, pat, l, :], pr.t[:].rearrange("p (h d) -> p h d", h=8), [pr], [SA])
            cnt += 1
        act(SA.t[:], SA.t[:], AF.Exp, [SA], [SA], scale=0.125)
        tt(SA.t[:, 0, :, :], SA.t[:, 0, :, :], mask1.t[:].unsqueeze(2).to_broadcast([128, 4, 8]), ALU.mult, [SA, mask1], [SA])
        for l in range(4):
            tok = l * 16 + b
            pO = PB[2 + l % 2]
            zc = l % 2
            for pat, vt in ((0, VT[i][0].t[:, :]), (1, VT[i][1].t[:, l, :]), (2, VT[i][2].t[:, l, :])):
                mm(pO.t[0:8, :], SA.t[:, pat, l, :], vt, pat == 0, pat == 2, [SA, VT[i][pat]], [pO])
                mm(PB[6].t[0:8, zc:zc + 1], SA.t[:, pat, l, :], ones1.t[:], pat == 0, pat == 2, [SA, ones1], [PB[6]])
            om = Om[l % 2]
            tt(om.t[:, :, 0:64], pO.t[0:8, :].rearrange("p (h d) -> p h d", h=8), bd8.t[:].unsqueeze(2).to_broadcast([8, 8, 64]),
               ALU.mult, [pO, bd8], [om])
            DVE.op(lambda: V.tensor_scalar(out=om.t[:, :, 64], in0=bd8.t[:], scalar1=PB[6].t[0:8, zc:zc + 1], scalar2=None,
                                           op0=ALU.mult), reads=[bd8, PB[6]], writes=[om])
            last = (b == NS - 1 and l == 3)
            for hb in range(2):
                mm(PB[4 + hb].t[0:64, 0:260], selall.t[:, 63 - tok:127 - tok], om.t[:, 4 * hb:4 * hb + 4, :].rearrange("p h e -> p (h e)"),
                   False, last, [selall, om], [PB[4 + hb]], skip=True)
    zr = sb("s2_zr", [64, 8])
    for hb in range(2):
        pa = PB[4 + hb].t[0:64, 0:260].rearrange("p (h e) -> p h e", h=4)
        DVE.op(lambda: V.reciprocal(out=zr.t[:, 4 * hb:4 * hb + 4], in_=pa[:, :, 64]), reads=[PB[4 + hb]], writes=[zr])
        tt(ymix.t[:, 1536 + 256 * hb:1536 + 256 * (hb + 1)].rearrange("p (h d) -> p h d", h=4), pa[:, :, 0:64],
           zr.t[:, 4 * hb:4 * hb + 4].unsqueeze(2).to_broadcast([64, 4, 64]), ALU.mult, [PB[4 + hb], zr], [ymix])
    if dbg is not None:
        out_toks.append(POOL.dma(ld, dbg["d_att"][:, :], ymix.t[:, 1536:2048], reads=[ymix]))
    close3()
    if stage < 3:
        return

    sb = K["sb"]
    h1 = sb("s2_h1", [128, 1024])
    lg = [sb(f"s2_lng{i}", [128, 1024]) for i in range(2)]
    sb, close4 = K["scope"]()

    def bl(dst, name):
        SP.dma(ld, dst.t[:], T[name].rearrange("o e -> (o e)").partition_broadcast(128), writes=[dst])

    def layer_norm(r, gname, bname, out):
        bl(lg[0], gname)
        bl(lg[1], bname)
        stats = sb("s2_st_" + gname, [128, 2, 6])
        mv = sb("s2_mv_" + gname, [128, 2])
        rsd = sb("s2_rs_" + gname, [128, 1])
        for c in range(2):
            DVE.op(lambda c=c: V.bn_stats(out=stats.t[:, c, :], in_=r.t[:, c * 512:(c + 1) * 512]), reads=[r], writes=[stats])
        DVE.op(lambda: V.bn_aggr(out=mv.t[:], in_=stats.t[:].rearrange("p c s -> p (c s)")), reads=[stats], writes=[mv])
        DVE.op(lambda: V.tensor_scalar(out=rsd.t[:], in0=mv.t[:, 1:2], scalar1=LN_EPS, scalar2=None, op0=ALU.add), reads=[mv], writes=[rsd])
        act(rsd.t[:], rsd.t[:], AF.Sqrt, [rsd], [rsd])
        DVE.op(lambda: V.reciprocal(out=rsd.t[:], in_=rsd.t[:]), reads=[rsd], writes=[rsd])
        DVE.op(lambda: V.tensor_scalar(out=out.t[:], in0=r.t[:], scalar1=mv.t[:, 0:1], scalar2=rsd.t[:, 0:1], op0=ALU.subtract, op1=ALU.mult),
               reads=[r, mv, rsd], writes=[out])
        tt(out.t[:], out.t[:], lg[0].t[:], ALU.mult, [out, lg[0]], [out])
        tt(out.t[:], out.t[:], lg[1].t[:], ALU.add, [out, lg[1]], [out])

    ymT = sb("s2_ymT", [128, 16, 128])
    for kc in range(16):
        qd = PB[7].t[:, (kc % 4) * 64:(kc % 4) * 64 + 64]
        PE.op(lambda: nc.tensor.transpose(out=qd, in_=ymix.t[:, kc * 128:(kc + 1) * 128], identity=ident.t[0:64, 0:64]),
              reads=[ymix, ident], writes=[PB[7]])
        cp(ymT.t[:, kc, 0:64], qd, [PB[7]], [ymT], eng="act")
        cp(ymT.t[:, kc, 64:128], qd, [PB[7]], [ymT], eng="dve")
    wob = sb("s2_wob", [128, 16, 1024])
    for pc in range(4):
        SP.dma(ld, wob.t[:, 4 * pc:4 * pc + 4, :], T["s2_wout"][:, 4 * pc:4 * pc + 4, :], writes=[wob])
    xres = sb("s2_xres", [128, 1024])
    SP.dma(ld, xres.t[:], T["s2_x"][:, :], writes=[xres])
    r1 = sb("s2_r1", [128, 1024])
    for n in range(2):
        for kc in range(16):
            mm(PB[n].t[:, :], ymT.t[:, kc, :], wob.t[:, kc, n * 512:(n + 1) * 512], kc == 0, kc == 15, [ymT, wob], [PB[n]])
        DVE.op(lambda n=n: V.scalar_tensor_tensor(out=r1.t[:, n * 512:(n + 1) * 512], in0=xres.t[:, n * 512:(n + 1) * 512], scalar=ALPHA,
                                                  in1=PB[n].t[:, :], op0=ALU.mult, op1=ALU.add), reads=[xres, PB[n]], writes=[r1])
    layer_norm(r1, "s2_ln1g", "s2_ln1b", h1)
    if dbg is not None:
        out_toks.append(POOL.dma(ld, dbg["d_h1"][:, :], h1.t[:], reads=[h1]))
    close4()
    if stage < 4:
        return

    sb, close5 = K["scope"]()
    h1T = sb("s2_h1T", [128, 8, 128])
    for kc in range(8):
        qd = PB[7].t[:, (kc % 4) * 128:(kc % 4 + 1) * 128]
        PE.op(lambda: nc.tensor.transpose(out=qd, in_=h1.t[:, kc * 128:(kc + 1) * 128], identity=ident.t[:]), reads=[h1, ident], writes=[PB[7]])
        cp(h1T.t[:, kc, :], qd, [PB[7]], [h1T], eng=("act" if kc % 2 else "dve"))
    wqb = sb("s2_wqb", [128, 8, 2048])
    for pc in range(4):
        SP.dma(ld, wqb.t[:, :, 512 * pc:512 * (pc + 1)], T["s2_wq"][:, :, 512 * pc:512 * (pc + 1)], writes=[wqb])
    kTb = sb("s2_kTs", [128, 2, 128])
    SP.dma(ld, kTb.t[:, 0, :], T["s2_k1T"][:, :], writes=[kTb])
    SP.dma(ld, kTb.t[:, 1, :], T["s2_k2T"][:, :], writes=[kTb])
    qTb = sb("s2_qTb", [128, 16, 128])
    for j in range(16):
        P = PB[2 + j // 4]
        for kc in range(8):
            mm(P.t[:, (j % 4) * 128:(j % 4 + 1) * 128], wqb.t[:, kc, j * 128:(j + 1) * 128], h1T.t[:, kc, :], kc == 0, kc == 7, [wqb, h1T], [P])
        if j % 4 == 3:
            cp(qTb.t[:, j - 3:j + 1, :], P.t[:, :].rearrange("p (j t) -> p j t", j=4), [P], [qTb], eng=("act" if (j // 4) % 2 else "dve"))
    Ssb = sb("s2_Ssb", [128, 16, 128])
    S2b = sb("s2_S2b", [128, 16, 128])
    sbanks = [PB[0], PB[1], PB[6], PB[7]]
    for j in range(16):
        P = sbanks[j // 4]
        mm(P.t[:, (j % 4) * 128:(j % 4 + 1) * 128], qTb.t[:, j, :], kTb.t[:, j % 2, :], True, True, [qTb, kTb], [P])
        if j % 4 == 3:
            cp(Ssb.t[:, j - 3:j + 1, :], P.t[:, :].rearrange("p (j t) -> p j t", j=4), [P], [Ssb], eng=("act" if (j // 4) % 2 else "dve"))
    vals = sb("s2_vals", [128, 16, 16])
    idx = sb("s2_idx", [128, 16, 16], U32)

    def top16(src, tmp, vout, iout, rd):
        DVE.op(lambda: V.max(out=vout[:, 0:8], in_=src), reads=rd, writes=[vals_b])
        DVE.op(lambda: V.max_index(out=iout[:, 0:8], in_max=vout[:, 0:8], in_values=src), reads=rd + [vals_b], writes=[idx_b])
        DVE.op(lambda: V.match_replace(out=tmp, in_to_replace=vout[:, 0:8], in_values=src, imm_value=-1e30), reads=rd + [vals_b], writes=[tmp_b])
        DVE.op(lambda: V.max(out=vout[:, 8:16], in_=tmp), reads=[tmp_b], writes=[vals_b])
        DVE.op(lambda: V.max_index(out=iout[:, 8:16], in_max=vout[:, 8:16], in_values=tmp), reads=[tmp_b, vals_b], writes=[idx_b])

    vals_b, idx_b, tmp_b = vals, idx, S2b
    for j in range(16):
        top16(Ssb.t[:, j, :], S2b.t[:, j, :], vals.t[:, j, :], idx.t[:, j, :], [Ssb])
    idxf = sb("s2_idxf", [128, 16, 16])
    cp(idxf.t[:], idx.t[:], [idx], [idxf], eng="dve")
    cand = sb("s2_cand", [128, 8, 256])
    cand2 = sb("s2_cand2", [128, 8, 256])
    v4v = vals.t[:].rearrange("p (h s) a -> p h s a", s=2)
    tt(cand.t[:].rearrange("p h (a b) -> p h a b", a=16), v4v[:, :, 0, :].unsqueeze(3).to_broadcast([128, 8, 16, 16]),
       v4v[:, :, 1, :].unsqueeze(2).to_broadcast([128, 8, 16, 16]), ALU.add, [vals], [cand])
    sc = sb("s2_sc", [128, 8, 16])
    ci = sb("s2_ci", [128, 8, 16], U32)
    vals_b, idx_b, tmp_b = sc, ci, cand2
    for h in range(8):
        top16(cand.t[:, h, :], cand2.t[:, h, :], sc.t[:, h, :], ci.t[:, h, :], [cand])
    au = sb("s2_au", [128, 8, 16], U32)
    bu = sb("s2_bu", [128, 8, 16], U32)
    af = sb("s2_af", [128, 8, 16])
    bf = sb("s2_bf", [128, 8, 16])
    DVE.op(lambda: V.tensor_scalar(out=au.t[:], in0=ci.t[:], scalar1=4, scalar2=None, op0=ALU.logical_shift_right), reads=[ci], writes=[au])
    DVE.op(lambda: V.tensor_scalar(out=bu.t[:], in0=ci.t[:], scalar1=15, scalar2=None, op0=ALU.bitwise_and), reads=[ci], writes=[bu])
    cp(af.t[:], au.t[:], [au], [af], eng="dve")
    cp(bf.t[:], bu.t[:], [bu], [bf], eng="dve")
    eq = sb("s2_eq", [128, 8, 16, 16])
    isel = sb("s2_isel", [128, 2, 8, 16])
    i4 = idxf.t[:].rearrange("p (h s) a -> p h s a", s=2)
    io4 = iota16.t[:].unsqueeze(1).unsqueeze(1).to_broadcast([128, 8, 16, 16])
    for side, xf in ((0, af), (1, bf)):
        tt(eq.t[:], xf.t[:].unsqueeze(3).to_broadcast([128, 8, 16, 16]), io4, ALU.is_equal, [xf, iota16], [eq])
        tt(eq.t[:], eq.t[:], i4[:, :, side, :].unsqueeze(2).to_broadcast([128, 8, 16, 16]), ALU.mult, [eq, idxf], [eq])
        DVE.op(lambda side=side: V.tensor_reduce(out=isel.t[:, side, :, :].rearrange("p h k -> p (h k)"),
                                                 in_=eq.t[:].rearrange("p h k a -> p (h k) a"), axis=AX.X, op=ALU.add),
               reads=[eq], writes=[isel])
    ef = sb("s2_ef", [128, 128])
    eu = sb("s2_eu", [128, 128], U32)
    DVE.op(lambda: V.scalar_tensor_tensor(out=ef.t[:], in0=isel.t[:, 0, :, :].rearrange("p h k -> p (h k)"), scalar=128.0,
                                          in1=isel.t[:, 1, :, :].rearrange("p h k -> p (h k)"), op0=ALU.mult, op1=ALU.add),
           reads=[isel], writes=[ef])
    cp(eu.t[:], ef.t[:], [ef], [eu], eng="dve")
    gt = sb("s2_gt", [128, 8, 16])
    gs = sb("s2_gs", [128, 8])
    tt(gt.t[:], sc.t[:], sc.t[:, :, 0:1].to_broadcast([128, 8, 16]), ALU.subtract, [sc], [gt])
    act(gt.t[:], gt.t[:], AF.Exp, [gt], [gt])
    red(gs.t[:], gt.t[:], [gt], [gs])
    DVE.op(lambda: V.reciprocal(out=gs.t[:], in_=gs.t[:]), reads=[gs], writes=[gs])
    tt(gt.t[:], gt.t[:], gs.t[:].unsqueeze(2).to_broadcast([128, 8, 16]), ALU.mult, [gt, gs], [gt])
    NB = 4
    ub = [sb(f"s2_ub{i}", [128, 1024]) for i in range(NB)]
    ug = [dsem(f"s2ug{i}") for i in range(NB)]
    junk = sb("s2_junk", [128, 1024])
    hid = sb("s2_hid", [128, 128])

    def gather(slot, table):
        u = ub[gather.n % NB]
        d = ug[gather.n % NB]
        gather.n += 1
        POOL.deps([eu], [u])
        inst = G.indirect_dma_start(out=u.t[:], out_offset=None, in_=T[table][:, :],
                                    in_offset=bass.IndirectOffsetOnAxis(ap=eu.t[:, slot:slot + 1], axis=0))
        d.n += 16
        inst.then_inc(d.sem, 16)
        tok = (d.sem, d)
        POOL._mark(tok, [eu], [u])
        return u
    gather.n = 0
    for slot in range(128):
        u = gather(slot, "s2_pu")
        DVE.op(lambda: V.scalar_tensor_tensor(out=junk.t[:], in0=u.t[:], scalar=1.0, in1=h1.t[:], op0=ALU.mult, op1=ALU.mult,
                                              accum_out=hid.t[:, slot:slot + 1]), reads=[u, h1], writes=[junk, hid])
    act(hid.t[:], hid.t[:], AF.Gelu, [hid], [hid])
    tt(hid.t[:], hid.t[:], gt.t[:].rearrange("p h k -> p (h k)"), ALU.mult, [hid, gt], [hid])
    pacc = sb("s2_pacc", [128, 1024])
    for slot in range(128):
        u = gather(slot, "s2_pv")
        if slot == 0:
            DVE.op(lambda: V.tensor_scalar(out=pacc.t[:], in0=u.t[:], scalar1=hid.t[:, 0:1], scalar2=None, op0=ALU.mult),
                   reads=[u, hid], writes=[pacc])
        else:
            DVE.op(lambda: V.scalar_tensor_tensor(out=pacc.t[:], in0=u.t[:], scalar=hid.t[:, slot:slot + 1], in1=pacc.t[:],
                                                  op0=ALU.mult, op1=ALU.add), reads=[u, hid, pacc], writes=[pacc])
    r2 = pacc
    DVE.op(lambda: V.scalar_tensor_tensor(out=r2.t[:], in0=h1.t[:], scalar=ALPHA, in1=pacc.t[:], op0=ALU.mult, op1=ALU.add),
           reads=[h1, pacc], writes=[r2])
    yo = junk
    layer_norm(r2, "s2_ln2g", "s2_ln2b", yo)
    out_toks.append(POOL.dma(ld, ys_out[:, :], yo.t[0:64, :], reads=[yo]))
    close5()


TAIL_TILES = 16

F32 = mybir.dt.float32
BF16 = mybir.dt.bfloat16
ALU = mybir.AluOpType
AF = mybir.ActivationFunctionType
AX = mybir.AxisListType
LN_EPS = 1e-5
SEQ = 8192
X0, B0, C0, Z0, Q0, QS0, K0, KS0, V0, DT0, NW2 = 0, 384, 512, 640, 1024, 1152, 1280, 1408, 1536, 1664, 1670
BLK = 256
HALO = 3


def p2_inputs(seq):
    return {
        "p_xT": [128, 8, seq], "p_w": [128, 8, NW2], "p_cw": [4, 640], "p_cb": [1, 640], "p_dtb": [1, 6], "p_alog": [1, 6],
        "p_dsk": [1, 6], "p_ng": [1, 384], "p_cosT": [128, seq], "p_sinT": [128, seq], "p_tri": [128, 128], "p_trii": [128, 128],
        "p_maskT": [128, 2, 128], "p_esel": [65, 64], "p_ident": [128, 128],
    }


def p2_host_inputs(b, g, I, seq=SEQ):
    f = lambda a: np.ascontiguousarray(np.asarray(a, dtype=np.float32))
    sw = np.concatenate([np.arange(32, 64), np.arange(0, 32)])
    swap2 = np.concatenate([sw, 64 + sw])
    qc = 4120 + 128 * g + np.arange(128)
    kc = 4632 + 128 * g + np.arange(128)
    cols = np.concatenate([1536 + 384 * g + np.arange(384), 3072 + 128 * g + np.arange(128), 3584 + 128 * g + np.arange(128),
                           384 * g + np.arange(384), qc, qc[swap2], kc, kc[swap2], 5144 + 128 * g + np.arange(128),
                           4096 + 6 * g + np.arange(6)])
    ccols = np.concatenate([384 * g + np.arange(384), 1536 + 128 * g + np.arange(128), 2048 + 128 * g + np.arange(128)])
    inv = (10000.0 ** (-np.arange(32, dtype=np.float32) / 32)).astype(np.float32)
    ang = np.arange(seq, dtype=np.float32)[None, :] * inv[:, None]
    cos64 = np.concatenate([np.cos(ang), np.cos(ang)], axis=0)
    sin64 = np.concatenate([-np.sin(ang), np.sin(ang)], axis=0)
    i_ = np.arange(128)
    return {
        "p_xT": f(I["x_prompt"][b, :seq].T.reshape(8, 128, seq).transpose(1, 0, 2)),
        "p_w": f(I["w_in"][0][:, cols].reshape(8, 128, NW2).transpose(1, 0, 2)),
        "p_cw": f(I["conv_w"][0][:, ccols]), "p_cb": f(I["conv_b"][0][ccols][None]),
        "p_dtb": f(I["dt_bias"][0][6 * g:6 * g + 6][None]), "p_alog": f(I["a_log"][0][6 * g:6 * g + 6][None]),
        "p_dsk": f(I["d_skip"][0][6 * g:6 * g + 6][None]), "p_ng": f(I["ssd_norm_g"][0][384 * g:384 * g + 384][None]),
        "p_cosT": f(np.concatenate([cos64, cos64], axis=0)), "p_sinT": f(np.concatenate([sin64, sin64], axis=0)),
        "p_tri": f((i_[:, None] > i_[None, :])), "p_trii": f((i_[:, None] <= i_[None, :])),
        "p_maskT": f(np.stack([(i_[:, None] >= i_[None, :]), (i_[:, None] <= i_[None, :])], axis=1)),
        "p_esel": f(np.concatenate([np.zeros((64, 64)), np.ones((1, 64))], axis=0)), "p_ident": np.eye(128, dtype=np.float32),
    }


def emit_p2(nc, K, T, yssd_out, attT_out, seq=SEQ, dbg=None, att3d=False):
    PE, DVE, ACT, POOL, SP = K["PE"], K["DVE"], K["ACT"], K["POOL"], K["SP"]
    sb, dsem, out_toks, PB = K["sb"], K["dsem"], K["out_toks"], K["PB"]
    V, G, S = nc.vector, nc.gpsimd, nc.scalar
    NT = seq // 128
    NBLK = seq // BLK

    def mm(out, lhsT, rhs, start, stop, reads, writes):
        PE.op(lambda: nc.tensor.matmul(out, lhsT=lhsT, rhs=rhs, start=start, stop=stop), reads=reads, writes=writes)

    def tt(out, in0, in1, op, reads, writes, eng=None):
        E_, e_ = (DVE, V) if eng is None else eng
        E_.op(lambda: e_.tensor_tensor(out=out, in0=in0, in1=in1, op=op), reads=reads, writes=writes)

    def act(out, in_, func, reads, writes, **kw):
        ACT.op(lambda: S.activation(out=out, in_=in_, func=func, **kw), reads=reads, writes=writes)

    def cp(out, in_, reads, writes, eng="act"):
        if eng == "act":
            ACT.op(lambda: S.copy(out=out, in_=in_), reads=reads, writes=writes)
        else:
            DVE.op(lambda: V.tensor_copy(out=out, in_=in_), reads=reads, writes=writes)

    ld = dsem("p2ld")
    st = dsem("p2st")

    def bload(name, n):
        b = sb("r_" + name, [128, n])
        SP.dma(ld, b.t[:], T[name].rearrange("o e -> (o e)").partition_broadcast(128), writes=[b])
        return b

    qT_all = sb("p_qT", [128, seq], BF16)
    kT_all = sb("p_kT", [128, seq], BF16)
    vT_all = sb("p_vT", [128, seq])
    maskT = sb("p_maskTt", [128, 2, 128])
    esel = sb("p_eselt", [65, 64])
    identf = sb("p_identf", [128, 128])
    SP.dma(ld, identf.t[:], T["p_ident"][:, :], writes=[identf])
    SP.dma(ld, maskT.t[:], T["p_maskT"][:, :, :], writes=[maskT])
    SP.dma(ld, esel.t[:], T["p_esel"][:, :], writes=[esel])

    sb, closeA = K["scope"]()
    tri = sb("p_trit", [128, 128])
    trii = sb("p_triit", [128, 128])
    SP.dma(ld, tri.t[:], T["p_tri"][:, :], writes=[tri])
    SP.dma(ld, trii.t[:], T["p_trii"][:, :], writes=[trii])
    onest = sb("p_onest", [128, 128])
    DVE.op(lambda: V.memset(onest.t[:], 1.0), writes=[onest])
    wrep = sb("p_wrep", [128, 4, 640])
    SP.dma(ld, wrep.t[:], T["p_cw"].rearrange("t e -> (t e)").partition_broadcast(128).rearrange("p (t e) -> p t e", t=4), writes=[wrep])
    cbrep = bload("p_cb", 640)
    dtbrep = bload("p_dtb", 6)
    negA = bload("p_alog", 6)
    dsk = bload("p_dsk", 6)
    ngr = bload("p_ng", 384)
    cbcol = sb("p_cbcol", [128, 2])
    SP.dma(ld, cbcol.t[:, 0:1], T["p_cb"][0:1, 384:512].rearrange("o e -> e o"), writes=[cbcol])
    SP.dma(ld, cbcol.t[:, 1:2], T["p_cb"][0:1, 512:640].rearrange("o e -> e o"), writes=[cbcol])
    act(negA.t[:], negA.t[:], AF.Exp, [negA], [negA])
    DVE.op(lambda: V.tensor_scalar(out=negA.t[:], in0=negA.t[:], scalar1=-1.0, scalar2=None, op0=ALU.mult), reads=[negA], writes=[negA])

    wb = sb("p_wb", [128, 8, NW2], BF16)
    wf = sb("p_wf", [128, 4, 8, 640], BF16)
    wst = [sb(f"p_wst{i}", [128, 8, 128]) for i in range(2)]
    wld = [dsem(f"p2wld{i}") for i in range(2)]
    for i_, o in enumerate(range(0, NW2, 128)):
        n = min(128, NW2 - o)
        s_ = wst[i_ % 2]
        SP.dma(wld[i_ % 2], s_.t[:, :, 0:n], T["p_w"][:, :, o:o + n], writes=[s_])
        cp(wb.t[:, :, o:o + n], s_.t[:, :, 0:n], [s_], [wb], eng="dve")
        if o < 640:
            n2 = min(n, 640 - o)
            for tap in range(4):
                eng = (POOL, G) if tap % 2 else (DVE, V)
                tt(wf.t[:, tap, :, o:o + n2], s_.t[:, :, 0:n2], wrep.t[:, tap, o:o + n2].unsqueeze(1).to_broadcast([128, 8, n2]),
                   ALU.mult, [s_, wrep], [wf], eng=eng)

    xst_ = sb("p_xst", [128, 8, HALO + BLK])
    xst = [xst_, xst_]
    xtb = [sb(f"p_xtb{i}", [128, 8, HALO + BLK], BF16) for i in range(2)]
    xld = [dsem(f"p2xld{i}") for i in range(2)]
    xcount = [0]

    def load_block(blk):
        i = xcount[0] % 2
        xcount[0] += 1
        s_, tb = xst[i], xtb[i]
        if blk == 0:
            DVE.op(lambda: V.memset(s_.t[:, :, 0:HALO], 0.0), writes=[s_])
            SP.dma(xld[i], s_.t[:, :, HALO:HALO + BLK], T["p_xT"][:, :, 0:BLK], writes=[s_])
        else:
            SP.dma(xld[i], s_.t[:, :, :], T["p_xT"][:, :, blk * BLK - HALO:(blk + 1) * BLK], writes=[s_])
        eng, e = (DVE, V) if blk % 2 == 0 else (POOL, G)
        eng.op(lambda: e.tensor_copy(out=tb.t[:], in_=s_.t[:]), reads=[s_], writes=[tb])
        return tb

    psDT = PB[7]
    for blk in range(NBLK):
        tb = load_block(blk)
        for sub in range(BLK // 128):
            j = blk * (BLK // 128) + sub
            c0 = HALO + sub * 128
            for c in range(8):
                mm(psDT.t[:, j * 6:(j + 1) * 6], tb.t[:, c, c0:c0 + 128], wb.t[:, c, DT0:DT0 + 6], c == 0, c == 7, [tb, wb], [psDT])
    NC6 = NT * 6
    dts = sb("p_dts", [128, NC6])
    dtA = sb("p_dtA", [128, NC6])
    wloc = sb("p_wloc", [128, NC6])
    ainc = sb("p_ainc", [128, NC6])
    ea = sb("p_ea", [128, NC6])
    decrow = sb("p_decrow", [128, NC6])
    v3 = lambda ap: ap.rearrange("p (j h) -> p j h", h=6)
    tt(v3(dts.t[:]), v3(psDT.t[:, 0:NC6]), dtbrep.t[:].unsqueeze(1).to_broadcast([128, NT, 6]), ALU.add, [psDT, dtbrep], [dts])
    act(dts.t[:], dts.t[:], AF.Exp, [dts], [dts])
    act(dts.t[:], dts.t[:], AF.Ln, [dts], [dts], bias=1.0)
    tt(v3(dtA.t[:]), v3(dts.t[:]), negA.t[:].unsqueeze(1).to_broadcast([128, NT, 6]), ALU.mult, [dts, negA], [dtA])
    mm(PB[0].t[:, 0:NC6], tri.t[:], dtA.t[:], True, True, [tri, dtA], [PB[0]])
    mm(PB[1].t[:, 0:NC6], trii.t[:], dtA.t[:], True, True, [trii, dtA], [PB[1]])
    mm(PB[2].t[:, 0:NC6], onest.t[:], dtA.t[:], True, True, [onest, dtA], [PB[2]])
    act(wloc.t[:], PB[0].t[:, 0:NC6], AF.Exp, [PB[0]], [wloc])
    tt(wloc.t[:], wloc.t[:], dts.t[:], ALU.mult, [wloc, dts], [wloc])
    cp(ainc.t[:], PB[1].t[:, 0:NC6], [PB[1]], [ainc])
    act(ea.t[:], PB[1].t[:, 0:NC6], AF.Exp, [PB[1]], [ea])
    act(decrow.t[:], PB[2].t[:, 0:NC6], AF.Exp, [PB[2]], [decrow])

    psA, psSt, psZ, psKV, psBC, psSeg, psY, psYo = PB[0], PB[1], PB[2], PB[3], PB[4], PB[5], PB[6], PB[7]
    pre = sb("p_pre", [128, 512])
    xs = sb("p_xs", [128, 384])
    Btok = sb("p_Btok", [128, 128], BF16)
    xdt = sb("p_xdt", [128, 384], BF16)
    xwl = sb("p_xwl", [128, 384], BF16)
    BT = sb("p_BT", [128, 128], BF16)
    CT = sb("p_CT", [128, 128], BF16)
    Gm = sb("p_Gm", [128, 128])
    Uh = [sb(f"p_Uh{i}", [128, 128]) for i in range(2)]
    Eh = [sb(f"p_Eh{i}", [128, 128]) for i in range(2)]
    Mh = [sb(f"p_Mh{i}", [128, 128], BF16) for i in range(2)]
    hT = sb("p_hT", [128, 384])
    hTb = sb("p_hTb", [128, 384], BF16)
    DVE.op(lambda: V.memset(hT.t[:], 0.0), writes=[hT])
    DVE.op(lambda: V.memset(hTb.t[:], 0.0), writes=[hTb])
    zs = sb("p_zs", [128, 384])
    yt = sb("p_yt", [128, 384])
    ytmp = sb("p_ytmp", [128, 384])
    ss = sb("p_ss", [128, 1])
    yo = [sb(f"p_yo{i}", [128, 384]) for i in range(2)]
    yst = [dsem(f"p2yst{i}") for i in range(2)]
    cst = [sb(f"p_cst{i}", [128, 2, 128]) for i in range(2)]
    cld = [dsem(f"p2cld{i}") for i in range(2)]
    rq = sb("p_rq", [128, 128])
    rq2 = sb("p_rq2", [128, 128])

    for blk in range(NBLK):
        tb = load_block(blk)
        for sub in range(BLK // 128):
            j = blk * (BLK // 128) + sub
            i = j % 2
            c0 = HALO + sub * 128
            t0 = j * 128
            SP.dma(cld[i], cst[i].t[:, 0, :], T["p_cosT"][:, t0:t0 + 128], writes=[cst[i]])
            SP.dma(cld[i], cst[i].t[:, 1, :], T["p_sinT"][:, t0:t0 + 128], writes=[cst[i]])
            n = 0
            for tap in range(4):
                for c in range(8):
                    s0 = c0 - 3 + tap
                    mm(psA.t[:, :], tb.t[:, c, s0:s0 + 128], wf.t[:, tap, c, 0:512], n == 0, n == 31, [tb, wf], [psA])
                    n += 1
            for which in range(2):
                n = 0
                for tap in range(4):
                    for c in range(8):
                        s0 = c0 - 3 + tap
                        mm(psBC.t[:, which * 128:(which + 1) * 128], wf.t[:, tap, c, 384 + which * 128:512 + which * 128],
                           tb.t[:, c, s0:s0 + 128], n == 0, n == 31, [tb, wf], [psBC])
                        n += 1
            for c in range(8):
                mm(psZ.t[:, 0:384], tb.t[:, c, c0:c0 + 128], wb.t[:, c, Z0:Z0 + 384], c == 0, c == 7, [tb, wb], [psZ])
            for m in range(4):
                for c in range(8):
                    mm(psKV.t[:, m * 128:(m + 1) * 128], wb.t[:, c, Q0 + m * 128:Q0 + (m + 1) * 128], tb.t[:, c, c0:c0 + 128],
                       c == 0, c == 7, [tb, wb], [psKV])
            for c in range(8):
                mm(psZ.t[:, 384:512], wb.t[:, c, V0:V0 + 128], tb.t[:, c, c0:c0 + 128], c == 0, c == 7, [tb, wb], [psZ])
            tt(pre.t[:], psA.t[:, :], cbrep.t[:, 0:512], ALU.add, [psA, cbrep], [pre])
            act(xs.t[:], pre.t[:, 0:384], AF.Silu, [pre], [xs])
            act(Btok.t[:], pre.t[:, 384:512], AF.Silu, [pre], [Btok])
            act(BT.t[:], psBC.t[:, 0:128], AF.Silu, [psBC, cbcol], [BT], bias=cbcol.t[:, 0:1])
            act(CT.t[:], psBC.t[:, 128:256], AF.Silu, [psBC, cbcol], [CT], bias=cbcol.t[:, 1:2])
            act(zs.t[:], psZ.t[:, 0:384], AF.Silu, [psZ], [zs])
            cp(vT_all.t[:, t0:t0 + 128], psZ.t[:, 384:512], [psZ], [vT_all])
            for (dst, o) in ((qT_all, 0), (kT_all, 256)):
                tt(rq.t[:], psKV.t[:, o:o + 128], cst[i].t[:, 0, :], ALU.mult, [psKV, cst[i]], [rq])
                tt(rq2.t[:], psKV.t[:, o + 128:o + 256], cst[i].t[:, 1, :], ALU.mult, [psKV, cst[i]], [rq2])
                tt(dst.t[:, t0:t0 + 128], rq.t[:], rq2.t[:], ALU.add, [rq, rq2], [dst], eng=(POOL, G))
            dtj = dts.t[:, j * 6:(j + 1) * 6]
            tt(xdt.t[:].rearrange("p (h f) -> p h f", h=6), xs.t[:].rearrange("p (h f) -> p h f", h=6),
               dtj.unsqueeze(2).to_broadcast([128, 6, 64]), ALU.mult, [xs, dts], [xdt])
            tt(xwl.t[:].rearrange("p (h f) -> p h f", h=6), xs.t[:].rearrange("p (h f) -> p h f", h=6),
               wloc.t[:, j * 6:(j + 1) * 6].unsqueeze(2).to_broadcast([128, 6, 64]), ALU.mult, [xs, wloc], [xwl])
            mm(psBC.t[:, 256:384], BT.t[:], CT.t[:], True, True, [BT, CT], [psBC])
            tt(Gm.t[:], psBC.t[:, 256:384], trii.t[:], ALU.mult, [psBC, trii], [Gm])
            mm(psYo.t[:, 0:384], CT.t[:], hTb.t[:], True, True, [CT, hTb], [psYo])
            mm(psSt.t[:, 0:384], Btok.t[:], xwl.t[:], True, True, [Btok, xwl], [psSt])
            for h in range(6):
                u, e_, m_ = Uh[h % 2], Eh[h % 2], Mh[h % 2]
                DVE.op(lambda: V.tensor_scalar(out=u.t[:], in0=tri.t[:], scalar1=dtA.t[:, j * 6 + h:j * 6 + h + 1], scalar2=None, op0=ALU.mult),
                       reads=[tri, dtA], writes=[u])
                sg = psSeg.t[:, (h % 4) * 128:(h % 4 + 1) * 128]
                mm(sg, u.t[:], trii.t[:], True, True, [u, trii], [psSeg])
                act(e_.t[:], sg, AF.Exp, [psSeg], [e_])
                tt(m_.t[:], e_.t[:], Gm.t[:], ALU.mult, [e_, Gm], [m_], eng=((POOL, G) if h % 2 else None))
                mm(psY.t[:, h * 64:(h + 1) * 64], m_.t[:], xdt.t[:, h * 64:(h + 1) * 64], True, True, [m_, xdt], [psY])
            tt(yt.t[:].rearrange("p (h f) -> p h f", h=6), psYo.t[:, 0:384].rearrange("p (h f) -> p h f", h=6),
               ea.t[:, j * 6:(j + 1) * 6].unsqueeze(2).to_broadcast([128, 6, 64]), ALU.mult, [psYo, ea], [yt])
            tt(yt.t[:], yt.t[:], psY.t[:, 0:384], ALU.add, [yt, psY], [yt])
            tt(ytmp.t[:].rearrange("p (h f) -> p h f", h=6), xs.t[:].rearrange("p (h f) -> p h f", h=6),
               dsk.t[:].unsqueeze(2).to_broadcast([128, 6, 64]), ALU.mult, [xs, dsk], [ytmp])
            tt(yt.t[:], yt.t[:], ytmp.t[:], ALU.add, [yt, ytmp], [yt])
            tt(yt.t[:], yt.t[:], zs.t[:], ALU.mult, [yt, zs], [yt])
            DVE.op(lambda: V.scalar_tensor_tensor(out=ytmp.t[:], in0=yt.t[:], scalar=1.0, in1=yt.t[:], op0=ALU.mult, op1=ALU.mult,
                                                  accum_out=ss.t[:, 0:1]), reads=[yt], writes=[ytmp, ss])
            DVE.op(lambda: V.tensor_scalar(out=ss.t[:], in0=ss.t[:], scalar1=1.0 / 384.0, scalar2=LN_EPS, op0=ALU.mult, op1=ALU.add),
                   reads=[ss], writes=[ss])
            act(ss.t[:], ss.t[:], AF.Sqrt, [ss], [ss])
            DVE.op(lambda: V.reciprocal(out=ss.t[:], in_=ss.t[:]), reads=[ss], writes=[ss])
            DVE.op(lambda: V.scalar_tensor_tensor(out=yo[i].t[:], in0=yt.t[:], scalar=ss.t[:, 0:1], in1=ngr.t[:], op0=ALU.mult, op1=ALU.mult),
                   reads=[yt, ss, ngr], writes=[yo[i]])
            out_toks.append(POOL.dma(yst[i], yssd_out[t0:t0 + 128, :], yo[i].t[:], reads=[yo[i]]))
            tt(hT.t[:].rearrange("p (h f) -> p h f", h=6), hT.t[:].rearrange("p (h f) -> p h f", h=6),
               decrow.t[:, j * 6:(j + 1) * 6].unsqueeze(2).to_broadcast([128, 6, 64]), ALU.mult, [hT, decrow], [hT])
            tt(hT.t[:], hT.t[:], psSt.t[:, 0:384], ALU.add, [hT, psSt], [hT])
            cp(hTb.t[:], hT.t[:], [hT], [hTb], eng="dve")
    closeA()

    sb, closeB = K["scope"]()
    NSB = seq // 2048
    TPS = 48
    Vp = [sb(f"p_Vp{i}", [128, TPS, 2, 65], BF16) for i in range(2)]
    for v_ in Vp:
        DVE.op(lambda v_=v_: V.memset(v_.t[:], 1.0), writes=[v_])
    acc = [sb(f"p_acc{i}", [65, 2048]) for i in range(2)]
    Pt = [sb(f"p_Pt{i}", [128, 2, 128]) for i in range(2)]
    Pb = [sb(f"p_Pb{i}", [128, 2, 128], BF16) for i in range(2)]
    zr = sb("p_zr", [64, 512])
    ao = [sb(f"p_ao{i}", [64, 512]) for i in range(2)]
    ast = [dsem(f"p2ast{i}") for i in range(2)]
    pats = ((1, 0), (4, 16), (16, 32))

    def tile_id(d, off, r, nbl):
        return off + r * (16 // d) + nbl

    cnt = 0
    for sbk in range(NSB):
        vp = Vp[sbk % 2]
        base = sbk * 2048
        for d, off in pats:
            for r in range(d):
                for nbl in range(16 // d):
                    start = base + r + d * 128 * nbl
                    pst = PB[6 + cnt % 2]
                    PE.op(lambda: nc.tensor.transpose(out=pst.t[:, 0:128], in_=vT_all.t[:, start:start + d * 127 + 1:d], identity=identf.t[:]),
                          reads=[vT_all, identf], writes=[pst])
                    cp(vp.t[:, tile_id(d, off, r, nbl), :, 0:64], pst.t[:, 0:128].rearrange("p (h e) -> p h e", h=2), [pst], [vp],
                       eng=("act" if cnt % 2 else "dve"))
                    cnt += 1
        for h in range(2):
            ac = acc[h]
            DVE.op(lambda: V.memset(ac.t[:], 0.0), writes=[ac])
            hs = slice(64 * h, 64 * h + 64)
            u_ = 0
            for d, off in pats:
                for r in range(d):
                    for nbl in range(16 // d):
                        qs = base + r + d * 128 * nbl
                        qsl = slice(qs, qs + d * 127 + 1, d)
                        blocks = []
                        if nbl > 0:
                            blocks.append((0, vp, tile_id(d, off, r, nbl - 1), qs - d * 128))
                        elif sbk > 0:
                            blocks.append((0, Vp[(sbk - 1) % 2], tile_id(d, off, r, 16 // d - 1), qs - d * 128))
                        blocks.append((1, vp, tile_id(d, off, r, nbl), qs))
                        psc = PB[u_ % 2]
                        pt, pb = Pt[u_ % 2], Pb[u_ % 2]
                        for kb, _, _, ks in blocks:
                            mm(psc.t[:, kb * 128:(kb + 1) * 128], kT_all.t[hs, ks:ks + d * 127 + 1:d], qT_all.t[hs, qsl], True, True,
                               [kT_all, qT_all], [psc])
                        k0 = blocks[0][0]
                        act(pt.t[:, k0:2, :], psc.t[:, k0 * 128:256].rearrange("p (a q) -> p a q", q=128), AF.Exp, [psc], [pt], scale=0.125)
                        tt(pb.t[:, k0:2, :], pt.t[:, k0:2, :], maskT.t[:, k0:2, :], ALU.mult, [pt, maskT], [pb], eng=((POOL, G) if u_ % 2 else None))
                        po = PB[2 + u_ % 2]
                        for n_, (kb, vsrc, tid, _) in enumerate(blocks):
                            mm(po.t[0:65, 0:128], vsrc.t[:, tid, h, :], pb.t[:, kb, :], n_ == 0, n_ == len(blocks) - 1, [vsrc, pb], [po])
                        asl = ac.t[:, r + d * 128 * nbl:r + d * 128 * nbl + d * 127 + 1:d]
                        tt(asl, asl, po.t[0:65, 0:128], ALU.add, [ac, po], [ac])
                        u_ += 1
            for c4 in range(4):
                pz = PB[4 + c4 % 2]
                mm(pz.t[0:64, :], esel.t[:, :], ac.t[:, c4 * 512:(c4 + 1) * 512], True, True, [esel, ac], [pz])
                DVE.op(lambda: V.reciprocal(out=zr.t[:], in_=pz.t[0:64, :]), reads=[pz], writes=[zr])
                a_ = ao[c4 % 2]
                tt(a_.t[:], ac.t[0:64, c4 * 512:(c4 + 1) * 512], zr.t[:], ALU.mult, [ac, zr], [a_])
                dst_ = (attT_out[sbk, 64 * h:64 * h + 64, c4 * 512:(c4 + 1) * 512] if att3d
                        else attT_out[64 * h:64 * h + 64, base + c4 * 512:base + (c4 + 1) * 512])
                out_toks.append(POOL.dma(ast[c4 % 2], dst_, a_.t[:], reads=[a_]))
    closeB()


F32 = mybir.dt.float32
U32 = mybir.dt.uint32
ALU = mybir.AluOpType
AF = mybir.ActivationFunctionType
AX = mybir.AxisListType
ALPHA = 2.0 ** 0.25
LN_EPS = 1e-5

TAIL_INPUTS = lambda ntok: {
    "t_ymT": [128, 16, ntok], "t_x": [ntok, 1024], "t_wout": [128, 16, 1024], "t_ln1g": [1, 1024], "t_ln1b": [1, 1024],
    "t_ln2g": [1, 1024], "t_ln2b": [1, 1024], "t_wq": [128, 8, 2048], "t_k1T": [128, 128], "t_k2T": [128, 128],
    "t_pu": [16384, 1024], "t_pv": [16384, 1024], "t_ident": [128, 128], "t_iota16": [128, 16],
}


def emit_tail(nc, K, T, y_out, ntiles, ag=None):
    import concourse.bass as bass
    PE, DVE, ACT, POOL, SP = K["PE"], K["DVE"], K["ACT"], K["POOL"], K["SP"]
    sb, dsem, out_toks, PB = K["sb"], K["dsem"], K["out_toks"], K["PB"]
    V, G, S = nc.vector, nc.gpsimd, nc.scalar

    def mm(out, lhsT, rhs, start, stop, reads, writes):
        PE.op(lambda: nc.tensor.matmul(out, lhsT=lhsT, rhs=rhs, start=start, stop=stop), reads=reads, writes=writes)

    def tt(out, in0, in1, op, reads, writes):
        DVE.op(lambda: V.tensor_tensor(out=out, in0=in0, in1=in1, op=op), reads=reads, writes=writes)

    def act(out, in_, func, reads, writes, **kw):
        ACT.op(lambda: S.activation(out=out, in_=in_, func=func, **kw), reads=reads, writes=writes)

    def cp(out, in_, reads, writes, eng="act"):
        if eng == "act":
            ACT.op(lambda: S.copy(out=out, in_=in_), reads=reads, writes=writes)
        else:
            DVE.op(lambda: V.tensor_copy(out=out, in_=in_), reads=reads, writes=writes)

    def red(out, in_, reads, writes):
        DVE.op(lambda: V.tensor_reduce(out=out, in_=in_, axis=AX.X, op=ALU.add), reads=reads, writes=writes)

    ld = dsem("tld")
    ident = sb("t_identt", [128, 128])
    iota16 = sb("t_iotat", [128, 16])
    SP.dma(ld, ident.t[:], T["t_ident"][:, :], writes=[ident])
    SP.dma(ld, iota16.t[:], T["t_iota16"][:, :], writes=[iota16])
    lnp = {}
    for nm in ("t_ln1g", "t_ln1b", "t_ln2g", "t_ln2b"):
        lnp[nm] = sb("r_" + nm, [128, 1024])
        SP.dma(ld, lnp[nm].t[:], T[nm].rearrange("o e -> (o e)").partition_broadcast(128), writes=[lnp[nm]])
    kTb = sb("t_kT", [128, 2, 128])
    SP.dma(ld, kTb.t[:, 0, :], T["t_k1T"][:, :], writes=[kTb])
    SP.dma(ld, kTb.t[:, 1, :], T["t_k2T"][:, :], writes=[kTb])

    stats = sb("t_stats", [128, 2, 6])
    mv = sb("t_mv", [128, 2])
    rsd = sb("t_rsd", [128, 1])

    def layer_norm(r, g, b, out):
        for c in range(2):
            DVE.op(lambda c=c: V.bn_stats(out=stats.t[:, c, :], in_=r.t[:, c * 512:(c + 1) * 512]), reads=[r], writes=[stats])
        DVE.op(lambda: V.bn_aggr(out=mv.t[:], in_=stats.t[:].rearrange("p c s -> p (c s)")), reads=[stats], writes=[mv])
        DVE.op(lambda: V.tensor_scalar(out=rsd.t[:], in0=mv.t[:, 1:2], scalar1=LN_EPS, scalar2=None, op0=ALU.add), reads=[mv], writes=[rsd])
        act(rsd.t[:], rsd.t[:], AF.Sqrt, [rsd], [rsd])
        DVE.op(lambda: V.reciprocal(out=rsd.t[:], in_=rsd.t[:]), reads=[rsd], writes=[rsd])
        DVE.op(lambda: V.tensor_scalar(out=out.t[:], in0=r.t[:], scalar1=mv.t[:, 0:1], scalar2=rsd.t[:, 0:1], op0=ALU.subtract, op1=ALU.mult),
               reads=[r, mv, rsd], writes=[out])
        tt(out.t[:], out.t[:], g.t[:], ALU.mult, [out, g], [out])
        tt(out.t[:], out.t[:], b.t[:], ALU.add, [out, b], [out])

    ymT = [sb(f"t_ymT{i}", [128, 16, 128]) for i in range(2)]
    xres = [sb(f"t_xres{i}", [128, 1024]) for i in range(2)]
    tl = [dsem(f"t_tl{i}") for i in range(2)]
    wpc = [sb(f"t_wpc{i}", [128, 4096]) for i in range(2)]
    wl = [dsem(f"t_wl{i}") for i in range(2)]
    h1 = [sb(f"t_h1_{i}", [128, 1024]) for i in range(2)]
    r1buf = sb("t_r1", [128, 1024])
    h1T = sb("t_h1T", [128, 8, 128])
    qTb = sb("t_qTb", [128, 16, 128])
    Ssb = sb("t_Ssb", [128, 16, 128])
    S2b = sb("t_S2b", [128, 16, 128])
    vals = sb("t_vals", [128, 16, 16])
    idx = sb("t_idx", [128, 16, 16], U32)
    idxf = sb("t_idxf", [128, 16, 16])
    cand = Ssb
    cand2 = S2b
    sc = sb("t_sc", [128, 8, 16])
    ci = sb("t_ci", [128, 8, 16], U32)
    au = sb("t_au", [128, 8, 16], U32)
    bu = sb("t_bu", [128, 8, 16], U32)
    af = sb("t_af", [128, 8, 16])
    bf = sb("t_bf", [128, 8, 16])
    eq = sb("t_eq", [128, 8, 16, 16])
    isel = sb("t_isel", [128, 2, 8, 16])
    ef = sb("t_ef", [128, 128])
    eu = [sb(f"t_eu{i}", [128, 128], U32) for i in range(2)]
    gt = [sb(f"t_gt{i}", [128, 8, 16]) for i in range(2)]
    gs = sb("t_gs", [128, 8])
    NB = 5
    ub = [sb(f"t_ub{i}", [128, 1024]) for i in range(NB)]
    ug = [dsem(f"t_ug{i}") for i in range(NB)]
    junk = sb("t_junk", [128, 1024])
    hid = sb("t_hid", [128, 128])
    pacc = sb("t_pacc", [128, 1024])
    yst = dsem("t_yst")
    wcount = [0]
    gcount = [0]

    def wload(src_ap, shape3):
        i = wcount[0] % 2
        wcount[0] += 1
        w = wpc[i]
        view = w.t[:].rearrange("p (a b) -> p a b", a=shape3[0])
        SP.dma(wl[i], view, src_ap, writes=[w])
        return w, view

    eu_cur = [None]

    def gather(slot, table):
        u = ub[gcount[0] % NB]
        d = ug[gcount[0] % NB]
        gcount[0] += 1
        POOL.deps([eu_cur[0]], [u])
        inst = G.indirect_dma_start(out=u.t[:], out_offset=None, in_=T[table][:, :],
                                    in_offset=bass.IndirectOffsetOnAxis(ap=eu_cur[0].t[:, slot:slot + 1], axis=0))
        d.n += 16
        inst.then_inc(d.sem, 16)
        POOL._mark((d.sem, d), [eu_cur[0]], [u])
        return u

    def top16(src, tmp, vout, iout, src_b, tmp_b, v_b, i_b):
        DVE.op(lambda: V.max(out=vout[:, 0:8], in_=src), reads=[src_b], writes=[v_b])
        DVE.op(lambda: V.max_index(out=iout[:, 0:8], in_max=vout[:, 0:8], in_values=src), reads=[src_b, v_b], writes=[i_b])
        DVE.op(lambda: V.match_replace(out=tmp, in_to_replace=vout[:, 0:8], in_values=src, imm_value=-1e30), reads=[src_b, v_b], writes=[tmp_b])
        DVE.op(lambda: V.max(out=vout[:, 8:16], in_=tmp), reads=[tmp_b], writes=[v_b])
        DVE.op(lambda: V.max_index(out=iout[:, 8:16], in_max=vout[:, 8:16], in_values=tmp), reads=[tmp_b, v_b], writes=[i_b])

    if ag is not None:
        ridx = sb("t_ridx", [128, ntiles * 4], U32)
        aidx = sb("t_aidx", [128, 4], U32)
        SP.dma(ld, ridx.t[:], ag["rowidx"][:, :], writes=[ridx])
        SP.dma(ld, aidx.t[:], ag["attidx"][:, :], writes=[aidx])
        attseg = sb("t_attseg", [128, 4, 2048])
        agd = dsem("t_agd")
        for r in range(4):
            POOL.deps([aidx, ag["dep"]], [attseg])
            inst = G.indirect_dma_start(out=attseg.t[:, r, :], out_offset=None, in_=ag["a"],
                                        in_offset=bass.IndirectOffsetOnAxis(ap=aidx.t[:, r:r + 1], axis=0))
            agd.n += 16
            inst.then_inc(agd.sem, 16)
            POOL._mark((agd.sem, agd), [aidx, ag["dep"]], [attseg])
        ytok = [sb(f"t_ytok{i}", [128, 1536]) for i in range(2)]
        ygd = [dsem(f"t_ygd{i}") for i in range(2)]
    def phase1(t):
        i = t % 2
        i = t % 2
        if ag is None:
            SP.dma(tl[i], ymT[i].t[:], T["t_ymT"][:, :, t * 128:(t + 1) * 128], writes=[ymT[i]])
        else:
            yk = ytok[i]
            for r in range(4):
                POOL.deps([ridx, ag["dep"]], [yk])
                inst = G.indirect_dma_start(out=yk.t[:, r * 384:(r + 1) * 384], out_offset=None, in_=ag["y"],
                                            in_offset=bass.IndirectOffsetOnAxis(ap=ridx.t[:, t * 4 + r:t * 4 + r + 1], axis=0))
                ygd[i].n += 16
                inst.then_inc(ygd[i].sem, 16)
                POOL._mark((ygd[i].sem, ygd[i]), [ridx, ag["dep"]], [yk])
            for kc in range(12):
                qd = PB[7].t[:, (kc % 4) * 128:(kc % 4 + 1) * 128]
                PE.op(lambda: nc.tensor.transpose(out=qd, in_=yk.t[:, kc * 128:(kc + 1) * 128], identity=ident.t[:]), reads=[yk, ident], writes=[PB[7]])
                cp(ymT[i].t[:, kc, :], qd, [PB[7]], [ymT[i]], eng=("act" if kc % 2 else "dve"))
        yield
        SP.dma(tl[i], xres[i].t[:], T["t_x"][t * 128:(t + 1) * 128, :], writes=[xres[i]])
        yield
        for pc in range(4):
            w, wv = wload(T["t_wout"][:, 4 * pc:4 * pc + 4, :], (4, 1024))
            for n in range(2):
                for kk in range(4):
                    kc = 4 * pc + kk
                    if ag is not None and kc >= 12:
                        lhs, lrd = attseg.t[:, kc - 12, t * 128:(t + 1) * 128], attseg
                    else:
                        lhs, lrd = ymT[i].t[:, kc, :], ymT[i]
                    mm(PB[n].t[:, :], lhs, wv[:, kk, n * 512:(n + 1) * 512], kc == 0, kc == 15, [lrd, w], [PB[n]])
        yield
        r1 = r1buf
        for n in range(2):
            DVE.op(lambda n=n: V.scalar_tensor_tensor(out=r1.t[:, n * 512:(n + 1) * 512], in0=xres[i].t[:, n * 512:(n + 1) * 512], scalar=ALPHA,
                                                      in1=PB[n].t[:, :], op0=ALU.mult, op1=ALU.add), reads=[xres[i], PB[n]], writes=[r1])
        yield
        layer_norm(r1, lnp["t_ln1g"], lnp["t_ln1b"], h1[i])
        yield
        for kc in range(8):
            qd = PB[7].t[:, (kc % 4) * 128:(kc % 4 + 1) * 128]
            PE.op(lambda: nc.tensor.transpose(out=qd, in_=h1[i].t[:, kc * 128:(kc + 1) * 128], identity=ident.t[:]), reads=[h1[i], ident], writes=[PB[7]])
            cp(h1T.t[:, kc, :], qd, [PB[7]], [h1T], eng=("act" if kc % 2 else "dve"))
        yield
        for pc in range(4):
            w, wv = wload(T["t_wq"][:, :, 512 * pc:512 * (pc + 1)], (8, 512))
            P = PB[2 + pc]
            for jj in range(4):
                for kc in range(8):
                    mm(P.t[:, jj * 128:(jj + 1) * 128], wv[:, kc, jj * 128:(jj + 1) * 128], h1T.t[:, kc, :], kc == 0, kc == 7, [w, h1T], [P])
            cp(qTb.t[:, 4 * pc:4 * pc + 4, :], P.t[:, :].rearrange("p (j t) -> p j t", j=4), [P], [qTb], eng=("act" if pc % 2 else "dve"))
        yield
        sbanks = [PB[0], PB[1], PB[6], PB[7]]
        for j in range(16):
            P = sbanks[j // 4]
            mm(P.t[:, (j % 4) * 128:(j % 4 + 1) * 128], qTb.t[:, j, :], kTb.t[:, j % 2, :], True, True, [qTb, kTb], [P])
            if j % 4 == 3:
                cp(Ssb.t[:, j - 3:j + 1, :], P.t[:, :].rearrange("p (j t) -> p j t", j=4), [P], [Ssb], eng=("act" if (j // 4) % 2 else "dve"))
        yield
        for j in range(16):
            top16(Ssb.t[:, j, :], S2b.t[:, j, :], vals.t[:, j, :], idx.t[:, j, :], Ssb, S2b, vals, idx)
        yield
        cp(idxf.t[:], idx.t[:], [idx], [idxf], eng="dve")
        yield
        v4v = vals.t[:].rearrange("p (h s) a -> p h s a", s=2)
        yield
        c3 = cand.t[:].rearrange("p (h x) t -> p h (x t)", h=8)
        yield
        c23 = cand2.t[:].rearrange("p (h x) t -> p h (x t)", h=8)
        yield
        tt(c3.rearrange("p h (a b) -> p h a b", a=16), v4v[:, :, 0, :].unsqueeze(3).to_broadcast([128, 8, 16, 16]),
           v4v[:, :, 1, :].unsqueeze(2).to_broadcast([128, 8, 16, 16]), ALU.add, [vals], [cand])
        for h in range(8):
            top16(c3[:, h, :], c23[:, h, :], sc.t[:, h, :], ci.t[:, h, :], cand, cand2, sc, ci)
        yield
        DVE.op(lambda: V.tensor_scalar(out=au.t[:], in0=ci.t[:], scalar1=4, scalar2=None, op0=ALU.logical_shift_right), reads=[ci], writes=[au])
        yield
        DVE.op(lambda: V.tensor_scalar(out=bu.t[:], in0=ci.t[:], scalar1=15, scalar2=None, op0=ALU.bitwise_and), reads=[ci], writes=[bu])
        yield
        cp(af.t[:], au.t[:], [au], [af], eng="dve")
        yield
        cp(bf.t[:], bu.t[:], [bu], [bf], eng="dve")
        yield
        i4 = idxf.t[:].rearrange("p (h s) a -> p h s a", s=2)
        yield
        io4 = iota16.t[:].unsqueeze(1).unsqueeze(1).to_broadcast([128, 8, 16, 16])
        yield
        for side, xf in ((0, af), (1, bf)):
            tt(eq.t[:], xf.t[:].unsqueeze(3).to_broadcast([128, 8, 16, 16]), io4, ALU.is_equal, [xf, iota16], [eq])
            tt(eq.t[:], eq.t[:], i4[:, :, side, :].unsqueeze(2).to_broadcast([128, 8, 16, 16]), ALU.mult, [eq, idxf], [eq])
            DVE.op(lambda side=side: V.tensor_reduce(out=isel.t[:, side, :, :].rearrange("p h k -> p (h k)"),
                                                     in_=eq.t[:].rearrange("p h k a -> p (h k) a"), axis=AX.X, op=ALU.add),
                   reads=[eq], writes=[isel])
        yield
        DVE.op(lambda: V.scalar_tensor_tensor(out=ef.t[:], in0=isel.t[:, 0, :, :].rearrange("p h k -> p (h k)"), scalar=128.0,
                                              in1=isel.t[:, 1, :, :].rearrange("p h k -> p (h k)"), op0=ALU.mult, op1=ALU.add),
               reads=[isel], writes=[ef])
        yield
        cp(eu[i].t[:], ef.t[:], [ef], [eu[i]], eng="dve")
        yield
        tt(gt[i].t[:], sc.t[:], sc.t[:, :, 0:1].to_broadcast([128, 8, 16]), ALU.subtract, [sc], [gt[i]])
        yield
        act(gt[i].t[:], gt[i].t[:], AF.Exp, [gt[i]], [gt[i]])
        yield
        red(gs.t[:], gt[i].t[:], [gt[i]], [gs])
        yield
        DVE.op(lambda: V.reciprocal(out=gs.t[:], in_=gs.t[:]), reads=[gs], writes=[gs])
        yield
        tt(gt[i].t[:], gt[i].t[:], gs.t[:].unsqueeze(2).to_broadcast([128, 8, 16]), ALU.mult, [gt[i], gs], [gt[i]])

        yield

    def phase2(t, gnext):
        i = t % 2
        eu_cur[0] = eu[i]
        for slot in range(128):
            if gnext is not None and slot % 3 == 0:
                next(gnext, None)
            u = gather(slot, "t_pu")
            DVE.op(lambda: V.scalar_tensor_tensor(out=junk.t[:], in0=u.t[:], scalar=1.0, in1=h1[i].t[:], op0=ALU.mult, op1=ALU.mult,
                                                  accum_out=hid.t[:, slot:slot + 1]), reads=[u, h1[i]], writes=[junk, hid])
        act(hid.t[:], hid.t[:], AF.Gelu, [hid], [hid])
        tt(hid.t[:], hid.t[:], gt[i].t[:].rearrange("p h k -> p (h k)"), ALU.mult, [hid, gt[i]], [hid])
        for slot in range(128):
            if gnext is not None and slot % 3 == 0:
                next(gnext, None)
            u = gather(slot, "t_pv")
            if slot == 0:
                DVE.op(lambda: V.tensor_scalar(out=pacc.t[:], in0=u.t[:], scalar1=hid.t[:, 0:1], scalar2=None, op0=ALU.mult),
                       reads=[u, hid], writes=[pacc])
            else:
                DVE.op(lambda: V.scalar_tensor_tensor(out=pacc.t[:], in0=u.t[:], scalar=hid.t[:, slot:slot + 1], in1=pacc.t[:],
                                                      op0=ALU.mult, op1=ALU.add), reads=[u, hid, pacc], writes=[pacc])
        if gnext is not None:
            for _ in gnext:
                pass
        DVE.op(lambda: V.scalar_tensor_tensor(out=pacc.t[:], in0=h1[i].t[:], scalar=ALPHA, in1=pacc.t[:], op0=ALU.mult, op1=ALU.add),
               reads=[h1[i], pacc], writes=[pacc])
        layer_norm(pacc, lnp["t_ln2g"], lnp["t_ln2b"], junk)
        out_toks.append(SP.dma(yst, y_out[t * 128:(t + 1) * 128, :], junk.t[:], reads=[junk]))

    for _ in phase1(0):
        pass
    for t in range(ntiles):
        phase2(t, phase1(t + 1) if t + 1 < ntiles else None)


def build_program():
    nc = bass.Bass("TRN2", target_bir_lowering=False)

    def din(name, shape):
        return nc.dram_tensor(name, list(shape), F32, kind="ExternalInput").ap()

    def dout(name, shape):
        return nc.dram_tensor(name, list(shape), F32, kind="ExternalOutput").ap()

    xT = din("xT", [128, 8, SEQ])
    wc = din("wc", [128, 8, NW])
    cw = din("cw", [4, NCONV])
    cb = din("cb", [1, NCONV])
    dtb = din("dtb", [1, 6])
    alog = din("alog", [1, 6])
    cosP = din("cosP", [128, NT, 32])
    sinP = din("sinP", [128, NT, 32])
    cosS = din("cosS", [128, 4, 32])
    sinS = din("sinS", [128, 4, 32])
    tri = din("tri", [128, 128])
    xsT = din("xsT", [128, 8, 512])
    scv = din("scv", [128, 3, NCONV])
    ssm0 = din("ssm0", [64, 6, 64, 128])
    ck = din("ck", [NSEQ_COPY, 2048 * 512])
    cv = din("cv", [NSEQ_COPY, 2048 * 512])

    T2 = {n_: din(n_, shp_) for n_, shp_ in S2_INPUTS.items()}
    TP = {n_: (xT if n_ == "p_xT" else din(n_, shp_)) for n_, shp_ in p2_inputs(SEQ).items()}
    t_x = din("t_x", [TAIL_TILES * 128, 1024])
    rowidx = nc.dram_tensor("rowidx", [128, TAIL_TILES * 4], U32, kind="ExternalInput").ap()
    attidx = nc.dram_tensor("attidx", [128, 4], U32, kind="ExternalInput").ap()
    t_y = dout("t_y", [TAIL_TILES * 128, 1024])
    ag_y_in = nc.dram_tensor("ag_y_in", [SEQ, 384], F32)
    ag_y_out = nc.dram_tensor("ag_y_out", [4 * SEQ, 384], F32)
    ag_a_in = nc.dram_tensor("ag_a_in", [4, 128, 2048], F32)
    ag_a_out = nc.dram_tensor("ag_a_out", [4 * 512, 2048], F32)
    ys = dout("ys", [64, 1024])
    pk = dout("pk", [2048, 128])
    pv = dout("pv", [2048, 128])
    pcv = dout("pcv", [3, NCONV])
    pss = dout("pss", [6 * 64, 128])
    skc = dout("skc", [NSEQ_COPY, ROWS_COPY * 512])
    svc = dout("svc", [NSEQ_COPY, ROWS_COPY * 512])
    skn = dout("skn", [64, 4, 128])
    svn = dout("svn", [64, 4, 128])
    scvo = dout("scvo", [64, 3, NCONV])
    sss = dout("sss", [64, 6, 64, 128])

    es = ExitStack()
    with es:
        PE = Eng(nc, es, nc.tensor, "pe", same_sync=False)
        DVE = Eng(nc, es, nc.vector, "dve")
        ACT = Eng(nc, es, nc.scalar, "act")
        POOL = Eng(nc, es, nc.gpsimd, "pool")
        SP = Eng(nc, es, nc.sync, "sp")
        out_toks = []

        es_p = ExitStack()
        all_dsems = []

        es_main = ExitStack()

        def sb(name, shape, dt=F32):
            return Buf(es_main.enter_context(nc.sbuf_tensor(name, list(shape), dt)))

        def sb_top(name, shape, dt=F32):
            return Buf(es.enter_context(nc.sbuf_tensor(name, list(shape), dt)))

        def scope():
            sub = ExitStack()

            def sbs(name, shape, dt=F32):
                return Buf(sub.enter_context(nc.sbuf_tensor(name, list(shape), dt)))

            def close():
                sub.close()
                barrier()
            return sbs, close

        def sbp(name, shape, dt=F32):
            return Buf(es_p.enter_context(nc.sbuf_tensor(name, list(shape), dt)))

        def ps(name, shape, dt=F32):
            return Buf(es.enter_context(nc.psum_tensor(name, list(shape), dt)))

        def dsem(name):
            d = DSem(nc, es, name)
            all_dsems.append(d)
            return d

        def barrier():
            toks = [(e_.sem, e_.n) for e_ in (PE, DVE, ACT, POOL) if e_.n > 0]
            toks += [(d.sem, d.n) for d in all_dsems if d.n > 0 and d is not cp_sem]
            for e_ in (PE, DVE, ACT, POOL, SP):
                for t in toks:
                    e_.wait_tok(t)

        cp_sem = DSem(nc, es, "cp")
        for s in range(NSEQ_COPY):
            for src, dst in ((ck, skc), (cv, svc)):
                i_ap = src[s:s + 1, 4 * 512:2048 * 512].rearrange("o (a f) -> (o a) f", a=16)
                o_ap = dst[s:s + 1, :].rearrange("o (a f) -> (o a) f", a=16)
                out_toks.append(ACT.dma(cp_sem, o_ap, i_ap))

        wb = sb("wb", [128, 8, NW], BF16)
        wrep = sb("wrep", [128, 4, NCONV])
        cbrep = sb("cbrep", [128, NCONV])
        dtbrep = sb("dtbrep", [128, 6])
        negA = sb("negA", [128, 6])
        cosSt = sb("cosSt", [128, 4, 32])
        sinSt = sb("sinSt", [128, 4, 32])
        rtc = sb("rtc", [128, 128])
        rts = sb("rts", [128, 128])
        wf = sbp("wf", [128, 4, 8, 512], BF16)
        trit = sbp("trit", [128, 128])
        onest = sbp("onest", [128, 128])
        ones64 = sbp("ones64", [128, NT])
        cosPt = sbp("cosPt", [128, NT - KV0_TILE, 32])
        sinPt = sbp("sinPt", [128, NT - KV0_TILE, 32])
        wfac = sbp("wfac", [128, NT * 6])
        ld = dsem("ld_const")

        SP.dma(ld, wrep.t[:], cw.rearrange("t e -> (t e)").partition_broadcast(128)
               .rearrange("p (t e) -> p t e", t=4), writes=[wrep])
        SP.dma(ld, cbrep.t[:], cb.rearrange("o e -> (o e)").partition_broadcast(128), writes=[cbrep])
        SP.dma(ld, dtbrep.t[:], dtb.rearrange("o e -> (o e)").partition_broadcast(128), writes=[dtbrep])
        SP.dma(ld, negA.t[:], alog.rearrange("o e -> (o e)").partition_broadcast(128), writes=[negA])
        SP.dma(ld, trit.t[:], tri[:, :], writes=[trit])
        SP.dma(ld, cosPt.t[:], cosP[:, KV0_TILE:NT, :], writes=[cosPt])
        SP.dma(ld, sinPt.t[:], sinP[:, KV0_TILE:NT, :], writes=[sinPt])
        SP.dma(ld, cosSt.t[:], cosS[:, :, :], writes=[cosSt])
        SP.dma(ld, sinSt.t[:], sinS[:, :, :], writes=[sinSt])

        DVE.op(lambda: nc.vector.memset(onest.t[:], 1.0), writes=[onest])
        DVE.op(lambda: nc.vector.memset(ones64.t[:], 1.0), writes=[ones64])
        ACT.op(lambda: nc.scalar.activation(out=negA.t[:], in_=negA.t[:], func=AF.Exp),
               reads=[negA], writes=[negA])
        DVE.op(lambda: nc.vector.tensor_scalar(out=negA.t[:], in0=negA.t[:], scalar1=-1.0, scalar2=None,
                                               op0=ALU.mult), reads=[negA], writes=[negA])

        wst = [sbp(f"wst{i}", [128, 8, 256]) for i in range(2)]
        wld = [dsem(f"wld{i}") for i in range(2)]
        pieces = [(o, min(256, NW - o)) for o in range(0, NW, 256)]
        for i, (o, n) in enumerate(pieces):
            st = wst[i % 2]
            SP.dma(wld[i % 2], st.t[:, :, 0:n], wc[:, :, o:o + n], writes=[st])
            DVE.op(lambda st=st, o=o, n=n: nc.vector.tensor_copy(out=wb.t[:, :, o:o + n], in_=st.t[:, :, 0:n]),
                   reads=[st], writes=[wb])
            if o < 512:
                for tap in range(4):
                    eng = POOL if tap % 2 else DVE
                    e = nc.gpsimd if tap % 2 else nc.vector
                    eng.op(lambda e=e, st=st, o=o, n=n, tap=tap: e.tensor_tensor(
                        out=wf.t[:, tap, :, o:o + n], in0=st.t[:, :, 0:n],
                        in1=wrep.t[:, tap, o:o + n].unsqueeze(1).to_broadcast([128, 8, n]), op=ALU.mult),
                        reads=[st, wrep], writes=[wf])

        xst = [sbp(f"xst{i}", [128, 8, HALO + BLK]) for i in range(2)]
        xtb = [sbp(f"xtb{i}", [128, 8, HALO + BLK], BF16) for i in range(2)]
        xld = [dsem(f"xld{i}") for i in range(2)]
        xcount = [0]

        def load_block(blk):
            i = xcount[0] % 2
            xcount[0] += 1
            st, tb = xst[i], xtb[i]
            if blk == 0:
                DVE.op(lambda: nc.vector.memset(st.t[:, :, 0:HALO], 0.0), writes=[st])
                SP.dma(xld[i], st.t[:, :, HALO:HALO + BLK], xT[:, :, 0:BLK], writes=[st])
            else:
                SP.dma(xld[i], st.t[:, :, :], xT[:, :, blk * BLK - HALO:(blk + 1) * BLK], writes=[st])
            eng, e = (DVE, nc.vector) if blk % 2 == 0 else (POOL, nc.gpsimd)
            eng.op(lambda: e.tensor_copy(out=tb.t[:], in_=st.t[:]), reads=[st], writes=[tb])
            return tb

        psA = [ps(f"psA{i}", [128, 512]) for i in range(2)]
        psKV = ps("psKV", [128, 512])
        psDT = ps("psDT", [128, 512])
        psS = [ps(f"psS{i}", [128, 512]) for i in range(3)]

        for blk in range(NBLK):
            tb = load_block(blk)
            for sub in range(BLK // 128):
                j = blk * (BLK // 128) + sub
                c0 = HALO + sub * 128
                for c in range(8):
                    PE.op(lambda c=c: nc.tensor.matmul(psDT.t[:, j * 6:(j + 1) * 6], lhsT=tb.t[:, c, c0:c0 + 128],
                                                       rhs=wb.t[:, c, ODT:ODT + 6], start=(c == 0), stop=(c == 7)),
                          reads=[tb, wb], writes=[psDT])
        dts = sbp("dts", [128, NT * 6])
        dtA = sbp("dtA", [128, NT * 6])
        tot_hj = sbp("tot_hj", [128, 6, NT])
        pre_hj = sbp("pre_hj", [128, 6, NT])
        Rt = sbp("Rt", [128, NT * 6])
        v3 = lambda b: b.t[:].rearrange("p (j h) -> p j h", h=6)
        DVE.op(lambda: nc.vector.tensor_tensor(out=v3(dts), in0=psDT.t[:, 0:NT * 6].rearrange("p (j h) -> p j h", h=6),
                                               in1=dtbrep.t[:].unsqueeze(1).to_broadcast([128, NT, 6]), op=ALU.add),
               reads=[psDT, dtbrep], writes=[dts])
        ACT.op(lambda: nc.scalar.activation(out=dts.t[:], in_=dts.t[:], func=AF.Exp), reads=[dts], writes=[dts])
        ACT.op(lambda: nc.scalar.activation(out=dts.t[:], in_=dts.t[:], func=AF.Ln, bias=1.0), reads=[dts], writes=[dts])
        DVE.op(lambda: nc.vector.tensor_tensor(out=v3(dtA), in0=v3(dts),
                                               in1=negA.t[:].unsqueeze(1).to_broadcast([128, NT, 6]), op=ALU.mult),
               reads=[dts, negA], writes=[dtA])
        PE.op(lambda: nc.tensor.matmul(psA[0].t[:, 0:NT * 6], lhsT=trit.t[:], rhs=dtA.t[:], start=True, stop=True),
              reads=[trit, dtA], writes=[psA[0]])
        PE.op(lambda: nc.tensor.matmul(psA[1].t[:, 0:NT * 6], lhsT=onest.t[:], rhs=dtA.t[:], start=True, stop=True),
              reads=[onest, dtA], writes=[psA[1]])
        DVE.op(lambda: nc.vector.tensor_copy(out=tot_hj.t[:], in_=psA[1].t[:, 0:NT * 6].rearrange("p (j h) -> p h j", h=6)),
               reads=[psA[1]], writes=[tot_hj])
        for h in range(6):
            DVE.op(lambda h=h: nc.vector.tensor_tensor_scan(out=pre_hj.t[:, h, :], data0=ones64.t[:], data1=tot_hj.t[:, h, :],
                                                            initial=0.0, op0=ALU.mult, op1=ALU.add),
                   reads=[ones64, tot_hj], writes=[pre_hj])
        DVE.op(lambda: nc.vector.tensor_tensor(out=v3(Rt), in0=psA[0].t[:, 0:NT * 6].rearrange("p (j h) -> p j h", h=6),
                                               in1=pre_hj.t[:].rearrange("p h j -> p j h"), op=ALU.subtract),
               reads=[psA[0], pre_hj], writes=[Rt])
        DVE.op(lambda: nc.vector.tensor_tensor(out=v3(Rt), in0=v3(Rt),
                                               in1=pre_hj.t[:, :, NT - 1:NT].rearrange("p h o -> p o h").to_broadcast([128, NT, 6]),
                                               op=ALU.add),
               reads=[Rt, pre_hj], writes=[Rt])
        ACT.op(lambda: nc.scalar.activation(out=Rt.t[:], in_=Rt.t[:], func=AF.Exp), reads=[Rt], writes=[Rt])
        DVE.op(lambda: nc.vector.tensor_tensor(out=wfac.t[:], in0=Rt.t[:], in1=dts.t[:], op=ALU.mult),
               reads=[Rt, dts], writes=[wfac])

        pre_sb = [sbp(f"pre{i}", [128, 512]) for i in range(2)]
        xs_sb = [sbp(f"xs{i}", [128, 384]) for i in range(2)]
        B_sb = [sbp(f"Bb{i}", [128, 128], BF16) for i in range(2)]
        xw_sb = [sbp(f"xw{i}", [128, 384], BF16) for i in range(2)]
        kvo = [sbp(f"kvo{i}", [128, 256]) for i in range(2)]
        kvst = [dsem(f"kvst{i}") for i in range(2)]

        def rope(eng_pair, src_ap, dst_ap, cos_ap, sin_ap, reads, writes):
            ENG, e = eng_pair
            s4 = src_ap.rearrange("p (h two f) -> p h two f", h=2, two=2)
            d4 = dst_ap.rearrange("p (h two f) -> p h two f", h=2, two=2)
            c4 = rtc.t[:].rearrange("p (h two f) -> p h two f", h=2, two=2)
            q4 = rts.t[:].rearrange("p (h two f) -> p h two f", h=2, two=2)
            cb4 = cos_ap.unsqueeze(1).unsqueeze(1).to_broadcast([128, 2, 2, 32])
            sb3 = sin_ap.unsqueeze(1).to_broadcast([128, 2, 32])
            ENG.op(lambda: e.tensor_tensor(out=c4, in0=s4, in1=cb4, op=ALU.mult), reads=reads, writes=[rtc])
            ENG.op(lambda: e.tensor_tensor(out=q4[:, :, 0, :], in0=s4[:, :, 1, :], in1=sb3, op=ALU.mult),
                   reads=reads, writes=[rts])
            ENG.op(lambda: e.tensor_tensor(out=q4[:, :, 1, :], in0=s4[:, :, 0, :], in1=sb3, op=ALU.mult),
                   reads=reads, writes=[rts])
            ENG.op(lambda: e.tensor_tensor(out=d4[:, :, 0, :], in0=c4[:, :, 0, :], in1=q4[:, :, 0, :], op=ALU.subtract),
                   reads=[rtc, rts], writes=writes)
            ENG.op(lambda: e.tensor_tensor(out=d4[:, :, 1, :], in0=c4[:, :, 1, :], in1=q4[:, :, 1, :], op=ALU.add),
                   reads=[rtc, rts], writes=writes)

        def state_mm(j):
            i = j % 2
            for hp in range(3):
                PE.op(lambda hp=hp: nc.tensor.matmul(psS[hp].t[:, 0:128], lhsT=xw_sb[i].t[:, hp * 128:(hp + 1) * 128],
                                                     rhs=B_sb[i].t[:], start=(j == 0), stop=(j == NT - 1)),
                      reads=[xw_sb[i], B_sb[i]], writes=[psS[hp]])

        last_tb = None
        for blk in range(NBLK):
            tb = load_block(blk)
            last_tb = tb
            for sub in range(BLK // 128):
                j = blk * (BLK // 128) + sub
                i = j % 2
                c0 = HALO + sub * 128
                A = psA[i]
                n = 0
                for tap in range(4):
                    for c in range(8):
                        s0 = c0 - 3 + tap
                        PE.op(lambda tap=tap, c=c, s0=s0, n=n: nc.tensor.matmul(
                            A.t[:, :], lhsT=tb.t[:, c, s0:s0 + 128], rhs=wf.t[:, tap, c, :],
                            start=(n == 0), stop=(n == 31)), reads=[tb, wf], writes=[A])
                        n += 1
                if j >= KV0_TILE:
                    kvp = psKV.t[:, i * 256:(i + 1) * 256]
                    for c in range(8):
                        PE.op(lambda c=c: nc.tensor.matmul(kvp, lhsT=tb.t[:, c, c0:c0 + 128], rhs=wb.t[:, c, OK_:OK_ + 256],
                                                           start=(c == 0), stop=(c == 7)), reads=[tb, wb], writes=[psKV])
                if j >= 1:
                    state_mm(j - 1)
                DVE.op(lambda: nc.vector.tensor_tensor(out=pre_sb[i].t[:], in0=A.t[:, :], in1=cbrep.t[:, 0:512], op=ALU.add),
                       reads=[A, cbrep], writes=[pre_sb[i]])
                ACT.op(lambda: nc.scalar.activation(out=xs_sb[i].t[:], in_=pre_sb[i].t[:, 0:384], func=AF.Silu),
                       reads=[pre_sb[i]], writes=[xs_sb[i]])
                ACT.op(lambda: nc.scalar.activation(out=B_sb[i].t[:], in_=pre_sb[i].t[:, 384:512], func=AF.Silu),
                       reads=[pre_sb[i]], writes=[B_sb[i]])
                DVE.op(lambda: nc.vector.tensor_tensor(
                    out=xw_sb[i].t[:].rearrange("p (h f) -> p h f", h=6),
                    in0=xs_sb[i].t[:].rearrange("p (h f) -> p h f", h=6),
                    in1=wfac.t[:, j * 6:(j + 1) * 6].unsqueeze(2).to_broadcast([128, 6, 64]), op=ALU.mult),
                    reads=[xs_sb[i], wfac], writes=[xw_sb[i]])
                if j >= KV0_TILE:
                    jj = j - KV0_TILE
                    rope((POOL, nc.gpsimd) if False else (DVE, nc.vector), psKV.t[:, i * 256:i * 256 + 128],
                         kvo[i].t[:, 0:128], cosPt.t[:, jj, :], sinPt.t[:, jj, :],
                         reads=[psKV, cosPt, sinPt], writes=[kvo[i]])
                    ACT.op(lambda: nc.scalar.copy(out=kvo[i].t[:, 128:256], in_=psKV.t[:, i * 256 + 128:(i + 1) * 256]),
                           reads=[psKV], writes=[kvo[i]])
                    out_toks.append(POOL.dma(kvst[i], pk[jj * 128:(jj + 1) * 128, :], kvo[i].t[:, 0:128], reads=[kvo[i]]))
                    out_toks.append(POOL.dma(kvst[i], pv[jj * 128:(jj + 1) * 128, :], kvo[i].t[:, 128:256], reads=[kvo[i]]))
        state_mm(NT - 1)

        misc = dsem("misc")
        pcv_sb = sbp("pcv_sb", [3, NCONV])
        lc = HALO + BLK - 3
        for (o, n, dstp) in ((0, 512, psA[0]), (512, 128, psA[1])):
            for c in range(8):
                PE.op(lambda c=c, o=o, n=n, dstp=dstp: nc.tensor.matmul(
                    dstp.t[0:3, 0:n], lhsT=last_tb.t[:, c, lc:lc + 3], rhs=wb.t[:, c, o:o + n],
                    start=(c == 0), stop=(c == 7)), reads=[last_tb, wb], writes=[dstp])
            ACT.op(lambda o=o, n=n, dstp=dstp: nc.scalar.copy(out=pcv_sb.t[:, o:o + n], in_=dstp.t[0:3, 0:n]),
                   reads=[dstp], writes=[pcv_sb])
        out_toks.append(POOL.dma(misc, pcv[:, :], pcv_sb.t[:], reads=[pcv_sb]))

        pss_sb = sbp("pss_sb", [128, 3, 128])
        for hp in range(3):
            ACT.op(lambda hp=hp: nc.scalar.copy(out=pss_sb.t[:, hp, :], in_=psS[hp].t[:, 0:128]),
                   reads=[psS[hp]], writes=[pss_sb])
        out_toks.append(POOL.dma(misc, pss.rearrange("(hp q) n -> q hp n", hp=3), pss_sb.t[:], reads=[pss_sb]))

        es_p.close()
        barrier()
        S_sample(nc, es, locals())
        es_main.close()
        barrier()
        psX = ps("psX", [128, 512])
        PBK = [psA[0], psA[1], psKV, psDT, psS[0], psS[1], psS[2], psX]
        sbp2, closeP2 = scope()
        n_before = len(out_toks)
        KP = dict(scope=scope, PE=PE, DVE=DVE, ACT=ACT, POOL=POOL, SP=SP, sb=sbp2, dsem=dsem, out_toks=out_toks, PB=PBK)
        emit_p2(nc, KP, TP, ag_y_in.ap(), ag_a_in.ap(), seq=SEQ, att3d=True)
        closeP2()
        p2_toks = out_toks[n_before:]
        del out_toks[n_before:]
        for tk in p2_toks:
            POOL.wait_tok(tk)
        cc_sem = es.enter_context(nc.semaphore("cc_sem"))
        rg = [[0, 1, 2, 3], [4, 5, 6, 7]]
        n_cc = 0
        ayi, ayo = ag_y_in.ap(), ag_y_out.ap()
        for i_ in range(16):
            nc.gpsimd.collective_compute("AllGather", ALU.bypass, replica_groups=rg, ins=[ayi[512 * i_:512 * (i_ + 1), :].opt()],
                                         outs=[ayo[2048 * i_:2048 * (i_ + 1), :].opt()]).then_inc(cc_sem)
            n_cc += 1
        aai, aao = ag_a_in.ap().rearrange("s c t -> (s c) t"), ag_a_out.ap()
        for i_ in range(8):
            nc.gpsimd.collective_compute("AllGather", ALU.bypass, replica_groups=rg, ins=[aai[64 * i_:64 * (i_ + 1), :].opt()],
                                         outs=[aao[256 * i_:256 * (i_ + 1), :].opt()]).then_inc(cc_sem)
            n_cc += 1
        agbuf = Buf(None)
        agbuf.w = (cc_sem, n_cc)
        sbs2, closeS2 = scope()
        K2 = dict(scope=scope, PE=PE, DVE=DVE, ACT=ACT, POOL=POOL, SP=SP, sb=sbs2, dsem=dsem, out_toks=out_toks, PB=PBK)
        R2 = emit_s2(nc, es, K2, T2, ck, cv, ys)
        emit_s2b(nc, es, K2, T2, ck, cv, ys, R2)
        closeS2()
        TT = {"t_x": t_x, "t_wout": T2["s2_wout"], "t_ln1g": T2["s2_ln1g"], "t_ln1b": T2["s2_ln1b"], "t_ln2g": T2["s2_ln2g"],
              "t_ln2b": T2["s2_ln2b"], "t_wq": T2["s2_wq"], "t_k1T": T2["s2_k1T"], "t_k2T": T2["s2_k2T"], "t_pu": T2["s2_pu"],
              "t_pv": T2["s2_pv"], "t_ident": T2["c_ident"], "t_iota16": T2["c_iota16"]}
        KT_ = dict(scope=scope, PE=PE, DVE=DVE, ACT=ACT, POOL=POOL, SP=SP, sb=sb_top, dsem=dsem, out_toks=out_toks, PB=PBK)
        emit_tail(nc, KT_, TT, t_y, TAIL_TILES,
                  ag=dict(y=ag_y_out.ap(), a=ag_a_out.ap(), rowidx=rowidx, attidx=attidx, dep=agbuf))
        final = {}
        for sem, val in out_toks:
            if isinstance(val, DSem):
                val = val.n
            k = id(sem)
            if k not in final or final[k][1] < val:
                final[k] = (sem, val)
        for sem, val in final.values():
            nc.gpsimd.wait_ge(sem, val)
    return nc


def S_sample(nc, es, L):
    PE, DVE, ACT, POOL, SP = L["PE"], L["DVE"], L["ACT"], L["POOL"], L["SP"]
    sb, ps, dsem, out_toks = L["sb"], L["ps"], L["dsem"], L["out_toks"]
    wb, wrep, cbrep, dtbrep, negA = L["wb"], L["wrep"], L["cbrep"], L["dtbrep"], L["negA"]
    psA, psKV, psDT = L["psA"], L["psKV"], L["psDT"]
    cosSt, sinSt, rope = L["cosSt"], L["sinSt"], L["rope"]
    xsT, scv, ssm0 = L["xsT"], L["scv"], L["ssm0"]
    skn, svn, scvo, sss = L["skn"], L["svn"], L["scvo"], L["sss"]

    sld = dsem("sld")
    sst = dsem("sst")
    xs_st = sb("xs_st", [128, 8, 512])
    xs_b = sb("xs_b", [128, 8, 512], BF16)
    cat = sb("cat", [128, 7, NCONV])
    SP.dma(sld, xs_st.t[:], xsT[:, :, :], writes=[xs_st])
    SP.dma(sld, cat.t[:, 0:3, :], scv[:, :, :], writes=[cat])
    DVE.op(lambda: nc.vector.tensor_copy(out=xs_b.t[:], in_=xs_st.t[:]), reads=[xs_st], writes=[xs_b])

    kvs = sb("kvs", [128, 4, 256])
    for l in range(4):
        lhs = lambda c: xs_b.t[:, c, l * 128:(l + 1) * 128]
        A = psA[l % 2]
        for c in range(8):
            PE.op(lambda c=c: nc.tensor.matmul(A.t[:, :], lhsT=lhs(c), rhs=wb.t[:, c, 0:512], start=(c == 0), stop=(c == 7)),
                  reads=[xs_b, wb], writes=[A])
        ACT.op(lambda: nc.scalar.copy(out=cat.t[:, 3 + l, 0:512], in_=A.t[:, :]), reads=[A], writes=[cat])
        kvp = psKV.t[:, (l % 2) * 256:(l % 2 + 1) * 256]
        for c in range(8):
            PE.op(lambda c=c: nc.tensor.matmul(kvp[:, 0:128], lhsT=lhs(c), rhs=wb.t[:, c, 512:640], start=(c == 0), stop=(c == 7)),
                  reads=[xs_b, wb], writes=[psKV])
        ACT.op(lambda: nc.scalar.copy(out=cat.t[:, 3 + l, 512:640], in_=kvp[:, 0:128]), reads=[psKV], writes=[cat])
        for c in range(8):
            PE.op(lambda c=c: nc.tensor.matmul(kvp, lhsT=lhs(c), rhs=wb.t[:, c, OK_:OK_ + 256], start=(c == 0), stop=(c == 7)),
                  reads=[xs_b, wb], writes=[psKV])
        rope((DVE, nc.vector), kvp[:, 0:128], kvs.t[:, l, 0:128], cosSt.t[:, l, :], sinSt.t[:, l, :],
             reads=[psKV, cosSt, sinSt], writes=[kvs])
        ACT.op(lambda: nc.scalar.copy(out=kvs.t[:, l, 128:256], in_=kvp[:, 128:256]), reads=[psKV], writes=[kvs])
        for c in range(8):
            PE.op(lambda c=c: nc.tensor.matmul(psDT.t[:, l * 6:(l + 1) * 6], lhsT=lhs(c), rhs=wb.t[:, c, ODT:ODT + 6],
                                               start=(c == 0), stop=(c == 7)), reads=[xs_b, wb], writes=[psDT])
    out_toks.append(POOL.dma(sst, skn[:, :, :], kvs.t[0:64, :, 0:128], reads=[kvs]))
    out_toks.append(POOL.dma(sst, svn[:, :, :], kvs.t[0:64, :, 128:256], reads=[kvs]))
    out_toks.append(POOL.dma(sst, scvo[:, :, :], cat.t[0:64, 4:7, :], reads=[cat]))

    acc = sb("s_acc", [128, 4, NCONV])
    tmp = sb("s_tmp", [128, 4, NCONV])
    wtap = lambda k: wrep.t[:, k, :].unsqueeze(1).to_broadcast([128, 4, NCONV])
    DVE.op(lambda: nc.vector.tensor_tensor(out=acc.t[:], in0=cat.t[:, 0:4, :], in1=wtap(0), op=ALU.mult),
           reads=[cat, wrep], writes=[acc])
    for k in range(1, 4):
        POOL.op(lambda k=k: nc.gpsimd.tensor_tensor(out=tmp.t[:], in0=cat.t[:, k:k + 4, :], in1=wtap(k), op=ALU.mult),
                reads=[cat, wrep], writes=[tmp])
        DVE.op(lambda: nc.vector.tensor_tensor(out=acc.t[:], in0=acc.t[:], in1=tmp.t[:], op=ALU.add),
               reads=[acc, tmp], writes=[acc])
    DVE.op(lambda: nc.vector.tensor_tensor(out=acc.t[:], in0=acc.t[:],
                                           in1=cbrep.t[:].unsqueeze(1).to_broadcast([128, 4, NCONV]), op=ALU.add),
           reads=[acc, cbrep], writes=[acc])
    ACT.op(lambda: nc.scalar.activation(out=acc.t[:], in_=acc.t[:], func=AF.Silu), reads=[acc], writes=[acc])

    sdt = sb("sdt", [128, 4, 6])
    sdtA = sb("sdtA", [128, 4, 6])
    E = sb("sE", [128, 5, 6])
    swf = sb("swf", [128, 4, 6])
    DVE.op(lambda: nc.vector.tensor_tensor(out=sdt.t[:], in0=psDT.t[:, 0:24].rearrange("p (l h) -> p l h", h=6),
                                           in1=dtbrep.t[:].unsqueeze(1).to_broadcast([128, 4, 6]), op=ALU.add),
           reads=[psDT, dtbrep], writes=[sdt])
    ACT.op(lambda: nc.scalar.activation(out=sdt.t[:], in_=sdt.t[:], func=AF.Exp), reads=[sdt], writes=[sdt])
    ACT.op(lambda: nc.scalar.activation(out=sdt.t[:], in_=sdt.t[:], func=AF.Ln, bias=1.0), reads=[sdt], writes=[sdt])
    DVE.op(lambda: nc.vector.tensor_tensor(out=sdtA.t[:], in0=sdt.t[:],
                                           in1=negA.t[:].unsqueeze(1).to_broadcast([128, 4, 6]), op=ALU.mult),
           reads=[sdt, negA], writes=[sdtA])
    DVE.op(lambda: nc.vector.memset(E.t[:, 3, :], 0.0), writes=[E])
    DVE.op(lambda: nc.vector.tensor_copy(out=E.t[:, 2, :], in_=sdtA.t[:, 3, :]), reads=[sdtA], writes=[E])
    DVE.op(lambda: nc.vector.tensor_tensor(out=E.t[:, 1, :], in0=E.t[:, 2, :], in1=sdtA.t[:, 2, :], op=ALU.add),
           reads=[sdtA, E], writes=[E])
    DVE.op(lambda: nc.vector.tensor_tensor(out=E.t[:, 0, :], in0=E.t[:, 1, :], in1=sdtA.t[:, 1, :], op=ALU.add),
           reads=[sdtA, E], writes=[E])
    DVE.op(lambda: nc.vector.tensor_tensor(out=E.t[:, 4, :], in0=E.t[:, 0, :], in1=sdtA.t[:, 0, :], op=ALU.add),
           reads=[sdtA, E], writes=[E])
    ACT.op(lambda: nc.scalar.activation(out=E.t[:], in_=E.t[:], func=AF.Exp), reads=[E], writes=[E])
    DVE.op(lambda: nc.vector.tensor_tensor(out=swf.t[:], in0=sdt.t[:], in1=E.t[:, 0:4, :], op=ALU.mult),
           reads=[sdt, E], writes=[swf])

    xwsel = sb("xwsel", [128, 3, 4, 64])
    dAsel = sb("dAsel", [128, 3])
    for hp in range(3):
        for hh in range(2):
            h = 2 * hp + hh
            r = slice(64 * hh, 64 * hh + 64)
            DVE.op(lambda hp=hp, h=h, r=r: nc.vector.tensor_tensor(
                out=xwsel.t[r, hp, :, :], in0=acc.t[r, :, h * 64:(h + 1) * 64],
                in1=swf.t[r, :, h:h + 1].to_broadcast([64, 4, 64]), op=ALU.mult),
                reads=[acc, swf], writes=[xwsel])
            DVE.op(lambda hp=hp, h=h, r=r: nc.vector.tensor_copy(out=dAsel.t[r, hp:hp + 1], in_=E.t[r, 4, h:h + 1]),
                   reads=[E], writes=[dAsel])

    Ht = [sb(f"sH{i}", [128, 32, 128]) for i in range(2)]
    Tt = [sb(f"sT{i}", [128, 32, 128]) for i in range(2)]
    hld = [dsem(f"hld{i}") for i in range(2)]
    hst = [dsem(f"hst{i}") for i in range(2)]
    it = 0
    tcount = 0
    for hp in range(3):
        for ph in range(2):
            H = Ht[it % 2]
            for hh in range(2):
                SP.dma(hld[it % 2], H.t[64 * hh:64 * hh + 64, :, :], ssm0[:, 2 * hp + hh, 32 * ph:32 * ph + 32, :], writes=[H])
            for l in range(4):
                T = Tt[tcount % 2]
                tcount += 1
                POOL.op(lambda l=l, T=T: nc.gpsimd.tensor_tensor(
                    out=T.t[:], in0=xwsel.t[:, hp, l, 32 * ph:32 * ph + 32].unsqueeze(2).to_broadcast([128, 32, 128]),
                    in1=acc.t[:, l, 384:512].unsqueeze(1).to_broadcast([128, 32, 128]), op=ALU.mult),
                    reads=[xwsel, acc], writes=[T])
                if l == 0:
                    DVE.op(lambda T=T: nc.vector.scalar_tensor_tensor(
                        out=H.t[:].rearrange("p a b -> p (a b)"), in0=H.t[:].rearrange("p a b -> p (a b)"),
                        scalar=dAsel.t[:, hp:hp + 1], in1=T.t[:].rearrange("p a b -> p (a b)"),
                        op0=ALU.mult, op1=ALU.add), reads=[H, dAsel, T], writes=[H])
                else:
                    DVE.op(lambda T=T: nc.vector.tensor_tensor(out=H.t[:], in0=H.t[:], in1=T.t[:], op=ALU.add),
                           reads=[H, T], writes=[H])
            for hh in range(2):
                out_toks.append(SP.dma(hst[it % 2], sss[:, 2 * hp + hh, 32 * ph:32 * ph + 32, :],
                                       H.t[64 * hh:64 * hh + 64, :, :], reads=[H]))
            it += 1


_NC_CACHE = {}


def _rope_tables(pos):
    half = 32
    inv = (10000.0 ** (-np.arange(half, dtype=np.float32) / half)).astype(np.float32)
    ang = pos.astype(np.float32)[:, None] * inv[None, :]
    return np.cos(ang).astype(np.float32), np.sin(ang).astype(np.float32)


def kernel(x_prompt, x_sample, cache_attn_k, cache_attn_v, state_conv, state_ssm,
           w_in, conv_w, conv_b, dt_bias, a_log, d_skip, ssd_norm_g, w_out, ln1_g, ln1_b,
           peer_w_q, peer_keys_1, peer_keys_2, peer_u, peer_v, ln2_g, ln2_b):
    f = lambda a: np.ascontiguousarray(np.asarray(a, dtype=np.float32))
    x_prompt, x_sample = f(x_prompt), f(x_sample)
    w_in0 = f(w_in)[0]
    if "nc" not in _NC_CACHE:
        _NC_CACHE["nc"] = build_program()
    nc = _NC_CACHE["nc"]

    cosp, sinp = _rope_tables(np.arange(SEQ))
    cosP = f(cosp.reshape(NT, 128, 32).transpose(1, 0, 2))
    sinP = f(sinp.reshape(NT, 128, 32).transpose(1, 0, 2))
    coss, sins = _rope_tables(8192 + np.arange(4))
    cosS = f(np.broadcast_to(coss[None], (128, 4, 32)))
    sinS = f(np.broadcast_to(sins[None], (128, 4, 32)))
    tri = f(np.triu(np.ones((128, 128), np.float32), 1).T)

    ck_all = f(cache_attn_k)[0].reshape(128, 2048 * 512)
    cv_all = f(cache_attn_v)[0].reshape(128, 2048 * 512)
    sc0 = f(state_conv)[0]
    ss0 = f(state_ssm)[0]
    cw0, cb0 = f(conv_w)[0], f(conv_b)[0]
    dtb0, alog0 = f(dt_bias)[0], f(a_log)[0]

    I_all = {"x_sample": x_sample, "state_conv": f(state_conv), "state_ssm": f(state_ssm), "w_in": f(w_in),
             "conv_w": f(conv_w), "conv_b": f(conv_b), "dt_bias": f(dt_bias), "a_log": f(a_log), "d_skip": f(d_skip),
             "ssd_norm_g": f(ssd_norm_g), "w_out": f(w_out), "ln1_g": f(ln1_g), "ln1_b": f(ln1_b), "ln2_g": f(ln2_g),
             "ln2_b": f(ln2_b), "peer_w_q": f(peer_w_q), "peer_keys_1": f(peer_keys_1), "peer_keys_2": f(peer_keys_2),
             "peer_u": f(peer_u), "peer_v": f(peer_v)}
    consts2 = s2_consts()
    I_p2 = {"x_prompt": x_prompt, "w_in": f(w_in), "conv_w": f(conv_w), "conv_b": f(conv_b), "dt_bias": f(dt_bias),
            "a_log": f(a_log), "d_skip": f(d_skip), "ssd_norm_g": f(ssd_norm_g)}
    in_maps = []
    meta = []
    for core in range(8):
        b, g = core // 4, core % 4
        cols = np.concatenate([
            1536 + 384 * g + np.arange(384), 3072 + 128 * g + np.arange(128), 3584 + 128 * g + np.arange(128),
            384 * g + np.arange(384), 4120 + 128 * g + np.arange(128), 4632 + 128 * g + np.arange(128),
            5144 + 128 * g + np.arange(128), 4096 + 6 * g + np.arange(6)])
        ccols = np.concatenate([384 * g + np.arange(384), 1536 + 128 * g + np.arange(128), 2048 + 128 * g + np.arange(128)])
        seqs = 64 * b + np.arange(64)
        xs = x_sample[seqs]
        xs_t = xs.transpose(2, 1, 0)
        xs_t = np.concatenate([xs_t, xs_t], axis=2)
        scv = sc0[seqs][:, :, ccols]
        in_maps.append({
            "xT": f(x_prompt[b].T.reshape(8, 128, SEQ).transpose(1, 0, 2)),
            "wc": f(w_in0[:, cols].reshape(8, 128, NW).transpose(1, 0, 2)),
            "cw": f(cw0[:, ccols]), "cb": f(cb0[ccols][None]),
            "dtb": f(dtb0[6 * g:6 * g + 6][None]), "alog": f(alog0[6 * g:6 * g + 6][None]),
            "cosP": cosP, "sinP": sinP, "cosS": cosS, "sinS": sinS, "tri": tri,
            "xsT": f(xs_t.reshape(8, 128, 512).transpose(1, 0, 2)),
            "scv": f(np.concatenate([scv, scv], axis=0)),
            "ssm0": f(ss0[seqs][:, 6 * g:6 * g + 6]),
            "ck": ck_all[16 * core:16 * core + 16], "cv": cv_all[16 * core:16 * core + 16],
        })
        in_maps[-1].update(s2_host_inputs(core, I_all))
        pin = p2_host_inputs(b, g, I_p2)
        pin.pop("p_xT")
        in_maps[-1].update(pin)
        seg = core % 4
        nt_ = TAIL_TILES * 128
        in_maps[-1]["t_x"] = f(x_prompt[b, nt_ * seg:nt_ * (seg + 1)])
        pp = np.arange(128, dtype=np.int64)[:, None]
        tt_ = np.arange(TAIL_TILES, dtype=np.int64)[None, :, None]
        rr = np.arange(4, dtype=np.int64)[None, None, :]
        chunk_ = seg * 4 + tt_ // 4
        in_maps[-1]["rowidx"] = np.ascontiguousarray((chunk_ * 2048 + rr * 512 + (tt_ % 4) * 128 + pp[:, :, None])
                                                     .reshape(128, TAIL_TILES * 4).astype(np.uint32))
        row_in = seg * 128 + pp
        in_maps[-1]["attidx"] = np.ascontiguousarray(((row_in // 64) * 256 + np.arange(4, dtype=np.int64)[None, :] * 64 + row_in % 64)
                                                     .astype(np.uint32))
        in_maps[-1].update(consts2)
        meta.append((b, g, ccols, seqs))

    res = run_bass_kernel_spmd(nc, in_maps, core_ids=list(range(8)))
    R = res.results

    y_prompt = np.zeros((2, SEQ, D_MODEL), np.float32)
    y_sample = np.zeros((128, 4, D_MODEL), np.float32)
    p_k = np.zeros((1, 2, 2048, 8, 64), np.float32)
    p_v = np.zeros((1, 2, 2048, 8, 64), np.float32)
    p_conv = np.zeros((1, 2, 3, 2560), np.float32)
    p_ssm = np.zeros((1, 2, 24, 64, 128), np.float32)
    s_k = np.zeros((1, 128, 2048, 8, 64), np.float32)
    s_v = np.zeros((1, 128, 2048, 8, 64), np.float32)
    s_conv = np.zeros((1, 128, 3, 2560), np.float32)
    s_ssm = np.zeros((1, 128, 24, 64, 128), np.float32)
    for core in range(8):
        b, g, ccols, seqs = meta[core]
        r = R[core]
        p_k[0, b, :, 2 * g:2 * g + 2, :] = r["pk"].reshape(2048, 2, 64)
        p_v[0, b, :, 2 * g:2 * g + 2, :] = r["pv"].reshape(2048, 2, 64)
        p_conv[0, b][:, ccols] = r["pcv"]
        p_ssm[0, b, 6 * g:6 * g + 6] = r["pss"].reshape(6, 64, 128)
        s_k[0, 16 * core:16 * core + 16, 0:ROWS_COPY] = r["skc"].reshape(16, ROWS_COPY, 8, 64)
        s_v[0, 16 * core:16 * core + 16, 0:ROWS_COPY] = r["svc"].reshape(16, ROWS_COPY, 8, 64)
        s_k[0, seqs, ROWS_COPY:2048, 2 * g:2 * g + 2, :] = r["skn"].reshape(64, 4, 2, 64)
        s_v[0, seqs, ROWS_COPY:2048, 2 * g:2 * g + 2, :] = r["svn"].reshape(64, 4, 2, 64)
        sc = s_conv[0, seqs]
        sc[:, :, ccols] = r["scvo"]
        s_conv[0, seqs] = sc
        s_ssm[0, seqs, 6 * g:6 * g + 6] = r["sss"]
        y_sample[16 * core:16 * core + 16] = r["ys"].reshape(4, 16, D_MODEL).transpose(1, 0, 2)
        nt_ = TAIL_TILES * 128
        y_prompt[b, nt_ * (core % 4):nt_ * (core % 4 + 1)] = r["t_y"]
    return (y_prompt, y_sample, p_k, p_v, p_conv, p_ssm, s_k, s_v, s_conv, s_ssm)
```
